# Optimizing a Trainium2 kernel written in Bass

```python
import jax, jax.numpy as jnp
from jax import lax
import numpy as np


D_MODEL = 1024
BATCH = 16
SEQ = 4096
DEPTH = 2
DEC_BATCH = 2
DEC_SEQ = 16384
PAST_LEN = 128

GRID_W = 64
HEAD_DIM = 64
NA_HEADS = 8
RW_HEADS = 8
NA_WIDTH = NA_HEADS * HEAD_DIM
RW_WIDTH = RW_HEADS * HEAD_DIM
MIX_WIDTH = NA_WIDTH + RW_WIDTH
WIN_ROWS_MAX = 8
WIN_COLS = 16
DECAY_LORA = 64
AAA_LORA = 64
GATE_LORA = 128
N_DIR = 2
RW_COLS = 3 * RW_WIDTH + N_DIR * DECAY_LORA + N_DIR * AAA_LORA + GATE_LORA
PROJ_WIDTH = 3 * NA_WIDTH + RW_COLS
D_FF = 2816
PLE_DIM = 256
NORM_EPS = 1e-6
LNX_EPS = 64e-5
DECAY_SCALE = 0.606531

kernel_name = 'hybrid_natten_rwkv7_encoder'


def rms_norm(x, g):
    xf = x.astype(jnp.float32)
    y = xf * lax.rsqrt(jnp.mean(xf * xf, axis=-1, keepdims=True) + NORM_EPS)
    return (y * g.astype(jnp.float32)).astype(x.dtype)


def swiglu(x, wg, wu, wd):
    return (jax.nn.silu(x @ wg) * (x @ wu)) @ wd


def centred_shift_delta(u):
    prev = jnp.pad(u[:, :-1], ((0, 0), (1, 0), (0, 0)))
    nxt = jnp.pad(u[:, 1:], ((0, 0), (0, 1), (0, 0)))
    return 0.5 * (prev + nxt) - u


def neighbourhood_attention(q, k, v, rpb):
    B, T, _ = q.shape
    rows = T // GRID_W
    wr = min(WIN_ROWS_MAX, rows)
    scale = HEAD_DIM ** -0.5

    def grid(z):
        return z.reshape(B, rows, GRID_W, NA_HEADS, HEAD_DIM)

    qg = grid(q * scale)
    kg = grid(k)
    vg = grid(v)
    col = jnp.arange(GRID_W)
    col_start = jnp.clip(col - WIN_COLS // 2, 0, GRID_W - WIN_COLS)
    col_idx = col_start[:, None] + jnp.arange(WIN_COLS)[None, :]
    col_bias_idx = col_idx - col[:, None] + (WIN_COLS - 1)

    def one_row(args):
        i, q_row = args
        row_start = jnp.clip(i - wr // 2, 0, rows - wr)
        k_rows = lax.dynamic_slice_in_dim(kg, row_start, wr, axis=1)
        v_rows = lax.dynamic_slice_in_dim(vg, row_start, wr, axis=1)
        k_win = k_rows[:, :, col_idx]
        v_win = v_rows[:, :, col_idx]
        row_bias_idx = row_start + jnp.arange(wr) - i + (WIN_ROWS_MAX - 1)
        bias = rpb[:, row_bias_idx[:, None, None], col_bias_idx[None, :, :]]
        s = jnp.einsum('bjhd,bajchd->bhjac', q_row, k_win).astype(jnp.float32)
        s = s + jnp.transpose(bias, (0, 2, 1, 3)).astype(jnp.float32)[None]
        p = jax.nn.softmax(s.reshape(B, NA_HEADS, GRID_W, wr * WIN_COLS), axis=-1)
        p = p.reshape(B, NA_HEADS, GRID_W, wr, WIN_COLS).astype(v.dtype)
        return jnp.einsum('bhjac,bajchd->bjhd', p, v_win)

    o = lax.map(one_row, (jnp.arange(rows), jnp.moveaxis(qg, 1, 0)))
    return jnp.moveaxis(o, 0, 1).reshape(B, T, NA_WIDTH)


def wkv7_scan(r, w, k, v, a, b, reverse):
    T, B, H, N = r.shape

    def step(S, inp):
        r_t, w_t, k_t, v_t, a_t, b_t = inp
        sa = jnp.einsum('bhvk,bhk->bhv', S, a_t)
        S = S * w_t[:, :, None, :] + sa[..., None] * b_t[:, :, None, :] + v_t[..., None] * k_t[:, :, None, :]
        y = jnp.einsum('bhvk,bhk->bhv', S, r_t)
        return S, y

    S0 = jnp.zeros((B, H, N, N), jnp.float32)
    _, y = lax.scan(step, S0, (r, w, k, v, a, b), reverse=reverse)
    return y


def rwkv7_bidirectional(u, mu, w0, w_up, a0, a_up, g_up, k_k, k_a, r_k, lnx_w, lnx_b):
    B, T, _ = u.shape
    out_dtype = u.dtype
    u = (u + mu * centred_shift_delta(u)).astype(jnp.float32)
    r = u[..., 0:RW_WIDTH]
    k = u[..., RW_WIDTH:2 * RW_WIDTH]
    v = u[..., 2 * RW_WIDTH:3 * RW_WIDTH]
    lo = 3 * RW_WIDTH
    w_low = [u[..., lo + d * DECAY_LORA:lo + (d + 1) * DECAY_LORA] for d in range(N_DIR)]
    lo = lo + N_DIR * DECAY_LORA
    a_low = [u[..., lo + d * AAA_LORA:lo + (d + 1) * AAA_LORA] for d in range(N_DIR)]
    lo = lo + N_DIR * AAA_LORA
    g_low = u[..., lo:lo + GATE_LORA]

    def heads(z):
        return z.reshape(B, T, RW_HEADS, HEAD_DIM)

    def tmajor(z):
        return jnp.swapaxes(z, 0, 1)

    g = jax.nn.sigmoid(g_low) @ g_up.astype(jnp.float32)
    kk = heads(k * k_k)
    kk = kk / jnp.maximum(jnp.sqrt(jnp.sum(kk * kk, axis=-1, keepdims=True)), 1e-12)
    rh = heads(r)
    vh = heads(v)
    wkv = None
    bonus = None
    for d in range(N_DIR):
        z = w0[d] + jnp.tanh(w_low[d]) @ w_up[d]
        w = jnp.exp(-DECAY_SCALE * jax.nn.sigmoid(z.astype(jnp.float32)))
        a = jax.nn.sigmoid((a0[d] + a_low[d] @ a_up[d]).astype(jnp.float32))
        kd = heads(k * (1.0 + (a - 1.0) * k_a))
        ah = heads(a)
        y = wkv7_scan(tmajor(rh), tmajor(heads(w)), tmajor(kd), tmajor(vh), tmajor(-kk), tmajor(kk * ah), reverse=(d == 1))
        y = tmajor(y)
        bd = jnp.sum(rh * kd * r_k, axis=-1, keepdims=True) * vh
        wkv = y if wkv is None else wkv + y
        bonus = bd if bonus is None else bonus + bd
    mean = jnp.mean(wkv, axis=-1, keepdims=True)
    var = jnp.mean(jnp.square(wkv - mean), axis=-1, keepdims=True)
    yn = ((wkv - mean) * lax.rsqrt(var + LNX_EPS)).reshape(B, T, RW_WIDTH) * lnx_w + lnx_b
    out = (yn + bonus.reshape(B, T, RW_WIDTH)) * g
    return out.astype(out_dtype)


def encoder_trunk(x, p, w):
    h = x
    for i in range(DEPTH):
        h = h + 0.5 * swiglu(rms_norm(h, w['ffn1_norm'][i]), w['ffn1_wg'][i], w['ffn1_wu'][i], w['ffn1_wd'][i])
        n = rms_norm(h, w['mix_norm'][i])
        proj = n @ w['w_in'][i]
        q = proj[..., 0:NA_WIDTH]
        k = proj[..., NA_WIDTH:2 * NA_WIDTH]
        v = proj[..., 2 * NA_WIDTH:3 * NA_WIDTH]
        y_na = neighbourhood_attention(q, k, v, w['na_rpb'][i])
        y_rw = rwkv7_bidirectional(proj[..., 3 * NA_WIDTH:], w['rw_mu'][i], w['rw_w0'][i], w['rw_w_up'][i],
                                   w['rw_a0'][i], w['rw_a_up'][i], w['rw_g_up'][i], w['rw_k_k'][i],
                                   w['rw_k_a'][i], w['rw_r_k'][i], w['rw_lnx_w'][i], w['rw_lnx_b'][i])
        h = h + jnp.concatenate([y_na, y_rw], axis=-1) @ w['w_out'][i]
        h = h + 0.5 * swiglu(rms_norm(h, w['ffn2_norm'][i]), w['ffn2_wg'][i], w['ffn2_wu'][i], w['ffn2_wd'][i])
        gate = jax.nn.sigmoid(rms_norm(h, w['ple_norm'][i]) @ w['ple_gate'][i])
        h = h + gate * (p[i] @ w['ple_up'][i])
    return rms_norm(h, w['final_norm'])


def setup_inputs(seed: int = 0) -> dict:
    key = jax.random.key(seed)
    ks = jax.random.split(key, 40)
    f32 = jnp.float32

    def nrm(k, shape, scale):
        return jax.random.normal(k, shape, f32) * scale

    L = DEPTH
    return {
        'x_prompt': nrm(ks[0], (BATCH, SEQ, D_MODEL), 1.0),
        'x_sample': nrm(ks[1], (DEC_BATCH, DEC_SEQ, D_MODEL), 1.0),
        'p_prompt': nrm(ks[2], (DEPTH, BATCH, SEQ, PLE_DIM), 1.0),
        'p_sample': nrm(ks[3], (DEPTH, DEC_BATCH, DEC_SEQ, PLE_DIM), 1.0),
        'ffn1_norm': 1.0 + nrm(ks[4], (L, D_MODEL), 0.05),
        'ffn1_wg': nrm(ks[5], (L, D_MODEL, D_FF), D_MODEL ** -0.5),
        'ffn1_wu': nrm(ks[6], (L, D_MODEL, D_FF), D_MODEL ** -0.5),
        'ffn1_wd': nrm(ks[7], (L, D_FF, D_MODEL), D_FF ** -0.5),
        'mix_norm': 1.0 + nrm(ks[8], (L, D_MODEL), 0.05),
        'w_in': nrm(ks[9], (L, D_MODEL, PROJ_WIDTH), D_MODEL ** -0.5),
        'na_rpb': nrm(ks[10], (L, NA_HEADS, 2 * WIN_ROWS_MAX - 1, 2 * WIN_COLS - 1), 0.5),
        'rw_mu': jax.random.uniform(ks[11], (L, RW_COLS), f32),
        'rw_w0': -1.0 + nrm(ks[12], (L, N_DIR, RW_WIDTH), 1.5),
        'rw_w_up': nrm(ks[13], (L, N_DIR, DECAY_LORA, RW_WIDTH), 0.5 * DECAY_LORA ** -0.5),
        'rw_a0': nrm(ks[14], (L, N_DIR, RW_WIDTH), 0.5),
        'rw_a_up': nrm(ks[15], (L, N_DIR, AAA_LORA, RW_WIDTH), 0.5 * AAA_LORA ** -0.5),
        'rw_g_up': nrm(ks[16], (L, GATE_LORA, RW_WIDTH), GATE_LORA ** -0.5),
        'rw_k_k': 0.85 + nrm(ks[17], (L, RW_WIDTH), 0.05),
        'rw_k_a': 1.0 + nrm(ks[18], (L, RW_WIDTH), 0.05),
        'rw_r_k': nrm(ks[19], (L, RW_HEADS, HEAD_DIM), 0.1),
        'rw_lnx_w': 1.0 + nrm(ks[20], (L, RW_WIDTH), 0.05),
        'rw_lnx_b': nrm(ks[21], (L, RW_WIDTH), 0.01),
        'w_out': nrm(ks[22], (L, MIX_WIDTH, D_MODEL), MIX_WIDTH ** -0.5),
        'ffn2_norm': 1.0 + nrm(ks[23], (L, D_MODEL), 0.05),
        'ffn2_wg': nrm(ks[24], (L, D_MODEL, D_FF), D_MODEL ** -0.5),
        'ffn2_wu': nrm(ks[25], (L, D_MODEL, D_FF), D_MODEL ** -0.5),
        'ffn2_wd': nrm(ks[26], (L, D_FF, D_MODEL), D_FF ** -0.5),
        'ple_norm': 1.0 + nrm(ks[27], (L, D_MODEL), 0.05),
        'ple_gate': nrm(ks[28], (L, D_MODEL, D_MODEL), D_MODEL ** -0.5),
        'ple_up': nrm(ks[29], (L, PLE_DIM, D_MODEL), PLE_DIM ** -0.5),
        'final_norm': 1.0 + nrm(ks[30], (D_MODEL,), 0.05),
    }


def reference(x_prompt, x_sample, p_prompt, p_sample, ffn1_norm, ffn1_wg, ffn1_wu, ffn1_wd, mix_norm, w_in,
              na_rpb, rw_mu, rw_w0, rw_w_up, rw_a0, rw_a_up, rw_g_up, rw_k_k, rw_k_a, rw_r_k, rw_lnx_w,
              rw_lnx_b, w_out, ffn2_norm, ffn2_wg, ffn2_wu, ffn2_wd, ple_norm, ple_gate, ple_up, final_norm):
    weights = dict(ffn1_norm=ffn1_norm, ffn1_wg=ffn1_wg, ffn1_wu=ffn1_wu, ffn1_wd=ffn1_wd, mix_norm=mix_norm,
                   w_in=w_in, na_rpb=na_rpb, rw_mu=rw_mu, rw_w0=rw_w0, rw_w_up=rw_w_up, rw_a0=rw_a0,
                   rw_a_up=rw_a_up, rw_g_up=rw_g_up, rw_k_k=rw_k_k, rw_k_a=rw_k_a, rw_r_k=rw_r_k,
                   rw_lnx_w=rw_lnx_w, rw_lnx_b=rw_lnx_b, w_out=w_out, ffn2_norm=ffn2_norm, ffn2_wg=ffn2_wg,
                   ffn2_wu=ffn2_wu, ffn2_wd=ffn2_wd, ple_norm=ple_norm, ple_gate=ple_gate, ple_up=ple_up,
                   final_norm=final_norm)
    y_prompt = encoder_trunk(x_prompt, p_prompt, weights)
    y_sample = encoder_trunk(x_sample, p_sample, weights)
    return (y_prompt, y_sample)
```

```python
import numpy as np
import concourse.bass as bass
import concourse.mybir as mybir
from concourse.bass_utils import run_bass_kernel_spmd

F32 = mybir.dt.float32
BF16 = mybir.dt.bfloat16
AF = mybir.ActivationFunctionType
ALU = mybir.AluOpType
AX = mybir.AxisListType

D = 1024
DFF = 2816
NJF = DFF // 128
PW = 3456
NJP = PW // 128
PLE = 256
L = 2
NEG = -30000.0
ENGS = ("pe", "act", "dve", "pool", "sp")


class Tk:
    __slots__ = ("name", "w", "r", "sem", "semval", "last_dma")

    def __init__(self, name):
        self.name = name
        self.w = None
        self.r = {}
        self.sem = None
        self.semval = 0
        self.last_dma = None


class Sched:
    def __init__(self, nc):
        self.nc = nc
        self.q = {e: [] for e in ENGS}
        self.esem = {e: nc.alloc_semaphore("es_" + e) for e in ENGS}
        self.ecount = {e: 0 for e in ENGS}
        self.waited = {e: {} for e in ENGS}
        self.nops = 0
        self.nsem = 0
        self.owners = []

    def tk(self, name):
        return Tk(name)

    def _need(self, eng, ev):
        if ev is None:
            return
        sem, val = ev
        k = id(sem)
        if self.waited[eng].get(k, 0) < val:
            self.waited[eng][k] = val
            self.q[eng].append(("w", sem, val))

    def op(self, eng, fn, reads=(), writes=(), owner=None, ndma=1):
        for t in reads:
            self._need(eng, t.w)
        for t in writes:
            self._need(eng, t.w)
            for ev in t.r.values():
                self._need(eng, ev)
        if owner is not None:
            if owner.sem is None:
                owner.sem = self.nc.alloc_semaphore("ds%d" % self.nsem)
                self.nsem += 1
                self.owners.append(owner)
            self._need(eng, owner.last_dma)
            owner.semval += 16 * ndma
            ev = (owner.sem, owner.semval)
            owner.last_dma = ev
            self.q[eng].append(("d", fn, owner.sem))
        else:
            self.ecount[eng] += 1
            ev = (self.esem[eng], self.ecount[eng])
            self.q[eng].append(("o", fn, self.esem[eng]))
        for t in reads:
            t.r[id(ev[0])] = ev
        for t in writes:
            t.w = ev
            t.r = {}
        self.nops += 1
        return ev

    def barrier(self):
        evs = [(self.esem[e], self.ecount[e]) for e in ENGS if self.ecount[e]]
        evs += [o.last_dma for o in self.owners]
        for e in ENGS:
            for ev in evs:
                self._need(e, ev)

    def finish(self, eng="pool"):
        for e in ENGS:
            if self.ecount[e]:
                self._need(eng, (self.esem[e], self.ecount[e]))
        for o in self.owners:
            self._need(eng, o.last_dma)

    def emit(self):
        nc = self.nc
        q = self.q

        def run(e, name):
            for it in q[name]:
                if it[0] == "w":
                    e.wait_ge(it[1], it[2])
                elif it[0] == "o":
                    ins = it[1](e)
                    ins.then_inc(it[2], 1)
                else:
                    r = it[1](e)
                    if isinstance(r, (list, tuple)):
                        for ins in r:
                            ins.then_inc(it[2], 16)
                    else:
                        r.then_inc(it[2], 16)

        with nc.Block() as block:
            @block.tensor
            def _(e):
                run(e, "pe")

            @block.scalar
            def _(e):
                run(e, "act")

            @block.vector
            def _(e):
                run(e, "dve")

            @block.gpsimd
            def _(e):
                run(e, "pool")

            @block.sync
            def _(e):
                run(e, "sp")


WSPEC = {
    "wg1": (128, 8, NJF), "wu1": (128, 8, NJF), "wd1": (128, NJF, 8),
    "win": (128, 8, NJP), "wona": (64, 8, 8), "worw": (128, 4, 8),
    "wg2": (128, 8, NJF), "wu2": (128, 8, NJF), "wd2": (128, NJF, 8),
    "pgate": (128, 8, 8), "pup": (128, 2, 8),
}
NORMS = ("ffn1_norm", "mix_norm", "ffn2_norm", "ple_norm")


def build_program(NSEG, SEGLEN, mode="full"):
    do_mix = mode != "nomix"
    T = NSEG * SEGLEN
    NT = T // 512
    nc = bass.Bass("TRN2", target_bir_lowering=False)
    sc = Sched(nc)

    def dram_in(name, shape, dt=F32):
        return nc.dram_tensor(name, list(shape), dt, kind="ExternalInput").ap()

    def dram_tmp(name, shape, dt):
        return nc.dram_tensor(name, list(shape), dt, kind="Internal").ap()

    xin = dram_in("xin", [T, D])
    pin = dram_in("pin", [L, T, PLE])
    flags_d = dram_in("flags", [128, 2])
    yout = nc.dram_tensor("yout", [T, D], F32, kind="ExternalOutput").ap()
    wsrc = {}
    wbf = {}
    for nm, (kp, KC, NJ) in WSPEC.items():
        wsrc[nm] = dram_in("w_" + nm, [L, NJ, kp, KC * 128])
        wbf[nm] = dram_tmp("b_" + nm, [L, NJ, kp, KC * 128], BF16)
    norms_d = {nm: dram_in(nm, [L, D]) for nm in NORMS}
    fnorm_d = dram_in("final_norm", [D])
    h1T = dram_tmp("h1T", [8, 128, T], F32)
    projT = dram_tmp("projT", [15, 128, T], F32)
    qkvT = dram_tmp("qkvT", [12, 128, T], BF16)
    tk_qkvT = sc.tk("qkvT")
    ynaT = dram_tmp("ynaT", [64, 8, T], BF16)
    import os as _os0
    DBG = int(_os0.environ.get("DBG", "0"))
    LRUN = int(_os0.environ.get("LRUN", str(L)))
    if DBG:
        yrwT = nc.dram_tensor("yrwT", [128, 4, T], BF16, kind="ExternalOutput").ap()
    else:
        yrwT = dram_tmp("yrwT", [128, 4, T], BF16)
    tk_h1T = sc.tk("h1T")
    tk_projT = sc.tk("projT")
    tk_ynaT = sc.tk("ynaT")
    tk_yrwT = sc.tk("yrwT")

    def sb(name, shape, dt=F32):
        return nc.alloc_sbuf_tensor(name, list(shape), dt)

    ident = sb("ident", [128, 128]); tk_ident = sc.tk("ident")
    identb = sb("identb", [128, 128], BF16); tk_identb = sc.tk("identb")
    onesb = sb("onesb", [128, 128], BF16); tk_ones = sc.tk("ones")
    gains = sb("gains", [128, L * 4 + 1, 8]); tk_gains = sc.tk("gains")
    flags = sb("flags_s", [128, 2]); tk_flags = sc.tk("flags")
    NRING = 3
    RING_EL = NJF * 128
    ring = [sb("ring%d" % i, [128, RING_EL], BF16) for i in range(NRING)]
    tk_ring = [sc.tk("ring%d" % i) for i in range(NRING)]
    ring_i = [0]
    ARENA = 148 * 1024
    arena = sb("arena", [128, ARENA // 4])

    def carver():
        off = [0]

        def view(o, shape, dt):
            esz = 4 if dt == F32 else 2
            n = 1
            for d_ in shape[1:]:
                n *= d_
            nbytes = (n * esz + 63) // 64 * 64
            assert o + nbytes <= ARENA, (o, nbytes)
            ap = arena[:shape[0], o // 4:(o + nbytes) // 4]
            if dt != F32:
                ap = ap.bitcast(dt)
            ap = ap[:, :n]
            if len(shape) == 3:
                ap = ap.rearrange("p (a b) -> p a b", a=shape[1])
            elif len(shape) == 4:
                ap = ap.rearrange("p (a b c) -> p a b c", a=shape[1], b=shape[2])
            return ap, nbytes

        def take(shape, dt=F32):
            ap, nbytes = view(off[0], shape, dt)
            take.last = off[0]
            off[0] += nbytes
            return ap

        def at(o, shape, dt=F32):
            return view(o, shape, dt)[0]
        take.at = at
        take.off = off
        return take

    tkW = carver()
    cst = [tkW([128, 4096]) for i in range(2)]; tk_cst = [sc.tk("cst%d" % i) for i in range(2)]
    cstb = [tkW([128, 4096], BF16) for i in range(2)]; tk_cstb = [sc.tk("cstb%d" % i) for i in range(2)]
    tkA = carver()
    h = tkA([128, 8, 512]); tk_h = [sc.tk("h%d" % c) for c in range(8)]
    nb = tkA([128, 8, 512], BF16); tk_n = sc.tk("n")
    sq = tkA([128, 8, 512], BF16); tk_sq = sc.tk("sq")
    act = tkA([128, NJF, 512], BF16); tk_act = [sc.tk("act%d" % j) for j in range(NJF)]
    rstd = tkA([128, 512]); tk_rstd = sc.tk("rstd")
    sg = [tkA([128, 512]) for i in range(2)]; tk_sg = [sc.tk("sg%d" % i) for i in range(2)]
    ev_f = [tkA([128, 512]) for i in range(3)]; tk_evf = [sc.tk("evf%d" % i) for i in range(3)]
    ev_b = [tkA([128, 512], BF16) for i in range(3)]; tk_evb = [sc.tk("evb%d" % i) for i in range(3)]
    tokbuf = tkA([128, 4, D]); tk_tok = sc.tk("tokbuf")
    pbuf = tkA([128, 4, PLE]); tk_pbuf = sc.tk("pbuf")
    pT = tkA([128, 2, 512], BF16); tk_pT = sc.tk("pT")
    ynat = tkA([64, 8, 512], BF16); tk_ynat = sc.tk("ynat")
    yrwt = tkA([128, 4, 512], BF16); tk_yrwt = sc.tk("yrwt")
    sqf = tkA([128, 8, 512]); tk_sqf = sc.tk("sqf")

    pall = nc.alloc_psum_tensor("pall", [128, 4096], F32)
    pb = [pall[:, i * 512:(i + 1) * 512] for i in range(8)]
    tk_pb = [sc.tk("pb%d" % i) for i in range(8)]
    rr = [0]

    def next_bank():
        i = 1 + (rr[0] % 6)
        rr[0] += 1
        return pb[i], tk_pb[i]

    sc.op("pool", lambda e: e.memset(ident[:], 1.0), writes=[tk_ident])
    sc.op("pool", lambda e: e.affine_select(out=ident[:], in_=ident[:], pattern=[[-1, 128]],
                                            compare_op=ALU.is_equal, fill=0.0, base=0, channel_multiplier=1),
          reads=[tk_ident], writes=[tk_ident])
    sc.op("pool", lambda e: e.tensor_copy(identb[:], ident[:]), reads=[tk_ident], writes=[tk_identb])
    sc.op("pool", lambda e: e.memset(onesb[:], 1.0), writes=[tk_ones])
    sc.op("pool", lambda e: e.dma_start(out=flags[:], in_=flags_d), writes=[tk_flags], owner=tk_flags)

    def ld_gains(e):
        r = []
        for l in range(L):
            for i, nm in enumerate(NORMS):
                r.append(e.dma_start(out=gains[:, l * 4 + i, :],
                                     in_=norms_d[nm][l].rearrange("(c p) -> p c", p=128),
                                     allow_slow_non_contiguous=True))
        r.append(e.dma_start(out=gains[:, L * 4, :], in_=fnorm_d.rearrange("(c p) -> p c", p=128),
                             allow_slow_non_contiguous=True))
        return r
    sc.op("pool", ld_gains, writes=[tk_gains], owner=tk_gains, ndma=L * 4 + 1)

    cvt_i = [0]

    def convert(src_flat, dst_flat, nper):
        nblk = (nper + 4095) // 4096
        while nper % nblk:
            nblk += 1
        F = nper // nblk
        for b in range(nblk):
            i = cvt_i[0] % 2
            cvt_i[0] += 1
            s_ap = src_flat[:, b * F:(b + 1) * F]
            d_ap = dst_flat[:, b * F:(b + 1) * F]
            sc.op("sp", (lambda e, i=i, s_ap=s_ap, F=F: e.dma_start(out=cst[i][:, :F], in_=s_ap)),
                  writes=[tk_cst[i]], owner=tk_cst[i])
            eng = ("dve", "act")[i]
            if eng == "dve":
                sc.op("dve", (lambda e, i=i, F=F: e.tensor_copy(cstb[i][:, :F], cst[i][:, :F])),
                      reads=[tk_cst[i]], writes=[tk_cstb[i]])
            else:
                sc.op("act", (lambda e, i=i, F=F: e.activation(cstb[i][:, :F], cst[i][:, :F], AF.Copy)),
                      reads=[tk_cst[i]], writes=[tk_cstb[i]])
            sc.op("pool", (lambda e, i=i, d_ap=d_ap, F=F: e.dma_start(out=d_ap, in_=cstb[i][:, :F])),
                  reads=[tk_cstb[i]], owner=tk_cstb[i])

    tk_wbf = {}
    for nm, (kp, KC, NJ) in WSPEC.items():
        tot = L * NJ * kp * KC * 128
        nper = tot // 128
        s = wsrc[nm].rearrange("l j k x -> (l j k x)").rearrange("(p n) -> p n", p=128)
        d = wbf[nm].rearrange("l j k x -> (l j k x)").rearrange("(p n) -> p n", p=128)
        convert(s, d, nper)
        tk_wbf[nm] = sc.tk("wbf_" + nm)
    sc.barrier()

    def wload(nm, l, j):
        kp, KC, NJ = WSPEC[nm]
        s = ring_i[0] % NRING
        ring_i[0] += 1
        n = KC * 128
        src = wbf[nm][l, j]
        sc.op("sp", (lambda e, s=s, kp=kp, n=n, src=src: e.dma_start(out=ring[s][:kp, :n], in_=src)),
              writes=[tk_ring[s]], owner=tk_ring[s])
        return ring[s], tk_ring[s]

    def rmsnorm(gidx, out_fp32=None):
        sc.op("act", lambda e: e.activation(sq[:], h[:], AF.Square), reads=tk_h, writes=[tk_sq])

        def mm(e):
            r = None
            for c in range(8):
                r = e.matmul(pb[0][:], onesb[:], sq[:, c, :], start=(c == 0), stop=(c == 7))
            return r
        sc.op("pe", mm, reads=[tk_sq, tk_ones], writes=[tk_pb[0]])
        sc.op("act", lambda e: e.activation(rstd[:], pb[0][:], AF.Sqrt, bias=1e-6, scale=1.0 / D),
              reads=[tk_pb[0]], writes=[tk_rstd])
        sc.op("dve", lambda e: e.reciprocal(rstd[:], rstd[:]), reads=[tk_rstd], writes=[tk_rstd])
        dst = nb if out_fp32 is None else out_fp32

        def nrm(e):
            r = None
            for c in range(8):
                r = e.scalar_tensor_tensor(out=dst[:, c, :], in0=h[:, c, :], scalar=gains[:, gidx, c:c + 1],
                                           in1=rstd[:], op0=ALU.mult, op1=ALU.mult)
            return r
        return nrm

    def norm_to_n(gidx):
        nrm = rmsnorm(gidx)
        sc.op("dve", nrm, reads=tk_h + [tk_rstd, tk_gains], writes=[tk_n])

    def ffn(l, wg, wu, wd):
        for j in range(NJF):
            gbuf, gtk = wload(wg, l, j)
            ubuf, utk = wload(wu, l, j)
            pg, tpg = next_bank()
            pu, tpu = next_bank()

            def mmg(e, gbuf=gbuf, pg=pg):
                r = None
                for c in range(8):
                    r = e.matmul(pg[:], gbuf[:, c * 128:(c + 1) * 128], nb[:, c, :], start=(c == 0), stop=(c == 7))
                return r

            def mmu(e, ubuf=ubuf, pu=pu):
                r = None
                for c in range(8):
                    r = e.matmul(pu[:], ubuf[:, c * 128:(c + 1) * 128], nb[:, c, :], start=(c == 0), stop=(c == 7))
                return r
            sc.op("pe", mmg, reads=[gtk, tk_n], writes=[tpg])
            sc.op("pe", mmu, reads=[utk, tk_n], writes=[tpu])
            si = j % 2
            sc.op("act", (lambda e, si=si, pg=pg: e.activation(sg[si][:], pg[:], AF.Silu)),
                  reads=[tpg], writes=[tk_sg[si]])
            sc.op("dve", (lambda e, si=si, pu=pu, j=j: e.tensor_tensor(act[:, j, :], sg[si][:], pu[:], ALU.mult)),
                  reads=[tk_sg[si], tpu], writes=[tk_act[j]])
        for c in range(8):
            dbuf, dtk = wload(wd, l, c)
            po, tpo = next_bank()

            def mmd(e, dbuf=dbuf, po=po):
                r = None
                for j in range(NJF):
                    r = e.matmul(po[:], dbuf[:, j * 128:(j + 1) * 128], act[:, j, :], start=(j == 0), stop=(j == NJF - 1))
                return r
            sc.op("pe", mmd, reads=[dtk] + tk_act, writes=[tpo])
            sc.op("dve", (lambda e, po=po, c=c: e.scalar_tensor_tensor(out=h[:, c, :], in0=po[:], scalar=0.5, in1=h[:, c, :],
                                                                    op0=ALU.mult, op1=ALU.add)),
                  reads=[tpo, tk_h[c]], writes=[tk_h[c]])

    def load_x_tile(t):
        t0 = t * 512
        sc.op("pool", lambda e: e.dma_start(out=tokbuf[:], in_=xin[t0:t0 + 512, :].rearrange("(b p) d -> p b d", p=128)),
              writes=[tk_tok], owner=tk_tok)
        for c in range(8):
            def tr(e, c=c):
                r = None
                for b in range(4):
                    r = e.transpose(pb[7][:, b * 128:(b + 1) * 128], tokbuf[:, b, c * 128:(c + 1) * 128], ident[:])
                return r
            sc.op("pe", tr, reads=[tk_tok, tk_ident], writes=[tk_pb[7]])
            if c % 2 == 0:
                sc.op("act", (lambda e, c=c: e.activation(h[:, c, :], pb[7][:], AF.Copy)), reads=[tk_pb[7]], writes=[tk_h[c]])
            else:
                sc.op("dve", (lambda e, c=c: e.tensor_copy(h[:, c, :], pb[7][:])), reads=[tk_pb[7]], writes=[tk_h[c]])

    def store_out_tile(t):
        t0 = t * 512
        nrm = rmsnorm(L * 4, out_fp32=sqf)
        sc.op("dve", nrm, reads=tk_h + [tk_rstd, tk_gains], writes=[tk_sqf])
        for c in range(8):
            def tr(e, c=c):
                r = None
                for b in range(4):
                    r = e.transpose(pb[7][:, b * 128:(b + 1) * 128], sqf[:, c, b * 128:(b + 1) * 128], ident[:])
                return r
            sc.op("pe", tr, reads=[tk_sqf, tk_ident], writes=[tk_pb[7]])
            dst = tokbuf[:, :, c * 128:(c + 1) * 128]
            src = pb[7][:].rearrange("p (b q) -> p b q", b=4)
            if c % 2 == 0:
                sc.op("act", (lambda e, dst=dst, src=src: e.activation(dst, src, AF.Copy)), reads=[tk_pb[7]], writes=[tk_tok])
            else:
                sc.op("dve", (lambda e, dst=dst, src=src: e.tensor_copy(dst, src)), reads=[tk_pb[7]], writes=[tk_tok])
        sc.op("pool", lambda e: e.dma_start(out=yout[t0:t0 + 512, :].rearrange("(b p) d -> p b d", p=128), in_=tokbuf[:]),
              reads=[tk_tok], owner=tk_tok)


    def phase_a(l, t):
        t0 = t * 512
        norm_to_n(l * 4 + 0)
        ffn(l, "wg1", "wu1", "wd1")
        sc.op("pool", lambda e: e.dma_start(out=h1T[:, :, t0:t0 + 512].rearrange("c p t -> p c t"), in_=h[:]),
              reads=tk_h, writes=[tk_h1T], owner=tk_h[0])
        if not do_mix:
            return
        norm_to_n(l * 4 + 1)
        for j in range(NJP):
            wbuf, wtk = wload("win", l, j)
            po, tpo = next_bank()

            def mmp(e, wbuf=wbuf, po=po):
                r = None
                for c in range(8):
                    r = e.matmul(po[:], wbuf[:, c * 128:(c + 1) * 128], nb[:, c, :], start=(c == 0), stop=(c == 7))
                return r
            sc.op("pe", mmp, reads=[wtk, tk_n], writes=[tpo])
            k = j % 3
            scale = 0.125 if j < 4 else 1.0
            dstb = ev_b[k] if j < 12 else ev_f[k]
            dtk = tk_evb[k] if j < 12 else tk_evf[k]
            if j % 2 == 0:
                sc.op("act", (lambda e, dstb=dstb, po=po, scale=scale: e.activation(dstb[:], po[:], AF.Copy, scale=scale)),
                      reads=[tpo], writes=[dtk])
            else:
                sc.op("dve", (lambda e, dstb=dstb, po=po, scale=scale: e.tensor_scalar(dstb[:], po[:], scale, None, ALU.mult)),
                      reads=[tpo], writes=[dtk])
            if j < 12:
                sc.op("pool", (lambda e, dstb=dstb, j=j: e.dma_start(out=qkvT[j, :, t0:t0 + 512], in_=dstb[:])),
                      reads=[dtk], writes=[tk_qkvT], owner=dtk)
            else:
                sc.op("pool", (lambda e, dstb=dstb, j=j: e.dma_start(out=projT[j - 12, :, t0:t0 + 512], in_=dstb[:])),
                      reads=[dtk], writes=[tk_projT], owner=dtk)

    def phase_c(l, t):
        t0 = t * 512
        sc.op("pool", lambda e: e.dma_start(out=h[:], in_=h1T[:, :, t0:t0 + 512].rearrange("c p t -> p c t")),
              reads=[tk_h1T], writes=tk_h, owner=tk_h[0])
        if do_mix:
            sc.op("pool", lambda e: e.dma_start(out=ynat[:], in_=ynaT[:, :, t0:t0 + 512]),
                  reads=[tk_ynaT], writes=[tk_ynat], owner=tk_ynat)
            sc.op("pool", lambda e: e.dma_start(out=yrwt[:], in_=yrwT[:, :, t0:t0 + 512]),
                  reads=[tk_yrwT], writes=[tk_yrwt], owner=tk_yrwt)
            for j in range(8):
                w1, w1tk = wload("wona", l, j)
                w2, w2tk = wload("worw", l, j)
                po, tpo = next_bank()

                def mmo(e, w1=w1, w2=w2, po=po):
                    for hh in range(8):
                        e.matmul(po[:], w1[:64, hh * 128:(hh + 1) * 128], ynat[:, hh, :], start=(hh == 0), stop=False)
                    r = None
                    for c in range(4):
                        r = e.matmul(po[:], w2[:, c * 128:(c + 1) * 128], yrwt[:, c, :], start=False, stop=(c == 3))
                    return r
                sc.op("pe", mmo, reads=[w1tk, w2tk, tk_ynat, tk_yrwt], writes=[tpo])
                sc.op("dve", (lambda e, po=po, j=j: e.tensor_tensor(h[:, j, :], h[:, j, :], po[:], ALU.add)),
                      reads=[tpo, tk_h[j]], writes=[tk_h[j]])
        norm_to_n(l * 4 + 2)
        ffn(l, "wg2", "wu2", "wd2")
        sc.op("pool", lambda e: e.dma_start(out=pbuf[:], in_=pin[l, t0:t0 + 512, :].rearrange("(b p) d -> p b d", p=128)),
              writes=[tk_pbuf], owner=tk_pbuf)
        for c in range(2):
            def tr(e, c=c):
                r = None
                for b in range(4):
                    r = e.transpose(pb[7][:, b * 128:(b + 1) * 128], pbuf[:, b, c * 128:(c + 1) * 128], ident[:])
                return r
            sc.op("pe", tr, reads=[tk_pbuf, tk_ident], writes=[tk_pb[7]])
            sc.op("act", (lambda e, c=c: e.activation(pT[:, c, :], pb[7][:], AF.Copy)), reads=[tk_pb[7]], writes=[tk_pT])
        norm_to_n(l * 4 + 3)
        for j in range(8):
            wgt, wgtk = wload("pgate", l, j)
            wup, wuptk = wload("pup", l, j)
            pg, tpg = next_bank()
            pu, tpu = next_bank()

            def mmg(e, wgt=wgt, pg=pg):
                r = None
                for c in range(8):
                    r = e.matmul(pg[:], wgt[:, c * 128:(c + 1) * 128], nb[:, c, :], start=(c == 0), stop=(c == 7))
                return r

            def mmu(e, wup=wup, pu=pu):
                r = None
                for c in range(2):
                    r = e.matmul(pu[:], wup[:, c * 128:(c + 1) * 128], pT[:, c, :], start=(c == 0), stop=(c == 1))
                return r
            sc.op("pe", mmg, reads=[wgtk, tk_n], writes=[tpg])
            sc.op("pe", mmu, reads=[wuptk, tk_pT], writes=[tpu])
            si = j % 2
            sc.op("act", (lambda e, si=si, pg=pg: e.activation(sg[si][:], pg[:], AF.Sigmoid)), reads=[tpg], writes=[tk_sg[si]])
            sc.op("dve", (lambda e, si=si, pu=pu: e.tensor_tensor(sg[si][:], sg[si][:], pu[:], ALU.mult)),
                  reads=[tk_sg[si], tpu], writes=[tk_sg[si]])
            sc.op("dve", (lambda e, si=si, j=j: e.tensor_tensor(h[:, j, :], h[:, j, :], sg[si][:], ALU.add)),
                  reads=[tk_sg[si], tk_h[j]], writes=[tk_h[j]])


    R_ = T // 64
    RS = SEGLEN // 64
    tb_d = dram_in("tb", [L, 128, 8 * 14 * 64])
    tkN = carver()
    KW = 22 * 64
    kT = tkN([64, 8, KW], BF16); tk_kT = sc.tk("kT")
    vT = tkN([128, 4, KW], BF16); tk_vT = sc.tk("vT")
    qT = tkN([64, 8, 512], BF16); tk_qT = sc.tk("qT")
    Vb = tkN([128, 21, 512], BF16); tk_Vb = sc.tk("Vb")
    Tb = tkN([128, 8, 7, 2, 64][:4] if False else [128, 8 * 14 * 64]); tk_Tb = sc.tk("Tb")
    Tb5 = Tb.rearrange("p (h m2 par j) -> p h m2 par j", h=8, m2=7, par=2)
    Ssb = tkN([128, 8, 4, 64]); tk_Ssb = sc.tk("Ssb")
    Eb = tkN([128, 8, 4, 64], BF16); tk_E = sc.tk("E")
    rden = tkN([64, 512]); tk_rden = sc.tk("rden")
    tmpS = tkN([64, 512]); tk_tmpS = sc.tk("tmpS")
    tmpP = tkN([64, 512]); tk_tmpP = sc.tk("tmpP")
    ob = tkN([64, 8, 512], BF16); tk_ob = sc.tk("ob")
    zt = tkN([128, 4, 512], BF16); tk_zt = sc.tk("zt")
    Sps = pall[:, 512:2560].rearrange("p (h k q) -> p h k q", h=8, k=4)
    tk_S = tk_pb[1:5]
    pb7b = pb[7].bitcast(BF16)

    def na_phase(l):
        sc.op("pool", lambda e: e.dma_start(out=Tb[:], in_=tb_d[l]), writes=[tk_Tb], owner=tk_Tb)
        for blk in range(R_ // 8):
            r0 = blk * 8
            klo = max(0, r0 - 7)
            khi = min(R_, r0 + 15)
            nk = (khi - klo) * 64
            sc.op("pool", (lambda e, klo=klo, khi=khi, nk=nk: e.dma_start(
                out=kT[:, :, :nk], in_=qkvT[4:8, :, klo * 64:khi * 64].rearrange("c (two p) t -> p (c two) t", two=2))),
                reads=[tk_qkvT], writes=[tk_kT], owner=tk_kT)
            sc.op("pool", (lambda e, klo=klo, khi=khi, nk=nk: e.dma_start(
                out=vT[:, :, :nk], in_=qkvT[8:12, :, klo * 64:khi * 64].rearrange("c p t -> p c t"))),
                reads=[tk_qkvT], writes=[tk_vT], owner=tk_vT)
            sc.op("pool", (lambda e, r0=r0: e.dma_start(
                out=qT[:], in_=qkvT[0:4, :, r0 * 64:r0 * 64 + 512].rearrange("c (two p) t -> p (c two) t", two=2))),
                reads=[tk_qkvT], writes=[tk_qT], owner=tk_qT)
            for o in range(klo, khi - 1):
                oo = o - klo

                def trv(e, oo=oo):
                    r = None
                    for hp in range(4):
                        r = e.transpose(pb7b[:, hp * 128:(hp + 1) * 128], vT[:, hp, oo * 64:oo * 64 + 128], identb[:])
                    return r
                sc.op("pe", trv, reads=[tk_vT, tk_identb], writes=[tk_pb[7]])
                if oo % 2 == 0:
                    sc.op("act", (lambda e, oo=oo: e.activation(Vb[:, oo, :], pb7b[:, 0:512], AF.Copy)),
                          reads=[tk_pb[7]], writes=[tk_Vb])
                else:
                    sc.op("dve", (lambda e, oo=oo: e.tensor_copy(Vb[:, oo, :], pb7b[:, 0:512])),
                          reads=[tk_pb[7]], writes=[tk_Vb])
            rvs = []
            for i in range(r0, r0 + 8):
                seg = i // RS
                il = i % RS
                rsP = seg * RS + min(max(il - 4, 0), RS - 8)
                rsS = min(max(i - 4, 0), R_ - 8)
                if rsP == rsS:
                    rvs.append((i, rsP, 0))
                else:
                    rvs.append((i, rsS, 1))
                    rvs.append((i, rsP, 2))

            def emit_S(i, rs, r0=r0, klo=klo):
                qo = (i - r0) * 64

                def mms(e):
                    r = None
                    for hh in range(8):
                        for kc in range(4):
                            ks = (rs + 2 * kc - klo) * 64
                            r = e.matmul(Sps[:, hh, kc, :], kT[:, hh, ks:ks + 128], qT[:, hh, qo:qo + 64],
                                         start=True, stop=True)
                    return r
                sc.op("pe", mms, reads=[tk_kT, tk_qT], writes=tk_S)

            def emit_soft(i, rs):
                cl = i - rs
                s0 = 7 - cl
                tbv = Tb5[:, :, s0 // 2:s0 // 2 + 4, s0 % 2, :]
                sc.op("dve", lambda e: e.tensor_tensor(Ssb[:], Sps, tbv, ALU.add), reads=tk_S + [tk_Tb], writes=[tk_Ssb])
                sc.op("act", lambda e: e.activation(Eb[:], Ssb[:], AF.Exp), reads=[tk_Ssb], writes=[tk_E])

            def emit_pv(i, rs, kind, r0=r0, klo=klo):
                qo = (i - r0) * 64

                def mmpv(e):
                    r = None
                    for hh in range(8):
                        for kc in range(4):
                            r = e.matmul(pb[5][:64, hh * 64:(hh + 1) * 64], Vb[:, rs + 2 * kc - klo, hh * 64:(hh + 1) * 64],
                                         Eb[:, hh, kc, :], start=(kc == 0), stop=(kc == 3))
                    return r
                sc.op("pe", mmpv, reads=[tk_Vb, tk_E], writes=[tk_pb[5]])

                def mmden(e):
                    r = None
                    for kc in range(4):
                        r = e.matmul(pb[6][:64, :].rearrange("p (h q) -> p h q", h=8), onesb[:, :64], Eb[:, :, kc, :],
                                     start=(kc == 0), stop=(kc == 3))
                    return r
                sc.op("pe", mmden, reads=[tk_E, tk_ones], writes=[tk_pb[6]])
                sc.op("dve", lambda e: e.reciprocal(rden[:], pb[6][:64, :]), reads=[tk_pb[6]], writes=[tk_rden])
                o3 = pb[5][:64, :].rearrange("p (h q) -> p h q", h=8)
                r3 = rden[:].rearrange("p (h q) -> p h q", h=8)
                if kind == 0:
                    sc.op("dve", lambda e: e.tensor_tensor(ob[:, :, qo:qo + 64], o3, r3, ALU.mult),
                          reads=[tk_pb[5], tk_rden], writes=[tk_ob])
                elif kind == 1:
                    sc.op("dve", lambda e: e.scalar_tensor_tensor(out=tmpS[:], in0=pb[5][:64, :], scalar=flags[:64, 0:1], in1=rden[:],
                                                                  op0=ALU.mult, op1=ALU.mult),
                          reads=[tk_pb[5], tk_rden, tk_flags], writes=[tk_tmpS])
                else:
                    sc.op("dve", lambda e: e.scalar_tensor_tensor(out=tmpP[:], in0=pb[5][:64, :], scalar=flags[:64, 1:2], in1=rden[:],
                                                                  op0=ALU.mult, op1=ALU.mult),
                          reads=[tk_pb[5], tk_rden, tk_flags], writes=[tk_tmpP])
                    sc.op("dve", lambda e: e.tensor_tensor(ob[:, :, qo:qo + 64], tmpS[:].rearrange("p (h q) -> p h q", h=8),
                                                           tmpP[:].rearrange("p (h q) -> p h q", h=8), ALU.add),
                          reads=[tk_tmpS, tk_tmpP], writes=[tk_ob])

            import os
            NAS = int(os.environ.get("NAS", "3"))
            if NAS >= 1:
                emit_S(rvs[0][0], rvs[0][1])
            for n, (i, rs, kind) in enumerate(rvs):
                if NAS >= 2:
                    emit_soft(i, rs)
                if n + 1 < len(rvs) and NAS >= 1:
                    emit_S(rvs[n + 1][0], rvs[n + 1][1])
                if NAS >= 3:
                    emit_pv(i, rs, kind)
            sc.op("pool", (lambda e, r0=r0: e.dma_start(out=ynaT[:, :, r0 * 64:r0 * 64 + 512], in_=ob[:])),
                  reads=[tk_ob], writes=[tk_ynaT], owner=tk_ob)

    def zero_fill(dst, tk_dst, npart):
        sc.op("pool", lambda e: e.memset(zt[:], 0.0), writes=[tk_zt])
        for t in range(NT):
            if npart == 128:
                sc.op("pool", (lambda e, t=t: e.dma_start(out=dst[:, :, t * 512:(t + 1) * 512], in_=zt[:])),
                      reads=[tk_zt], writes=[tk_dst], owner=tk_zt)
            else:
                for hh2 in range(2):
                    sc.op("pool", (lambda e, t=t, hh2=hh2: e.dma_start(out=dst[:, hh2 * 4:hh2 * 4 + 4, t * 512:(t + 1) * 512], in_=zt[:64])),
                          reads=[tk_zt], writes=[tk_dst], owner=tk_zt)

    TT_ = 256
    NTR = T // TT_
    SC = 0.606531
    rwp_d = {nm: dram_in(nm, shp) for nm, shp in [
        ("rw_mu", [L, 1920]), ("rw_w0", [L, 2, 512]), ("rw_w_up", [L, 2, 64, 512]), ("rw_a0", [L, 2, 512]),
        ("rw_a_up", [L, 2, 64, 512]), ("rw_g_up", [L, 128, 512]), ("rw_k_k", [L, 512]), ("rw_k_a", [L, 512]),
        ("rw_r_k", [L, 512]), ("rw_lnx_w", [L, 512]), ("rw_lnx_b", [L, 512])]}
    rwmask_d = dram_in("rwmask", [128, 4, 128])
    yf_d = dram_tmp("yf_d", [128, T // 64, 4, 64], F32); tk_yfd = sc.tk("yfd")
    tkR = carver()
    uA0 = tkR([128, 4, TT_ + 2]); uA = [uA0, uA0]; tk_uA0 = sc.tk("uA0"); tk_uA = [tk_uA0, tk_uA0]
    shb = tkR([128, 4, TT_]); tk_sh = sc.tk("sh"); o_sh = tkR.last
    xr = tkR([128, 4, TT_]); xk = tkR([128, 4, TT_]); xv = tkR([128, 4, TT_])
    tk_x = [sc.tk("xr"), sc.tk("xk"), sc.tk("xv")]
    xs3 = [xr, xk, xv]
    uB = tkR([64, 4, TT_ + 2]); tk_uB = sc.tk("uB")
    xB = tkR([64, 4, TT_]); tk_xB = sc.tk("xB")
    uC = tkR([128, 1, TT_ + 2]); tk_uC = sc.tk("uC")
    xCf = tkR([128, 1, TT_]); tk_xCf = sc.tk("xCf")
    xCb = tkR([128, TT_], BF16); tk_xCb = sc.tk("xCb")
    twb = tkR([64, TT_], BF16); tk_twb = sc.tk("twb")
    alb = tkR([64, 2, TT_], BF16); tk_alb = sc.tk("alb")
    sqkb = tkR([128, 4, TT_], BF16); tk_sqkb = sc.tk("sqkb")
    kkn = tkR([128, 4, TT_]); tk_kkn = sc.tk("kkn"); o_kkn = tkR.last
    sig = tkR([128, 4, TT_]); tk_sig = sc.tk("sig"); o_sig = tkR.last
    cs = tkR([128, 4, TT_]); tk_cs = sc.tk("cs"); o_cs = tkR.last
    t1 = tkR([128, 4, TT_]); tk_t1 = sc.tk("t1"); o_t1 = tkR.last
    t2 = tkR([128, 4, TT_]); tk_t2 = sc.tk("t2"); o_t2 = tkR.last
    e0 = tkR([128, 4, TT_]); tk_e0 = sc.tk("e0"); o_e0 = tkR.last
    e1 = tkR([128, 4, TT_]); tk_e1 = sc.tk("e1")
    asg = tkR([128, 4, TT_]); tk_asg = sc.tk("asg"); o_asg = tkR.last
    asf = tkR.at(o_sh, [128, 4, TT_]); tk_asf = tk_sh
    bb = tkR([128, 4, TT_]); tk_bb = sc.tk("bb"); o_bb = tkR.last
    kd = tkR([128, 4, TT_]); tk_kd = sc.tk("kd"); o_kd = tkR.last
    bonus = tkR.at(o_asg, [128, 4, TT_]); tk_bonus = tk_asg
    gbuf = tkR.at(o_kkn, [128, 4, TT_]); tk_g = tk_kkn
    NCH = TT_ // 64
    ARbd = tkR([128, 4, NCH, 256], BF16); tk_AR = sc.tk("AR")
    Btbd = tkR([128, 4, NCH, 128], BF16); tk_Bt = sc.tk("Bt")
    Ktbd = tkR([128, 4, NCH, 128], BF16); tk_Kt = sc.tk("Kt")
    Bhbd = tkR([128, 4, NCH, 128], BF16); tk_Bh = sc.tk("Bh")
    Khbd = tkR([128, 4, NCH, 128], BF16); tk_Kh = sc.tk("Kh")
    Vbd = tkR([128, 4, NCH, 128], BF16); tk_Vbd = sc.tk("Vbd")
    Sb = [tkR([128, 4, 128], BF16) for _ in range(2)]; tk_Sb = [sc.tk("Sb0"), sc.tk("Sb1")]
    STb = [tkR([128, 4, 128], BF16) for _ in range(2)]; tk_STb = [sc.tk("STb0"), sc.tk("STb1")]
    TTb = [tkR([128, 4, 128], BF16) for _ in range(2)]; tk_TTb = [sc.tk("TTb0"), sc.tk("TTb1")]
    M1 = tkR([128, 4, 256], BF16); tk_M1 = sc.tk("M1")
    M2 = tkR([128, 4, 256], BF16); tk_M2 = sc.tk("M2")
    Vtm = tkR([128, 4, 128], BF16); tk_Vtm = sc.tk("Vtm")
    Bhtm = tkR([128, 4, 128], BF16); tk_Bhtm = sc.tk("Bhtm")
    Khtm = tkR([128, 4, 128], BF16); tk_Khtm = sc.tk("Khtm")
    Wb = tkR([128, 4, 128], BF16); tk_Wb = sc.tk("Wb")
    Ub = tkR([128, 4, 128], BF16); tk_Ub = sc.tk("Ub")
    Hs = tkR([128, 4, 128]); tk_H = sc.tk("H")
    Hb = tkR([128, 4, 128], BF16); tk_Hb = sc.tk("Hb")
    Yt = tkR.at(o_sig, [128, NCH, 4, 64]); tk_Yt = tk_sig
    Yf = tkR.at(o_e0, [128, NCH, 4, 64]); tk_Yf = tk_e0
    cen = tkR.at(o_t1, [128, NCH, 4, 64]); tk_cen = tk_t1
    ynbd = tkR([128, NCH, 4, 128], BF16); tk_ynbd = sc.tk("ynbd")
    ynT = tkR.at(o_t2, [128, 4, TT_]); tk_ynT = tk_t2
    orw = tkR([128, 4, TT_], BF16); tk_orw = sc.tk("orw")
    st1 = tkR([128, 16]); st2 = tkR([128, 16]); pcb = tkR([128, 16]); tk_st = sc.tk("st")
    tk_pc = sc.tk("pc")
    mask01 = tkR([128, 4 * TT_]); tk_m01 = sc.tk("m01")
    rwm = tkR([128, 4, 128]); tk_rwm = sc.tk("rwm")
    mAf = tkR([128, 256]); mAb = tkR([128, 256]); tk_mA = sc.tk("mA")
    bones = tkR([128, 128], BF16); tk_bones = sc.tk("bones")
    wupf = tkR.at(o_cs, [64, 2, 512]); aupf = tkR.at(o_kd, [64, 2, 512]); gupf = tkR.at(o_bb, [128, 512])
    wupb = tkR([64, 2, 512], BF16); aupb = tkR([64, 2, 512], BF16); gupb = tkR([128, 512], BF16)
    tk_wts = sc.tk("rwwts")
    tk_stage = [tk_cs, tk_kd, tk_bb]
    muA = tkR([128, 12]); muB = tkR([64, 4]); muC = tkR([128, 1])
    w0s = tkR([128, 8]); a0s = tkR([128, 8]); kks = tkR([128, 4]); kas = tkR([128, 4]); rks = tkR([128, 4])
    lws = tkR([128, 4]); lbs = tkR([128, 4])
    tk_par = sc.tk("rwpar")

    def bank2(i):
        return pall[:, i * 512:(i + 2) * 512]

    def seq(eng, fns, reads, writes):
        for fn in fns:
            sc.op(eng, fn, reads=list(reads) + list(writes), writes=writes)

    def bcast(ap2, shape):
        return ap2.unsqueeze(2).to_broadcast(shape)

    def rw_setup(l):
        def ldp(e):
            r = []
            nsc = dict(allow_slow_non_contiguous=True)
            r.append(e.dma_start(out=muA[:], in_=rwp_d["rw_mu"][l, 0:1536].rearrange("(c p) -> p c", p=128), **nsc))
            r.append(e.dma_start(out=muB[:], in_=rwp_d["rw_mu"][l, 1536:1792].rearrange("(c p) -> p c", p=64), **nsc))
            r.append(e.dma_start(out=muC[:], in_=rwp_d["rw_mu"][l, 1792:1920].rearrange("(c p) -> p c", p=128), **nsc))
            r.append(e.dma_start(out=w0s[:], in_=rwp_d["rw_w0"][l].rearrange("d (c p) -> p (d c)", p=128), **nsc))
            r.append(e.dma_start(out=a0s[:], in_=rwp_d["rw_a0"][l].rearrange("d (c p) -> p (d c)", p=128), **nsc))
            for dst, nm in ((kks, "rw_k_k"), (kas, "rw_k_a"), (rks, "rw_r_k"), (lws, "rw_lnx_w"), (lbs, "rw_lnx_b")):
                r.append(e.dma_start(out=dst[:], in_=rwp_d[nm][l].rearrange("(c p) -> p c", p=128), **nsc))
            return r
        sc.op("pool", ldp, writes=[tk_par], owner=tk_par, ndma=10)

        def ldw(e):
            r = [e.dma_start(out=wupf[:], in_=rwp_d["rw_w_up"][l].rearrange("d k n -> k d n")),
                 e.dma_start(out=aupf[:], in_=rwp_d["rw_a_up"][l].rearrange("d k n -> k d n")),
                 e.dma_start(out=gupf[:], in_=rwp_d["rw_g_up"][l]),
                 e.dma_start(out=rwm[:], in_=rwmask_d)]
            return r
        sc.op("pool", ldw, writes=[tk_wts, tk_rwm] + tk_stage, owner=tk_wts, ndma=4)

        def cvt(e):
            e.tensor_copy(wupb[:], wupf[:])
            e.tensor_copy(aupb[:], aupf[:])
            return e.tensor_copy(gupb[:], gupf[:])
        sc.op("dve", cvt, reads=[tk_wts] + tk_stage, writes=[tk_wts])

        def mkmasks(e):
            e.tensor_copy(mAf[:, 0:128], rwm[:, 1, :])
            e.tensor_copy(mAf[:, 128:256], rwm[:, 3, :])
            e.tensor_copy(mAb[:, 0:128], rwm[:, 0, :])
            return e.tensor_copy(mAb[:, 128:256], rwm[:, 2, :])
        sc.op("dve", mkmasks, reads=[tk_rwm], writes=[tk_mA])

        seq("pool", [lambda e: e.memset(mask01[:], 1.0),
                     lambda e: e.memset(mask01[:].rearrange("p (a b) -> p a b", b=64)[:, :, 0:1], 0.0)], [], [tk_m01])
        seq("pool", [lambda e: e.memset(bones[:], 0.0),
                     lambda e: e.memset(bones[0:64, 0:64], 1.0),
                     lambda e: e.memset(bones[64:128, 64:128], 1.0)], [], [tk_bones])
        for bdt, tkb in ((ARbd, tk_AR), (Btbd, tk_Bt), (Ktbd, tk_Kt), (Bhbd, tk_Bh), (Khbd, tk_Kh), (Vbd, tk_Vbd), (ynbd, tk_ynbd)):
            sc.op("pool", (lambda e, bdt=bdt: e.memset(bdt[:], 0.0)), writes=[tkb])
        sc.op("pool", lambda e: e.memset(Hs[:], 0.0), writes=[tk_H])

    def load_shift(src3, P, ncol, ub, tk_ub, mu, out_ap, tk_out, t0, eng="pool"):
        tlo = max(t0 - 1, 0)
        thi = min(t0 + TT_ + 1, T)
        off = tlo - (t0 - 1)
        n = thi - tlo

        def ld(e):
            return e.dma_start(out=ub[:, :, off:off + n], in_=src3[:, :, tlo:thi])
        sc.op("pool", ld, reads=[tk_projT], writes=[tk_ub], owner=tk_ub)
        if t0 == 0:
            sc.op(eng, lambda e: e.memset(ub[:, :, 0:1], 0.0), writes=[tk_ub])
        elif t0 % SEGLEN == 0:
            sc.op(eng, lambda e: e.tensor_scalar(ub[:, :, 0:1], ub[:, :, 0:1], flags[:P, 0:1], None, ALU.mult),
                  reads=[tk_flags], writes=[tk_ub])
        if t0 + TT_ == T:
            sc.op(eng, lambda e: e.memset(ub[:, :, TT_ + 1:TT_ + 2], 0.0), writes=[tk_ub])
        elif (t0 + TT_) % SEGLEN == 0:
            sc.op(eng, lambda e: e.tensor_scalar(ub[:, :, TT_ + 1:TT_ + 2], ub[:, :, TT_ + 1:TT_ + 2], flags[:P, 0:1], None, ALU.mult),
                  reads=[tk_flags], writes=[tk_ub])
        sh = shb[:P, :ncol, :]
        u1 = ub[:, :, 1:TT_ + 1]

        seq(eng, [lambda e: e.tensor_tensor(sh, ub[:, :, 0:TT_], ub[:, :, 2:TT_ + 2], ALU.add),
                  lambda e: e.tensor_scalar(sh, sh, 0.5, None, ALU.mult),
                  lambda e: e.tensor_tensor(sh, sh, u1, ALU.subtract),
                  lambda e: e.tensor_tensor(sh, sh, bcast(mu, [P, ncol, TT_]), ALU.mult),
                  lambda e: e.tensor_tensor(out_ap, sh, u1, ALU.add)], [tk_ub, tk_par], [tk_sh, tk_out])

    def bd_write(eng, dst4, col0, fn, reads, tk_dst):
        for hh in range(2):
            ov = dst4[hh * 64:(hh + 1) * 64, :, :, col0 + hh * 64:col0 + (hh + 1) * 64]
            sc.op(eng, (lambda e, ov=ov, hh=hh: fn(e, ov, hh)), reads=reads, writes=[tk_dst])

    def half4(ap3, hh):
        return ap3[hh * 64:(hh + 1) * 64].rearrange("p a (c q) -> p a c q", q=64)

    import os as _os
    RWS = float(_os.environ.get("RWS", "9"))

    def rw_tile(l, d, ti):
        t0 = ti * TT_
        last = (d == 1)
        for grp in range(3):
            load_shift(projT[4 * grp:4 * grp + 4].rearrange("c p t -> p c t"), 128, 4, uA[grp % 2], tk_uA[grp % 2],
                       muA[:, 4 * grp:4 * grp + 4], xs3[grp][:], tk_x[grp], t0)
        load_shift(projT[12:14].rearrange("c (two p) t -> p (c two) t", two=2), 64, 4, uB, tk_uB, muB[:], xB[:], tk_xB, t0)
        if RWS < 2:
            return
        sc.op("dve", lambda e: e.tensor_tensor(kkn[:], xk[:], bcast(kks[:], [128, 4, TT_]), ALU.mult),
              reads=[tk_x[1], tk_par], writes=[tk_kkn])
        sc.op("dve", lambda e: e.tensor_tensor(sqkb[:], kkn[:], kkn[:], ALU.mult), reads=[tk_kkn], writes=[tk_sqkb])

        def mmss(e):
            r = None
            for hp in range(4):
                r = e.matmul(bank2(0)[:, hp * TT_:(hp + 1) * TT_], bones[:], sqkb[:, hp, :], start=True, stop=True)
            return r
        sc.op("pe", mmss, reads=[tk_sqkb, tk_bones], writes=[tk_pb[0], tk_pb[1]])
        t1f = t1[:].rearrange("p a b -> p (a b)")
        sc.op("act", lambda e: e.activation(t1f, bank2(0), AF.Sqrt), reads=[tk_pb[0], tk_pb[1]], writes=[tk_t1])

        seq("dve", [lambda e: e.tensor_scalar(t1f, t1f, 1e-12, None, ALU.max),
                    lambda e: e.reciprocal(t1f, t1f),
                    lambda e: e.tensor_tensor(kkn[:], kkn[:], t1[:], ALU.mult)], [], [tk_t1, tk_kkn])
        if RWS < 3:
            return
        sc.op("act", lambda e: e.activation(twb[:], xB[:, d, :], AF.Tanh), reads=[tk_xB], writes=[tk_twb])
        sc.op("act", lambda e: e.activation(alb[:], xB[:, 2:4, :], AF.Copy), reads=[tk_xB], writes=[tk_alb])

        def mmz(e):
            r = None
            for hp in range(4):
                r = e.matmul(bank2(2)[:, hp * TT_:(hp + 1) * TT_], wupb[:, d, hp * 128:(hp + 1) * 128], twb[:], start=True, stop=True)
            return r
        sc.op("pe", mmz, reads=[tk_twb, tk_wts], writes=[tk_pb[2], tk_pb[3]])

        def mma(dd, bk):
            def f(e):
                r = None
                for hp in range(4):
                    r = e.matmul(bank2(bk)[:, hp * TT_:(hp + 1) * TT_], aupb[:, dd, hp * 128:(hp + 1) * 128], alb[:, dd, :], start=True, stop=True)
                return r
            return f
        sc.op("pe", mma(d, 4), reads=[tk_alb, tk_wts], writes=[tk_pb[4], tk_pb[5]])

        def sigz(e):
            r = None
            for hp in range(4):
                r = e.activation(sig[:, hp, :], bank2(2)[:, hp * TT_:(hp + 1) * TT_], AF.Sigmoid, bias=w0s[:, d * 4 + hp:d * 4 + hp + 1])
            return r
        sc.op("act", sigz, reads=[tk_pb[2], tk_pb[3], tk_par], writes=[tk_sig])

        def siga(dst, dd, bk):
            def f(e):
                r = None
                for hp in range(4):
                    r = e.activation(dst[:, hp, :], bank2(bk)[:, hp * TT_:(hp + 1) * TT_], AF.Sigmoid, bias=a0s[:, dd * 4 + hp:dd * 4 + hp + 1])
                return r
            return f
        sc.op("act", siga(asg, d, 4), reads=[tk_pb[4], tk_pb[5], tk_par], writes=[tk_asg])
        sc.op("dve", lambda e: e.tensor_tensor(bb[:], kkn[:], asg[:], ALU.mult), reads=[tk_kkn, tk_asg], writes=[tk_bb])

        seq("dve", [lambda e: e.scalar_tensor_tensor(out=kd[:], in0=asg[:], scalar=-1.0, in1=bcast(kas[:], [128, 4, TT_]), op0=ALU.add, op1=ALU.mult),
                    lambda e: e.scalar_tensor_tensor(out=kd[:], in0=kd[:], scalar=1.0, in1=xk[:], op0=ALU.add, op1=ALU.mult)],
            [tk_asg, tk_par, tk_x[1]], [tk_kd])
        if RWS < 4:
            return
        csf = cs[:].rearrange("p a b -> p (a b)")
        sc.op("dve", lambda e: e.tensor_tensor_scan(csf, mask01[:], sig[:].rearrange("p a b -> p (a b)"), 0.0, ALU.mult, ALU.add),
              reads=[tk_sig, tk_m01], writes=[tk_cs])
        cs16 = cs[:].rearrange("p a (c q) -> p (a c) q", q=64)
        totb = cs16[:, :, 63:64].to_broadcast([128, 16, 64])
        as16 = lambda ap: ap[:].rearrange("p a (c q) -> p (a c) q", q=64)
        sc.op("act", lambda e: e.activation(pcb[:], cs16[:, :, 63], AF.Exp, scale=-SC), reads=[tk_cs], writes=[tk_pc])
        if d == 0:
            inc, tk_inc = cs, tk_cs
            sc.op("dve", lambda e: e.tensor_tensor(t1[:], cs[:], sig[:], ALU.subtract), reads=[tk_cs, tk_sig], writes=[tk_t1])
            exc, tk_exc = t1, tk_t1
            sc.op("dve", lambda e: e.tensor_tensor(as16(t2), totb, cs16, ALU.subtract), reads=[tk_cs], writes=[tk_t2])
            rem, tk_rem = t2, tk_t2
        else:
            sc.op("dve", lambda e: e.tensor_tensor(as16(t2), totb, cs16, ALU.subtract), reads=[tk_cs], writes=[tk_t2])
            exc, tk_exc = t2, tk_t2
            sc.op("dve", lambda e: e.tensor_tensor(t1[:], t2[:], sig[:], ALU.add), reads=[tk_t2, tk_sig], writes=[tk_t1])
            inc, tk_inc = t1, tk_t1
            sc.op("dve", lambda e: e.tensor_tensor(sig[:], cs[:], sig[:], ALU.subtract), reads=[tk_cs, tk_sig], writes=[tk_sig])
            rem, tk_rem = sig, tk_sig
        if RWS < 4.2:
            return
        sc.op("act", lambda e: e.activation(e0[:], inc[:], AF.Exp, scale=-SC), reads=[tk_inc], writes=[tk_e0])
        bd_write("dve", ARbd, 128, lambda e, ov, hh: e.tensor_tensor(ov, half4(xr, hh), half4(e0, hh), ALU.mult),
                 [tk_x[0], tk_e0], tk_AR)
        if RWS < 4.4:
            return
        sc.op("act", lambda e: e.activation(e1[:], inc[:], AF.Exp, scale=SC), reads=[tk_inc], writes=[tk_e1])
        bd_write("pool", Btbd, 0, lambda e, ov, hh: e.tensor_tensor(ov, half4(bb, hh), half4(e1, hh), ALU.mult),
                 [tk_bb, tk_e1], tk_Bt)
        bd_write("dve", Ktbd, 0, lambda e, ov, hh: e.tensor_tensor(ov, half4(kd, hh), half4(e1, hh), ALU.mult),
                 [tk_kd, tk_e1], tk_Kt)
        if RWS < 4.6:
            return
        sc.op("act", lambda e: e.activation(e0[:], exc[:], AF.Exp, scale=-SC), reads=[tk_exc], writes=[tk_e0])
        bd_write("dve", ARbd, 0, lambda e, ov, hh: e.scalar_tensor_tensor(out=ov, in0=half4(kkn, hh), scalar=-1.0, in1=half4(e0, hh),
                                                                          op0=ALU.mult, op1=ALU.mult),
                 [tk_kkn, tk_e0], tk_AR)
        sc.op("act", lambda e: e.activation(e1[:], rem[:], AF.Exp, scale=-SC), reads=[tk_rem], writes=[tk_e1])
        bd_write("dve", Bhbd, 0, lambda e, ov, hh: e.tensor_tensor(ov, half4(bb, hh), half4(e1, hh), ALU.mult),
                 [tk_bb, tk_e1], tk_Bh)
        bd_write("pool", Khbd, 0, lambda e, ov, hh: e.tensor_tensor(ov, half4(kd, hh), half4(e1, hh), ALU.mult),
                 [tk_kd, tk_e1], tk_Kh)
        bd_write("pool", Vbd, 0, lambda e, ov, hh: e.tensor_copy(ov, half4(xv, hh)), [tk_x[2]], tk_Vbd)
        if last:
            sc.op("pe", mma(0, 6), reads=[tk_alb, tk_wts], writes=[tk_pb[6], tk_pb[7]])
            sc.op("act", siga(asf, 0, 6), reads=[tk_pb[6], tk_pb[7], tk_par], writes=[tk_asf])

            seq("dve", [lambda e: e.tensor_tensor(asf[:], asf[:], asg[:], ALU.add),
                        lambda e: e.scalar_tensor_tensor(out=asf[:], in0=asf[:], scalar=-2.0, in1=bcast(kas[:], [128, 4, TT_]), op0=ALU.add, op1=ALU.mult),
                        lambda e: e.scalar_tensor_tensor(out=asf[:], in0=asf[:], scalar=2.0, in1=xk[:], op0=ALU.add, op1=ALU.mult),
                        lambda e: e.tensor_tensor(asf[:], asf[:], xr[:], ALU.mult),
                        lambda e: e.tensor_tensor(sqkb[:], asf[:], bcast(rks[:], [128, 4, TT_]), ALU.mult)],
                [tk_asg, tk_par, tk_x[0], tk_x[1]], [tk_asf, tk_sqkb])

            def mmbd(e):
                r = None
                for hp in range(4):
                    r = e.matmul(bank2(6)[:, hp * TT_:(hp + 1) * TT_], bones[:], sqkb[:, hp, :], start=True, stop=True)
                return r
            sc.op("pe", mmbd, reads=[tk_sqkb, tk_bones], writes=[tk_pb[6], tk_pb[7]])
            sc.op("dve", lambda e: e.tensor_tensor(bonus[:].rearrange("p a b -> p (a b)"), bank2(6), xv[:].rearrange("p a b -> p (a b)"), ALU.mult),
                  reads=[tk_pb[6], tk_pb[7], tk_x[2]], writes=[tk_bonus])
            load_shift(projT[14:15].rearrange("c p t -> p c t"), 128, 1, uC, tk_uC, muC[:], xCf[:], tk_xCf, t0)
            sc.op("act", lambda e: e.activation(xCb[:], xCf[:, 0, :], AF.Sigmoid), reads=[tk_xCf], writes=[tk_xCb])

            def mmg(e):
                r = None
                for hp in range(4):
                    r = e.matmul(bank2(6)[:, hp * TT_:(hp + 1) * TT_], gupb[:, hp * 128:(hp + 1) * 128], xCb[:], start=True, stop=True)
                return r
            sc.op("pe", mmg, reads=[tk_xCb, tk_wts], writes=[tk_pb[6], tk_pb[7]])
            sc.op("act", lambda e: e.activation(gbuf[:].rearrange("p a b -> p (a b)"), bank2(6), AF.Copy),
                  reads=[tk_pb[6], tk_pb[7]], writes=[tk_g])
            sc.op("pool", (lambda e: e.dma_start(out=Yf[:], in_=yf_d[:, ti * NCH:(ti + 1) * NCH, :, :])),
                  reads=[tk_yfd], writes=[tk_Yf], owner=tk_Yf)
        if RWS < 5:
            return
        bnd = (t0 > 0 and t0 % SEGLEN == 0) if d == 0 else (t0 + TT_ < T and (t0 + TT_) % SEGLEN == 0)
        if bnd:
            sc.op("dve", lambda e: e.tensor_scalar(Hs[:], Hs[:], flags[:, 0:1], None, ALU.mult), reads=[tk_flags], writes=[tk_H])
        first_tile = (ti == 0) if d == 0 else (ti == NTR - 1)
        if bnd or first_tile:
            sc.op("act", lambda e: e.activation(Hb[:], Hs[:], AF.Copy), reads=[tk_H], writes=[tk_Hb])
        mA = mAf if d == 0 else mAb
        mB = rwm[:, 0, :] if d == 0 else rwm[:, 1, :]
        TB0 = int(_os.environ.get("TB0", "0"))
        TB1 = int(_os.environ.get("TB1", "1"))
        pbT0 = pb[TB0].bitcast(BF16)
        pbT1 = pb[TB1].bitcast(BF16)
        chs = range(NCH) if d == 0 else range(NCH - 1, -1, -1)
        for ch in chs:
            def trs(e, ch=ch):
                r = None
                for hp in range(4):
                    e.matmul(pb[0][:, hp * 128:(hp + 1) * 128], Vbd[:, hp, ch, :], identb[:], start=True, stop=True)
                    e.matmul(pb[1][:, hp * 128:(hp + 1) * 128], Bhbd[:, hp, ch, :], identb[:], start=True, stop=True)
                    r = e.matmul(pb[7][:, hp * 128:(hp + 1) * 128], Khbd[:, hp, ch, :], identb[:], start=True, stop=True)
                return r
            sc.op("pe", trs, reads=[tk_Vbd, tk_Bh, tk_Kh, tk_identb], writes=[tk_pb[0], tk_pb[1], tk_pb[7]])
            sc.op("act", lambda e: e.activation(Vtm[:].rearrange("p a b -> p (a b)"), pb[0], AF.Copy), reads=[tk_pb[0]], writes=[tk_Vtm])
            sc.op("dve", lambda e: e.tensor_copy(Bhtm[:].rearrange("p a b -> p (a b)"), pb[1]), reads=[tk_pb[1]], writes=[tk_Bhtm])
            sc.op("act", lambda e: e.activation(Khtm[:].rearrange("p a b -> p (a b)"), pb[7], AF.Copy), reads=[tk_pb[7]], writes=[tk_Khtm])

            if RWS < 6:
                continue
            def prods(e, ch=ch):
                r = None
                for hp in range(4):
                    e.matmul(bank2(2)[:, hp * 256:(hp + 1) * 256], Btbd[:, hp, ch, :], ARbd[:, hp, ch, :], start=True, stop=True)
                    e.matmul(bank2(4)[:, hp * 256:(hp + 1) * 256], Ktbd[:, hp, ch, :], ARbd[:, hp, ch, :], start=True, stop=True)
                    r = e.matmul(pb[6][:, hp * 128:(hp + 1) * 128], ARbd[:, hp, ch, 0:128], Btbd[:, hp, ch, :], start=True, stop=True)
                return r
            sc.op("pe", prods, reads=[tk_AR, tk_Bt, tk_Kt], writes=[tk_pb[2], tk_pb[3], tk_pb[4], tk_pb[5], tk_pb[6]])
            mA4 = mA[:].unsqueeze(1).to_broadcast([128, 4, 256])
            mB4 = mB.unsqueeze(1).to_broadcast([128, 4, 128])
            sc.op("dve", lambda e: e.tensor_tensor(M1[:], bank2(2).rearrange("p (a b) -> p a b", a=4), mA4, ALU.mult),
                  reads=[tk_pb[2], tk_pb[3], tk_mA], writes=[tk_M1])
            sc.op("dve", lambda e: e.tensor_tensor(M2[:], bank2(4).rearrange("p (a b) -> p a b", a=4), mA4, ALU.mult),
                  reads=[tk_pb[4], tk_pb[5], tk_mA], writes=[tk_M2])
            sc.op("dve", lambda e: e.tensor_tensor(Sb[0][:], pb[6].rearrange("p (a b) -> p a b", a=4), mB4, ALU.mult),
                  reads=[tk_pb[6], tk_rwm], writes=[tk_Sb[0]])
            sc.op("pool", lambda e: e.tensor_copy(STb[0][:], M1[:, :, 0:128]), reads=[tk_M1], writes=[tk_STb[0]])
            idb4 = identb[:].unsqueeze(1).to_broadcast([128, 4, 128])
            sc.op("pool", lambda e: e.tensor_tensor(TTb[0][:], M1[:, :, 0:128], idb4, ALU.add), reads=[tk_M1, tk_identb], writes=[tk_TTb[0]])
            if RWS < 7:
                continue
            cur = 0
            for lev in range(1, 6):
                nxt = 1 - cur

                def sq1(e, cur=cur):
                    r = None
                    for hp in range(4):
                        r = e.matmul(pb[7][:, hp * 128:(hp + 1) * 128], STb[cur][:, hp, :], Sb[cur][:, hp, :], start=True, stop=True)
                    return r
                sc.op("pe", sq1, reads=[tk_STb[cur], tk_Sb[cur]], writes=[tk_pb[7]])
                if lev < 5:
                    def sq2(e, cur=cur):
                        r = None
                        for hp in range(4):
                            r = e.matmul(pb[0][:, hp * 128:(hp + 1) * 128], Sb[cur][:, hp, :], STb[cur][:, hp, :], start=True, stop=True)
                        return r
                    sc.op("pe", sq2, reads=[tk_STb[cur], tk_Sb[cur]], writes=[tk_pb[0]])
                sc.op("act", (lambda e, nxt=nxt: e.activation(Sb[nxt][:].rearrange("p a b -> p (a b)"), pb[7], AF.Copy)),
                      reads=[tk_pb[7]], writes=[tk_Sb[nxt]])
                if lev < 5:
                    sc.op("dve", (lambda e, nxt=nxt: e.tensor_copy(STb[nxt][:].rearrange("p a b -> p (a b)"), pb[0])),
                          reads=[tk_pb[0]], writes=[tk_STb[nxt]])

                def ttu(e, cur=cur, nxt=nxt):
                    r = None
                    for hp in range(4):
                        r = e.matmul(pb[1][:, hp * 128:(hp + 1) * 128], Sb[nxt][:, hp, :], TTb[cur][:, hp, :], start=True, stop=True)
                    return r
                sc.op("pe", ttu, reads=[tk_Sb[nxt], tk_TTb[cur]], writes=[tk_pb[1]])
                sc.op("dve", (lambda e, cur=cur, nxt=nxt: e.tensor_tensor(TTb[nxt][:].rearrange("p a b -> p (a b)"),
                                                                          TTb[cur][:].rearrange("p a b -> p (a b)"), pb[1], ALU.add)),
                      reads=[tk_pb[1], tk_TTb[cur]], writes=[tk_TTb[nxt]])
                cur = nxt
            TTf, tk_TTf = TTb[cur], tk_TTb[cur]
            if RWS < 8:
                continue

            def mmw(e, ch=ch):
                r = None
                for hp in range(4):
                    e.matmul(pb[2][:, hp * 128:(hp + 1) * 128], ARbd[:, hp, ch, 0:128], Hb[:, hp, :], start=True, stop=False)
                    r = e.matmul(pb[2][:, hp * 128:(hp + 1) * 128], M2[:, hp, 0:128], Vtm[:, hp, :], start=False, stop=True)
                return r
            sc.op("pe", mmw, reads=[tk_AR, tk_Hb, tk_M2, tk_Vtm], writes=[tk_pb[2]])
            sc.op("act", lambda e: e.activation(Wb[:].rearrange("p a b -> p (a b)"), pb[2], AF.Copy), reads=[tk_pb[2]], writes=[tk_Wb])

            def mmu(e, TTf=TTf):
                r = None
                for hp in range(4):
                    r = e.matmul(pb[3][:, hp * 128:(hp + 1) * 128], TTf[:, hp, :], Wb[:, hp, :], start=True, stop=True)
                return r
            sc.op("pe", mmu, reads=[tk_TTf, tk_Wb], writes=[tk_pb[3]])
            sc.op("act", lambda e: e.activation(Ub[:].rearrange("p a b -> p (a b)"), pb[3], AF.Copy), reads=[tk_pb[3]], writes=[tk_Ub])

            def mmy(e, ch=ch):
                r = None
                for hp in range(4):
                    o = pb[4][:, hp * 128:(hp + 1) * 128]
                    e.matmul(o, ARbd[:, hp, ch, 128:256], Hb[:, hp, :], start=True, stop=False)
                    e.matmul(o, M1[:, hp, 128:256], Ub[:, hp, :], start=False, stop=False)
                    r = e.matmul(o, M2[:, hp, 128:256], Vtm[:, hp, :], start=False, stop=True)
                return r
            sc.op("pe", mmy, reads=[tk_AR, tk_Hb, tk_M1, tk_Ub, tk_M2, tk_Vtm], writes=[tk_pb[4]])

            def mmh(e):
                r = None
                for hp in range(4):
                    o = pb[5][:, hp * 128:(hp + 1) * 128]
                    e.matmul(o, Bhtm[:, hp, :], Ub[:, hp, :], start=True, stop=False)
                    r = e.matmul(o, Khtm[:, hp, :], Vtm[:, hp, :], start=False, stop=True)
                return r
            sc.op("pe", mmh, reads=[tk_Bhtm, tk_Ub, tk_Khtm, tk_Vtm], writes=[tk_pb[5]])
            pcv = pcb[:].rearrange("p (a c) -> p a c", c=NCH)[:, :, ch:ch + 1].to_broadcast([128, 4, 128])

            seq("dve", [(lambda e, pcv=pcv: e.tensor_tensor(Hs[:], Hs[:], pcv, ALU.mult)),
                        lambda e: e.tensor_tensor(Hs[:], Hs[:], pb[5].rearrange("p (a b) -> p a b", a=4), ALU.add)],
                [tk_pb[5], tk_pc], [tk_H])
            sc.op("act", lambda e: e.activation(Hb[:], Hs[:], AF.Copy), reads=[tk_H], writes=[tk_Hb])
            for hh in range(2):
                src = pb[4][hh * 64:(hh + 1) * 64, :].rearrange("p (a b) -> p a b", a=4)[:, :, hh * 64:(hh + 1) * 64]
                dst = Yt[hh * 64:(hh + 1) * 64, ch, :, :]
                if last:
                    yfv = Yf[hh * 64:(hh + 1) * 64, ch, :, :]
                    sc.op("dve", (lambda e, src=src, dst=dst, yfv=yfv: e.tensor_tensor(dst, src, yfv, ALU.add)),
                          reads=[tk_pb[4], tk_Yf], writes=[tk_Yt])
                else:
                    sc.op("act", (lambda e, src=src, dst=dst: e.activation(dst, src, AF.Copy)), reads=[tk_pb[4]], writes=[tk_Yt])
        if RWS < 9:
            return
        if not last:
            sc.op("pool", (lambda e: e.dma_start(out=yf_d[:, ti * NCH:(ti + 1) * NCH, :, :], in_=Yt[:])),
                  reads=[tk_Yt], writes=[tk_yfd], owner=tk_Yt)
            return
        Y16 = Yt[:].rearrange("p c a v -> p (c a) v")
        cen16 = cen[:].rearrange("p c a v -> p (c a) v")

        seq("dve", [lambda e: e.tensor_reduce(st1[:], Y16, AX.X, ALU.add),
                    lambda e: e.tensor_scalar(st1[:], st1[:], -1.0 / 64, None, ALU.mult),
                    lambda e: e.tensor_tensor(cen16, Y16, st1[:].unsqueeze(2).to_broadcast([128, 16, 64]), ALU.add),
                    lambda e: e.tensor_tensor(Y16, cen16, cen16, ALU.mult),
                    lambda e: e.tensor_reduce(st2[:], Y16, AX.X, ALU.add)], [], [tk_Yt, tk_cen, tk_st])
        sc.op("act", lambda e: e.activation(st2[:], st2[:], AF.Sqrt, bias=64e-5, scale=1.0 / 64), reads=[tk_st], writes=[tk_st])
        sc.op("dve", lambda e: e.reciprocal(st2[:], st2[:]), reads=[tk_st], writes=[tk_st])
        for hh in range(2):
            ov = ynbd[hh * 64:(hh + 1) * 64, :, :, hh * 64:(hh + 1) * 64]
            iv = cen[hh * 64:(hh + 1) * 64]
            rv = st2[hh * 64:(hh + 1) * 64, :].rearrange("p (c a) -> p c a", a=4).unsqueeze(3).to_broadcast([64, NCH, 4, 64])
            sc.op("dve", (lambda e, ov=ov, iv=iv, rv=rv: e.tensor_tensor(ov, iv, rv, ALU.mult)), reads=[tk_cen, tk_st], writes=[tk_ynbd])
        for ch in range(NCH):
            def try_(e, ch=ch):
                r = None
                for hp in range(4):
                    r = e.matmul(pb[0][:, hp * 128:(hp + 1) * 128], ynbd[:, ch, hp, :], identb[:], start=True, stop=True)
                return r
            sc.op("pe", try_, reads=[tk_ynbd, tk_identb], writes=[tk_pb[0]])
            for hh in range(2):
                src = pb[0][hh * 64:(hh + 1) * 64, :].rearrange("p (a b) -> p a b", a=4)[:, :, hh * 64:(hh + 1) * 64]
                dst = ynT[hh * 64:(hh + 1) * 64, :, ch * 64:(ch + 1) * 64]
                if hh == 0:
                    sc.op("act", (lambda e, src=src, dst=dst: e.activation(dst, src, AF.Copy)), reads=[tk_pb[0]], writes=[tk_ynT])
                else:
                    sc.op("dve", (lambda e, src=src, dst=dst: e.tensor_copy(dst, src)), reads=[tk_pb[0]], writes=[tk_ynT])

        seq("dve", [lambda e: e.tensor_tensor(ynT[:], ynT[:], bcast(lws[:], [128, 4, TT_]), ALU.mult),
                    lambda e: e.tensor_tensor(ynT[:], ynT[:], bcast(lbs[:], [128, 4, TT_]), ALU.add),
                    lambda e: e.tensor_tensor(ynT[:], ynT[:], bonus[:], ALU.add),
                    lambda e: e.tensor_tensor(orw[:], ynT[:], gbuf[:], ALU.mult)], [tk_par, tk_bonus, tk_g], [tk_ynT, tk_orw])
        sc.op("pool", (lambda e: e.dma_start(out=yrwT[:, :, t0:t0 + TT_], in_=orw[:])), reads=[tk_orw], writes=[tk_yrwT], owner=tk_orw)

    def rw_phase(l):
        rw_setup(l)
        for ti in range(NTR):
            rw_tile(l, 0, ti)
        sc.op("pool", lambda e: e.memset(Hs[:], 0.0), writes=[tk_H])
        for ti in range(NTR - 1, -1, -1):
            rw_tile(l, 1, ti)


    def mixers(l):
        sc.barrier()
        if mode in ("full", "na"):
            na_phase(l)
        else:
            zero_fill(ynaT, tk_ynaT, 64)
        sc.barrier()
        if mode in ("full", "rw"):
            rw_phase(l)
        else:
            zero_fill(yrwT, tk_yrwT, 128)
        sc.barrier()

    for t in range(NT):
        load_x_tile(t)
        phase_a(0, t)
    for l in range(LRUN):
        if do_mix:
            mixers(l)
        for t in range(NT):
            phase_c(l, t)
            if l + 1 < LRUN:
                phase_a(l + 1, t)
            else:
                store_out_tile(t)
    sc.finish("pool")
    sc.emit()
    return nc, sc


def arrange(W, kp):
    Kd, Nd = W.shape
    KC = Kd // kp
    NJ = Nd // 128
    return np.ascontiguousarray(W.reshape(KC, kp, NJ, 128).transpose(2, 1, 0, 3)).reshape(NJ, kp, KC * 128)


def host_weights(inp):
    out = {}

    def st(fn):
        return np.stack([fn(l) for l in range(L)], 0)
    out["w_wg1"] = st(lambda l: arrange(inp["ffn1_wg"][l], 128))
    out["w_wu1"] = st(lambda l: arrange(inp["ffn1_wu"][l], 128))
    out["w_wd1"] = st(lambda l: arrange(inp["ffn1_wd"][l], 128))
    out["w_win"] = st(lambda l: arrange(inp["w_in"][l], 128))
    out["w_wona"] = st(lambda l: arrange(inp["w_out"][l][:512], 64))
    out["w_worw"] = st(lambda l: arrange(inp["w_out"][l][512:], 128))
    out["w_wg2"] = st(lambda l: arrange(inp["ffn2_wg"][l], 128))
    out["w_wu2"] = st(lambda l: arrange(inp["ffn2_wu"][l], 128))
    out["w_wd2"] = st(lambda l: arrange(inp["ffn2_wd"][l], 128))
    out["w_pgate"] = st(lambda l: arrange(inp["ple_gate"][l], 128))
    out["w_pup"] = st(lambda l: arrange(inp["ple_up"][l], 128))
    for nm in NORMS:
        out[nm] = np.ascontiguousarray(inp[nm], dtype=np.float32)
    out["final_norm"] = np.ascontiguousarray(inp["final_norm"], dtype=np.float32)
    return out


def host_extra(inp):
    rpb = np.asarray(inp["na_rpb"], dtype=np.float32)
    tb = np.full((L, 2, 64, 8, 14, 64), NEG, np.float32)
    j = np.arange(64)
    cs = np.clip(j - 8, 0, 48)
    for par in range(2):
        for m in range(14):
            if m + par > 14:
                continue
            for cp in range(64):
                ok = (cp >= cs) & (cp < cs + 16)
                jj = j[ok]
                tb[:, par, cp, :, m, jj] = np.transpose(rpb[:, :, m + par, cp - jj + 15], (2, 0, 1))
    out = {"tb": np.ascontiguousarray(tb.reshape(L, 128, 8 * 14 * 64))}
    p = np.arange(128)[:, None]
    f = np.arange(128)[None, :]
    same = (p // 64) == (f // 64)
    pl, fl = p % 64, f % 64
    rwm = np.stack([same & (fl < pl), same & (fl > pl), same & (fl <= pl), same & (fl >= pl)], 1).astype(np.float32)
    out["rwmask"] = np.ascontiguousarray(rwm)
    for nm in ("rw_mu", "rw_w0", "rw_w_up", "rw_a0", "rw_a_up", "rw_g_up", "rw_k_k", "rw_k_a", "rw_lnx_w", "rw_lnx_b"):
        out[nm] = np.ascontiguousarray(inp[nm], dtype=np.float32)
    out["rw_r_k"] = np.ascontiguousarray(inp["rw_r_k"], dtype=np.float32).reshape(L, 512)
    return out


def kernel(**inputs):
    NSEG, SEGLEN = 4, 4096
    TC = NSEG * SEGLEN
    inp = {k: np.asarray(v) for k, v in inputs.items()}
    nc, sc = build_program(NSEG, SEGLEN, "full")
    hw = host_weights(inp)
    hw.update(host_extra(inp))
    xp = np.asarray(inp["x_prompt"], dtype=np.float32)
    xs = np.asarray(inp["x_sample"], dtype=np.float32)
    pp = np.asarray(inp["p_prompt"], dtype=np.float32)
    ps = np.asarray(inp["p_sample"], dtype=np.float32)
    zx = np.zeros((TC, D), np.float32)
    zp = np.zeros((L, TC, PLE), np.float32)
    in_maps = []
    for c in range(8):
        m = dict(hw)
        fl = np.zeros((128, 2), np.float32)
        if c < 4:
            m["xin"] = np.ascontiguousarray(xp[4 * c:4 * c + 4].reshape(TC, D))
            m["pin"] = np.ascontiguousarray(pp[:, 4 * c:4 * c + 4].reshape(L, TC, PLE))
            fl[:, 1] = 1.0
        elif c < 6:
            m["xin"] = np.ascontiguousarray(xs[c - 4])
            m["pin"] = np.ascontiguousarray(ps[:, c - 4])
            fl[:, 0] = 1.0
        else:
            m["xin"] = zx
            m["pin"] = zp
            fl[:, 1] = 1.0
        m["flags"] = fl
        in_maps.append(m)
    res = run_bass_kernel_spmd(nc, in_maps, core_ids=list(range(8)))
    outs = [np.asarray(r["yout"], dtype=np.float32) for r in res.results]
    y_prompt = np.concatenate([outs[c].reshape(4, SEGLEN, D) for c in range(4)], axis=0)
    y_sample = np.stack([outs[4], outs[5]], axis=0)
    return (y_prompt, y_sample)
```

```python
import numpy as np
import concourse.bass as bass
import concourse.mybir as mybir
from concourse.bass_utils import run_bass_kernel_spmd

F32 = mybir.dt.float32
BF16 = mybir.dt.bfloat16
AF = mybir.ActivationFunctionType
ALU = mybir.AluOpType
AX = mybir.AxisListType

D = 1024
DFF = 2816
NJF = DFF // 128
PW = 3456
NJP = PW // 128
PLE = 256
L = 2
NEG = -30000.0
ENGS = ("pe", "act", "dve", "pool", "sp")


class Tk:
    __slots__ = ("name", "w", "r", "sem", "semval", "last_dma")

    def __init__(self, name):
        self.name = name
        self.w = None
        self.r = {}
        self.sem = None
        self.semval = 0
        self.last_dma = None


class Sched:
    def __init__(self, nc):
        self.nc = nc
        self.q = {e: [] for e in ENGS}
        self.esem = {e: nc.alloc_semaphore("es_" + e) for e in ENGS}
        self.ecount = {e: 0 for e in ENGS}
        self.waited = {e: {} for e in ENGS}
        self.nops = 0
        self.nsem = 0
        self.owners = []

    def tk(self, name):
        return Tk(name)

    def _need(self, eng, ev):
        if ev is None:
            return
        sem, val = ev
        k = id(sem)
        if self.waited[eng].get(k, 0) < val:
            self.waited[eng][k] = val
            self.q[eng].append(("w", sem, val))

    def op(self, eng, fn, reads=(), writes=(), owner=None, ndma=1):
        for t in reads:
            self._need(eng, t.w)
        for t in writes:
            self._need(eng, t.w)
            for ev in t.r.values():
                self._need(eng, ev)
        if owner is not None:
            if owner.sem is None:
                owner.sem = self.nc.alloc_semaphore("ds%d" % self.nsem)
                self.nsem += 1
                self.owners.append(owner)
            self._need(eng, owner.last_dma)
            owner.semval += 16 * ndma
            ev = (owner.sem, owner.semval)
            owner.last_dma = ev
            self.q[eng].append(("d", fn, owner.sem))
        else:
            self.ecount[eng] += 1
            ev = (self.esem[eng], self.ecount[eng])
            self.q[eng].append(("o", fn, self.esem[eng]))
        for t in reads:
            t.r[id(ev[0])] = ev
        for t in writes:
            t.w = ev
            t.r = {}
        self.nops += 1
        return ev

    def barrier(self):
        evs = [(self.esem[e], self.ecount[e]) for e in ENGS if self.ecount[e]]
        evs += [o.last_dma for o in self.owners]
        for e in ENGS:
            for ev in evs:
                self._need(e, ev)

    def finish(self, eng="pool"):
        for e in ENGS:
            if self.ecount[e]:
                self._need(eng, (self.esem[e], self.ecount[e]))
        for o in self.owners:
            self._need(eng, o.last_dma)

    def emit(self):
        nc = self.nc
        q = self.q

        def run(e, name):
            for it in q[name]:
                if it[0] == "w":
                    e.wait_ge(it[1], it[2])
                elif it[0] == "o":
                    ins = it[1](e)
                    ins.then_inc(it[2], 1)
                else:
                    r = it[1](e)
                    if isinstance(r, (list, tuple)):
                        for ins in r:
                            ins.then_inc(it[2], 16)
                    else:
                        r.then_inc(it[2], 16)

        with nc.Block() as block:
            @block.tensor
            def _(e):
                run(e, "pe")

            @block.scalar
            def _(e):
                run(e, "act")

            @block.vector
            def _(e):
                run(e, "dve")

            @block.gpsimd
            def _(e):
                run(e, "pool")

            @block.sync
            def _(e):
                run(e, "sp")


WSPEC = {
    "wg1": (128, 8, NJF), "wu1": (128, 8, NJF), "wd1": (128, NJF, 8),
    "win": (128, 8, NJP), "wona": (64, 8, 8), "worw": (128, 4, 8),
    "wg2": (128, 8, NJF), "wu2": (128, 8, NJF), "wd2": (128, NJF, 8),
    "pgate": (128, 8, 8), "pup": (128, 2, 8),
}
NORMS = ("ffn1_norm", "mix_norm", "ffn2_norm", "ple_norm")


def build_program(NSEG, SEGLEN, mode="full"):
    do_mix = mode != "nomix"
    T = NSEG * SEGLEN
    NT = T // 512
    nc = bass.Bass("TRN2", target_bir_lowering=False)
    sc = Sched(nc)

    def dram_in(name, shape, dt=F32):
        return nc.dram_tensor(name, list(shape), dt, kind="ExternalInput").ap()

    def dram_tmp(name, shape, dt):
        return nc.dram_tensor(name, list(shape), dt, kind="Internal").ap()

    xin = dram_in("xin", [T, D])
    pin = dram_in("pin", [L, T, PLE])
    flags_d = dram_in("flags", [128, 2])
    yout = nc.dram_tensor("yout", [T, D], F32, kind="ExternalOutput").ap()
    wsrc = {}
    wbf = {}
    for nm, (kp, KC, NJ) in WSPEC.items():
        wsrc[nm] = dram_in("w_" + nm, [L, NJ, kp, KC * 128])
        wbf[nm] = dram_tmp("b_" + nm, [L, NJ, kp, KC * 128], BF16)
    norms_d = {nm: dram_in(nm, [L, D]) for nm in NORMS}
    fnorm_d = dram_in("final_norm", [D])
    h1T = dram_tmp("h1T", [8, 128, T], F32)
    projT = dram_tmp("projT", [15, 128, T], F32)
    qkvT = dram_tmp("qkvT", [12, 128, T], BF16)
    tk_qkvT = sc.tk("qkvT")
    ynaT = dram_tmp("ynaT", [64, 8, T], BF16)
    import os as _os0
    DBG = int(_os0.environ.get("DBG", "0"))
    LRUN = int(_os0.environ.get("LRUN", str(L)))
    if DBG:
        yrwT = nc.dram_tensor("yrwT", [128, 4, T], BF16, kind="ExternalOutput").ap()
    else:
        yrwT = dram_tmp("yrwT", [128, 4, T], BF16)
    tk_h1T = sc.tk("h1T")
    tk_projT = sc.tk("projT")
    tk_ynaT = sc.tk("ynaT")
    tk_yrwT = sc.tk("yrwT")

    def sb(name, shape, dt=F32):
        return nc.alloc_sbuf_tensor(name, list(shape), dt)

    ident = sb("ident", [128, 128]); tk_ident = sc.tk("ident")
    identb = sb("identb", [128, 128], BF16); tk_identb = sc.tk("identb")
    onesb = sb("onesb", [128, 128], BF16); tk_ones = sc.tk("ones")
    gains = sb("gains", [128, L * 4 + 1, 8]); tk_gains = sc.tk("gains")
    flags = sb("flags_s", [128, 2]); tk_flags = sc.tk("flags")
    NRING = 3
    RING_EL = NJF * 128
    ring_i = [0]
    ARENA = 165 * 1024
    arena = sb("arena", [128, ARENA // 4])

    def carver():
        off = [0]

        def view(o, shape, dt):
            esz = 4 if dt == F32 else 2
            n = 1
            for d_ in shape[1:]:
                n *= d_
            nbytes = (n * esz + 63) // 64 * 64
            assert o + nbytes <= ARENA, (o, nbytes)
            ap = arena[:shape[0], o // 4:(o + nbytes) // 4]
            if dt != F32:
                ap = ap.bitcast(dt)
            ap = ap[:, :n]
            if len(shape) == 3:
                ap = ap.rearrange("p (a b) -> p a b", a=shape[1])
            elif len(shape) == 4:
                ap = ap.rearrange("p (a b c) -> p a b c", a=shape[1], b=shape[2])
            return ap, nbytes

        def take(shape, dt=F32):
            ap, nbytes = view(off[0], shape, dt)
            take.last = off[0]
            off[0] += nbytes
            return ap

        def at(o, shape, dt=F32):
            return view(o, shape, dt)[0]
        take.at = at
        take.off = off
        return take

    tkW = carver()
    cst = [tkW([128, 4096]) for i in range(2)]; tk_cst = [sc.tk("cst%d" % i) for i in range(2)]
    cstb = [tkW([128, 4096], BF16) for i in range(2)]; tk_cstb = [sc.tk("cstb%d" % i) for i in range(2)]
    tkA = carver()
    h = tkA([128, 8, 512]); tk_h = [sc.tk("h%d" % c) for c in range(8)]
    nb = tkA([128, 8, 512], BF16); tk_n = sc.tk("n")
    sq = tkA([128, 8, 512], BF16); tk_sq = sc.tk("sq")
    act = tkA([128, NJF, 512], BF16); tk_act = [sc.tk("act%d" % j) for j in range(NJF)]
    rstd = tkA([128, 512]); tk_rstd = sc.tk("rstd")
    sg = [tkA([128, 512]) for i in range(2)]; tk_sg = [sc.tk("sg%d" % i) for i in range(2)]
    ev_f = [tkA([128, 512]) for i in range(3)]; tk_evf = [sc.tk("evf%d" % i) for i in range(3)]
    ev_b = [tkA([128, 512], BF16) for i in range(3)]; tk_evb = [sc.tk("evb%d" % i) for i in range(3)]
    tokbuf = tkA([128, 4, D]); tk_tok = sc.tk("tokbuf")
    pbuf = tkA([128, 4, PLE]); tk_pbuf = sc.tk("pbuf")
    pT = tkA([128, 2, 512], BF16); tk_pT = sc.tk("pT")
    ynat = tkA([64, 8, 512], BF16); tk_ynat = sc.tk("ynat")
    yrwt = tkA([128, 4, 512], BF16); tk_yrwt = sc.tk("yrwt")
    sqf = tkA([128, 8, 512]); tk_sqf = sc.tk("sqf")
    ring = [tkA([128, RING_EL], BF16) for i in range(NRING)]
    tk_ring = [sc.tk("ring%d" % i) for i in range(NRING)]

    pall = nc.alloc_psum_tensor("pall", [128, 4096], F32)
    pb = [pall[:, i * 512:(i + 1) * 512] for i in range(8)]
    tk_pb = [sc.tk("pb%d" % i) for i in range(8)]
    rr = [0]

    def next_bank():
        i = 1 + (rr[0] % 6)
        rr[0] += 1
        return pb[i], tk_pb[i]

    sc.op("pool", lambda e: e.memset(ident[:], 1.0), writes=[tk_ident])
    sc.op("pool", lambda e: e.affine_select(out=ident[:], in_=ident[:], pattern=[[-1, 128]],
                                            compare_op=ALU.is_equal, fill=0.0, base=0, channel_multiplier=1),
          reads=[tk_ident], writes=[tk_ident])
    sc.op("pool", lambda e: e.tensor_copy(identb[:], ident[:]), reads=[tk_ident], writes=[tk_identb])
    sc.op("pool", lambda e: e.memset(onesb[:], 1.0), writes=[tk_ones])
    sc.op("pool", lambda e: e.dma_start(out=flags[:], in_=flags_d), writes=[tk_flags], owner=tk_flags)

    def ld_gains(e):
        r = []
        for l in range(L):
            for i, nm in enumerate(NORMS):
                r.append(e.dma_start(out=gains[:, l * 4 + i, :],
                                     in_=norms_d[nm][l].rearrange("(c p) -> p c", p=128),
                                     allow_slow_non_contiguous=True))
        r.append(e.dma_start(out=gains[:, L * 4, :], in_=fnorm_d.rearrange("(c p) -> p c", p=128),
                             allow_slow_non_contiguous=True))
        return r
    sc.op("pool", ld_gains, writes=[tk_gains], owner=tk_gains, ndma=L * 4 + 1)

    cvt_i = [0]

    def convert(src_flat, dst_flat, nper):
        nblk = (nper + 4095) // 4096
        while nper % nblk:
            nblk += 1
        F = nper // nblk
        for b in range(nblk):
            i = cvt_i[0] % 2
            cvt_i[0] += 1
            s_ap = src_flat[:, b * F:(b + 1) * F]
            d_ap = dst_flat[:, b * F:(b + 1) * F]
            sc.op("sp", (lambda e, i=i, s_ap=s_ap, F=F: e.dma_start(out=cst[i][:, :F], in_=s_ap)),
                  writes=[tk_cst[i]], owner=tk_cst[i])
            eng = ("dve", "act")[i]
            if eng == "dve":
                sc.op("dve", (lambda e, i=i, F=F: e.tensor_copy(cstb[i][:, :F], cst[i][:, :F])),
                      reads=[tk_cst[i]], writes=[tk_cstb[i]])
            else:
                sc.op("act", (lambda e, i=i, F=F: e.activation(cstb[i][:, :F], cst[i][:, :F], AF.Copy)),
                      reads=[tk_cst[i]], writes=[tk_cstb[i]])
            sc.op("pool", (lambda e, i=i, d_ap=d_ap, F=F: e.dma_start(out=d_ap, in_=cstb[i][:, :F])),
                  reads=[tk_cstb[i]], owner=tk_cstb[i])

    tk_wbf = {}
    for nm, (kp, KC, NJ) in WSPEC.items():
        tot = L * NJ * kp * KC * 128
        nper = tot // 128
        s = wsrc[nm].rearrange("l j k x -> (l j k x)").rearrange("(p n) -> p n", p=128)
        d = wbf[nm].rearrange("l j k x -> (l j k x)").rearrange("(p n) -> p n", p=128)
        convert(s, d, nper)
        tk_wbf[nm] = sc.tk("wbf_" + nm)
    sc.barrier()

    def wload(nm, l, j):
        kp, KC, NJ = WSPEC[nm]
        s = ring_i[0] % NRING
        ring_i[0] += 1
        n = KC * 128
        src = wbf[nm][l, j]
        sc.op("sp", (lambda e, s=s, kp=kp, n=n, src=src: e.dma_start(out=ring[s][:kp, :n], in_=src)),
              writes=[tk_ring[s]], owner=tk_ring[s])
        return ring[s], tk_ring[s]

    def rmsnorm(gidx, out_fp32=None):
        sc.op("act", lambda e: e.activation(sq[:], h[:], AF.Square), reads=tk_h, writes=[tk_sq])

        def mm(e):
            r = None
            for c in range(8):
                r = e.matmul(pb[0][:], onesb[:], sq[:, c, :], start=(c == 0), stop=(c == 7))
            return r
        sc.op("pe", mm, reads=[tk_sq, tk_ones], writes=[tk_pb[0]])
        sc.op("act", lambda e: e.activation(rstd[:], pb[0][:], AF.Sqrt, bias=1e-6, scale=1.0 / D),
              reads=[tk_pb[0]], writes=[tk_rstd])
        sc.op("dve", lambda e: e.reciprocal(rstd[:], rstd[:]), reads=[tk_rstd], writes=[tk_rstd])
        dst = nb if out_fp32 is None else out_fp32

        def nrm(e):
            r = None
            for c in range(8):
                r = e.scalar_tensor_tensor(out=dst[:, c, :], in0=h[:, c, :], scalar=gains[:, gidx, c:c + 1],
                                           in1=rstd[:], op0=ALU.mult, op1=ALU.mult)
            return r
        return nrm

    def norm_to_n(gidx):
        nrm = rmsnorm(gidx)
        sc.op("dve", nrm, reads=tk_h + [tk_rstd, tk_gains], writes=[tk_n])

    def ffn(l, wg, wu, wd):
        for j in range(NJF):
            gbuf, gtk = wload(wg, l, j)
            ubuf, utk = wload(wu, l, j)
            pg, tpg = next_bank()
            pu, tpu = next_bank()

            def mmg(e, gbuf=gbuf, pg=pg):
                r = None
                for c in range(8):
                    r = e.matmul(pg[:], gbuf[:, c * 128:(c + 1) * 128], nb[:, c, :], start=(c == 0), stop=(c == 7))
                return r

            def mmu(e, ubuf=ubuf, pu=pu):
                r = None
                for c in range(8):
                    r = e.matmul(pu[:], ubuf[:, c * 128:(c + 1) * 128], nb[:, c, :], start=(c == 0), stop=(c == 7))
                return r
            sc.op("pe", mmg, reads=[gtk, tk_n], writes=[tpg])
            sc.op("pe", mmu, reads=[utk, tk_n], writes=[tpu])
            si = j % 2
            sc.op("act", (lambda e, si=si, pg=pg: e.activation(sg[si][:], pg[:], AF.Silu)),
                  reads=[tpg], writes=[tk_sg[si]])
            sc.op("dve", (lambda e, si=si, pu=pu, j=j: e.tensor_tensor(act[:, j, :], sg[si][:], pu[:], ALU.mult)),
                  reads=[tk_sg[si], tpu], writes=[tk_act[j]])
        for c in range(8):
            dbuf, dtk = wload(wd, l, c)
            po, tpo = next_bank()

            def mmd(e, dbuf=dbuf, po=po):
                r = None
                for j in range(NJF):
                    r = e.matmul(po[:], dbuf[:, j * 128:(j + 1) * 128], act[:, j, :], start=(j == 0), stop=(j == NJF - 1))
                return r
            sc.op("pe", mmd, reads=[dtk] + tk_act, writes=[tpo])
            sc.op("dve", (lambda e, po=po, c=c: e.scalar_tensor_tensor(out=h[:, c, :], in0=po[:], scalar=0.5, in1=h[:, c, :],
                                                                    op0=ALU.mult, op1=ALU.add)),
                  reads=[tpo, tk_h[c]], writes=[tk_h[c]])

    def load_x_tile(t):
        t0 = t * 512
        sc.op("pool", lambda e: e.dma_start(out=tokbuf[:], in_=xin[t0:t0 + 512, :].rearrange("(b p) d -> p b d", p=128)),
              writes=[tk_tok], owner=tk_tok)
        for c in range(8):
            def tr(e, c=c):
                r = None
                for b in range(4):
                    r = e.transpose(pb[7][:, b * 128:(b + 1) * 128], tokbuf[:, b, c * 128:(c + 1) * 128], ident[:])
                return r
            sc.op("pe", tr, reads=[tk_tok, tk_ident], writes=[tk_pb[7]])
            if c % 2 == 0:
                sc.op("act", (lambda e, c=c: e.activation(h[:, c, :], pb[7][:], AF.Copy)), reads=[tk_pb[7]], writes=[tk_h[c]])
            else:
                sc.op("dve", (lambda e, c=c: e.tensor_copy(h[:, c, :], pb[7][:])), reads=[tk_pb[7]], writes=[tk_h[c]])

    def store_out_tile(t):
        t0 = t * 512
        nrm = rmsnorm(L * 4, out_fp32=sqf)
        sc.op("dve", nrm, reads=tk_h + [tk_rstd, tk_gains], writes=[tk_sqf])
        for c in range(8):
            def tr(e, c=c):
                r = None
                for b in range(4):
                    r = e.transpose(pb[7][:, b * 128:(b + 1) * 128], sqf[:, c, b * 128:(b + 1) * 128], ident[:])
                return r
            sc.op("pe", tr, reads=[tk_sqf, tk_ident], writes=[tk_pb[7]])
            dst = tokbuf[:, :, c * 128:(c + 1) * 128]
            src = pb[7][:].rearrange("p (b q) -> p b q", b=4)
            if c % 2 == 0:
                sc.op("act", (lambda e, dst=dst, src=src: e.activation(dst, src, AF.Copy)), reads=[tk_pb[7]], writes=[tk_tok])
            else:
                sc.op("dve", (lambda e, dst=dst, src=src: e.tensor_copy(dst, src)), reads=[tk_pb[7]], writes=[tk_tok])
        sc.op("pool", lambda e: e.dma_start(out=yout[t0:t0 + 512, :].rearrange("(b p) d -> p b d", p=128), in_=tokbuf[:]),
              reads=[tk_tok], owner=tk_tok)


    def phase_a(l, t):
        t0 = t * 512
        norm_to_n(l * 4 + 0)
        ffn(l, "wg1", "wu1", "wd1")
        sc.op("pool", lambda e: e.dma_start(out=h1T[:, :, t0:t0 + 512].rearrange("c p t -> p c t"), in_=h[:]),
              reads=tk_h, writes=[tk_h1T], owner=tk_h[0])
        if not do_mix:
            return
        norm_to_n(l * 4 + 1)
        for j in range(NJP):
            wbuf, wtk = wload("win", l, j)
            po, tpo = next_bank()

            def mmp(e, wbuf=wbuf, po=po):
                r = None
                for c in range(8):
                    r = e.matmul(po[:], wbuf[:, c * 128:(c + 1) * 128], nb[:, c, :], start=(c == 0), stop=(c == 7))
                return r
            sc.op("pe", mmp, reads=[wtk, tk_n], writes=[tpo])
            k = j % 3
            scale = 0.125 if j < 4 else 1.0
            dstb = ev_b[k] if j < 12 else ev_f[k]
            dtk = tk_evb[k] if j < 12 else tk_evf[k]
            if j % 2 == 0:
                sc.op("act", (lambda e, dstb=dstb, po=po, scale=scale: e.activation(dstb[:], po[:], AF.Copy, scale=scale)),
                      reads=[tpo], writes=[dtk])
            else:
                sc.op("dve", (lambda e, dstb=dstb, po=po, scale=scale: e.tensor_scalar(dstb[:], po[:], scale, None, ALU.mult)),
                      reads=[tpo], writes=[dtk])
            if j < 12:
                sc.op("pool", (lambda e, dstb=dstb, j=j: e.dma_start(out=qkvT[j, :, t0:t0 + 512], in_=dstb[:])),
                      reads=[dtk], writes=[tk_qkvT], owner=dtk)
            else:
                sc.op("pool", (lambda e, dstb=dstb, j=j: e.dma_start(out=projT[j - 12, :, t0:t0 + 512], in_=dstb[:])),
                      reads=[dtk], writes=[tk_projT], owner=dtk)

    def phase_c(l, t):
        t0 = t * 512
        sc.op("pool", lambda e: e.dma_start(out=h[:], in_=h1T[:, :, t0:t0 + 512].rearrange("c p t -> p c t")),
              reads=[tk_h1T], writes=tk_h, owner=tk_h[0])
        if do_mix:
            sc.op("pool", lambda e: e.dma_start(out=ynat[:], in_=ynaT[:, :, t0:t0 + 512]),
                  reads=[tk_ynaT], writes=[tk_ynat], owner=tk_ynat)
            sc.op("pool", lambda e: e.dma_start(out=yrwt[:], in_=yrwT[:, :, t0:t0 + 512]),
                  reads=[tk_yrwT], writes=[tk_yrwt], owner=tk_yrwt)
            for j in range(8):
                w1, w1tk = wload("wona", l, j)
                w2, w2tk = wload("worw", l, j)
                po, tpo = next_bank()

                def mmo(e, w1=w1, w2=w2, po=po):
                    for hh in range(8):
                        e.matmul(po[:], w1[:64, hh * 128:(hh + 1) * 128], ynat[:, hh, :], start=(hh == 0), stop=False)
                    r = None
                    for c in range(4):
                        r = e.matmul(po[:], w2[:, c * 128:(c + 1) * 128], yrwt[:, c, :], start=False, stop=(c == 3))
                    return r
                sc.op("pe", mmo, reads=[w1tk, w2tk, tk_ynat, tk_yrwt], writes=[tpo])
                sc.op("dve", (lambda e, po=po, j=j: e.tensor_tensor(h[:, j, :], h[:, j, :], po[:], ALU.add)),
                      reads=[tpo, tk_h[j]], writes=[tk_h[j]])
        norm_to_n(l * 4 + 2)
        ffn(l, "wg2", "wu2", "wd2")
        sc.op("pool", lambda e: e.dma_start(out=pbuf[:], in_=pin[l, t0:t0 + 512, :].rearrange("(b p) d -> p b d", p=128)),
              writes=[tk_pbuf], owner=tk_pbuf)
        for c in range(2):
            def tr(e, c=c):
                r = None
                for b in range(4):
                    r = e.transpose(pb[7][:, b * 128:(b + 1) * 128], pbuf[:, b, c * 128:(c + 1) * 128], ident[:])
                return r
            sc.op("pe", tr, reads=[tk_pbuf, tk_ident], writes=[tk_pb[7]])
            sc.op("act", (lambda e, c=c: e.activation(pT[:, c, :], pb[7][:], AF.Copy)), reads=[tk_pb[7]], writes=[tk_pT])
        norm_to_n(l * 4 + 3)
        for j in range(8):
            wgt, wgtk = wload("pgate", l, j)
            wup, wuptk = wload("pup", l, j)
            pg, tpg = next_bank()
            pu, tpu = next_bank()

            def mmg(e, wgt=wgt, pg=pg):
                r = None
                for c in range(8):
                    r = e.matmul(pg[:], wgt[:, c * 128:(c + 1) * 128], nb[:, c, :], start=(c == 0), stop=(c == 7))
                return r

            def mmu(e, wup=wup, pu=pu):
                r = None
                for c in range(2):
                    r = e.matmul(pu[:], wup[:, c * 128:(c + 1) * 128], pT[:, c, :], start=(c == 0), stop=(c == 1))
                return r
            sc.op("pe", mmg, reads=[wgtk, tk_n], writes=[tpg])
            sc.op("pe", mmu, reads=[wuptk, tk_pT], writes=[tpu])
            si = j % 2
            sc.op("act", (lambda e, si=si, pg=pg: e.activation(sg[si][:], pg[:], AF.Sigmoid)), reads=[tpg], writes=[tk_sg[si]])
            sc.op("dve", (lambda e, si=si, pu=pu: e.tensor_tensor(sg[si][:], sg[si][:], pu[:], ALU.mult)),
                  reads=[tk_sg[si], tpu], writes=[tk_sg[si]])
            sc.op("dve", (lambda e, si=si, j=j: e.tensor_tensor(h[:, j, :], h[:, j, :], sg[si][:], ALU.add)),
                  reads=[tk_sg[si], tk_h[j]], writes=[tk_h[j]])


    R_ = T // 64
    RS = SEGLEN // 64
    tb_d = dram_in("tb", [L, 128, 8 * 14 * 64])
    tkN = carver()
    KW = 22 * 64
    kT = tkN([64, 8, KW], BF16); tk_kT = sc.tk("kT")
    vT = tkN([128, 4, KW], BF16); tk_vT = sc.tk("vT")
    qT = tkN([64, 8, 512], BF16); tk_qT = sc.tk("qT")
    Vb = tkN([128, 21, 512], BF16); tk_Vb = sc.tk("Vb")
    Tb = tkN([128, 8, 7, 2, 64][:4] if False else [128, 8 * 14 * 64]); tk_Tb = sc.tk("Tb")
    Tb5 = Tb.rearrange("p (h m2 par j) -> p h m2 par j", h=8, m2=7, par=2)
    Ssb = tkN([128, 8, 4, 64]); tk_Ssb = sc.tk("Ssb")
    Eb = tkN([128, 8, 4, 64], BF16); tk_E = sc.tk("E")
    rden = tkN([64, 512]); tk_rden = sc.tk("rden")
    tmpS = tkN([64, 512]); tk_tmpS = sc.tk("tmpS")
    tmpP = tkN([64, 512]); tk_tmpP = sc.tk("tmpP")
    ob = tkN([64, 8, 512], BF16); tk_ob = sc.tk("ob")
    zt = tkN([128, 4, 512], BF16); tk_zt = sc.tk("zt")
    Sps = pall[:, 512:2560].rearrange("p (h k q) -> p h k q", h=8, k=4)
    tk_S = tk_pb[1:5]
    pb7b = pb[7].bitcast(BF16)

    def na_phase(l):
        sc.op("pool", lambda e: e.dma_start(out=Tb[:], in_=tb_d[l]), writes=[tk_Tb], owner=tk_Tb)
        for blk in range(R_ // 8):
            r0 = blk * 8
            klo = max(0, r0 - 7)
            khi = min(R_, r0 + 15)
            nk = (khi - klo) * 64
            sc.op("pool", (lambda e, klo=klo, khi=khi, nk=nk: e.dma_start(
                out=kT[:, :, :nk], in_=qkvT[4:8, :, klo * 64:khi * 64].rearrange("c (two p) t -> p (c two) t", two=2))),
                reads=[tk_qkvT], writes=[tk_kT], owner=tk_kT)
            sc.op("pool", (lambda e, klo=klo, khi=khi, nk=nk: e.dma_start(
                out=vT[:, :, :nk], in_=qkvT[8:12, :, klo * 64:khi * 64].rearrange("c p t -> p c t"))),
                reads=[tk_qkvT], writes=[tk_vT], owner=tk_vT)
            sc.op("pool", (lambda e, r0=r0: e.dma_start(
                out=qT[:], in_=qkvT[0:4, :, r0 * 64:r0 * 64 + 512].rearrange("c (two p) t -> p (c two) t", two=2))),
                reads=[tk_qkvT], writes=[tk_qT], owner=tk_qT)
            for o in range(klo, khi - 1):
                oo = o - klo

                def trv(e, oo=oo):
                    r = None
                    for hp in range(4):
                        r = e.transpose(pb7b[:, hp * 128:(hp + 1) * 128], vT[:, hp, oo * 64:oo * 64 + 128], identb[:])
                    return r
                sc.op("pe", trv, reads=[tk_vT, tk_identb], writes=[tk_pb[7]])
                if oo % 2 == 0:
                    sc.op("act", (lambda e, oo=oo: e.activation(Vb[:, oo, :], pb7b[:, 0:512], AF.Copy)),
                          reads=[tk_pb[7]], writes=[tk_Vb])
                else:
                    sc.op("dve", (lambda e, oo=oo: e.tensor_copy(Vb[:, oo, :], pb7b[:, 0:512])),
                          reads=[tk_pb[7]], writes=[tk_Vb])
            rvs = []
            for i in range(r0, r0 + 8):
                seg = i // RS
                il = i % RS
                rsP = seg * RS + min(max(il - 4, 0), RS - 8)
                rsS = min(max(i - 4, 0), R_ - 8)
                if rsP == rsS:
                    rvs.append((i, rsP, 0))
                else:
                    rvs.append((i, rsS, 1))
                    rvs.append((i, rsP, 2))

            def emit_S(i, rs, r0=r0, klo=klo):
                qo = (i - r0) * 64

                def mms(e):
                    r = None
                    for hh in range(8):
                        for kc in range(4):
                            ks = (rs + 2 * kc - klo) * 64
                            r = e.matmul(Sps[:, hh, kc, :], kT[:, hh, ks:ks + 128], qT[:, hh, qo:qo + 64],
                                         start=True, stop=True)
                    return r
                sc.op("pe", mms, reads=[tk_kT, tk_qT], writes=tk_S)

            def emit_soft(i, rs):
                cl = i - rs
                s0 = 7 - cl
                tbv = Tb5[:, :, s0 // 2:s0 // 2 + 4, s0 % 2, :]
                sc.op("dve", lambda e: e.tensor_tensor(Ssb[:], Sps, tbv, ALU.add), reads=tk_S + [tk_Tb], writes=[tk_Ssb])
                sc.op("act", lambda e: e.activation(Eb[:], Ssb[:], AF.Exp), reads=[tk_Ssb], writes=[tk_E])

            def emit_pv(i, rs, kind, r0=r0, klo=klo):
                qo = (i - r0) * 64

                def mmpv(e):
                    r = None
                    for hh in range(8):
                        for kc in range(4):
                            r = e.matmul(pb[5][:64, hh * 64:(hh + 1) * 64], Vb[:, rs + 2 * kc - klo, hh * 64:(hh + 1) * 64],
                                         Eb[:, hh, kc, :], start=(kc == 0), stop=(kc == 3))
                    return r
                sc.op("pe", mmpv, reads=[tk_Vb, tk_E], writes=[tk_pb[5]])

                def mmden(e):
                    r = None
                    for kc in range(4):
                        r = e.matmul(pb[6][:64, :].rearrange("p (h q) -> p h q", h=8), onesb[:, :64], Eb[:, :, kc, :],
                                     start=(kc == 0), stop=(kc == 3))
                    return r
                sc.op("pe", mmden, reads=[tk_E, tk_ones], writes=[tk_pb[6]])
                sc.op("dve", lambda e: e.reciprocal(rden[:], pb[6][:64, :]), reads=[tk_pb[6]], writes=[tk_rden])
                o3 = pb[5][:64, :].rearrange("p (h q) -> p h q", h=8)
                r3 = rden[:].rearrange("p (h q) -> p h q", h=8)
                if kind == 0:
                    sc.op("dve", lambda e: e.tensor_tensor(ob[:, :, qo:qo + 64], o3, r3, ALU.mult),
                          reads=[tk_pb[5], tk_rden], writes=[tk_ob])
                elif kind == 1:
                    sc.op("dve", lambda e: e.scalar_tensor_tensor(out=tmpS[:], in0=pb[5][:64, :], scalar=flags[:64, 0:1], in1=rden[:],
                                                                  op0=ALU.mult, op1=ALU.mult),
                          reads=[tk_pb[5], tk_rden, tk_flags], writes=[tk_tmpS])
                else:
                    sc.op("dve", lambda e: e.scalar_tensor_tensor(out=tmpP[:], in0=pb[5][:64, :], scalar=flags[:64, 1:2], in1=rden[:],
                                                                  op0=ALU.mult, op1=ALU.mult),
                          reads=[tk_pb[5], tk_rden, tk_flags], writes=[tk_tmpP])
                    sc.op("dve", lambda e: e.tensor_tensor(ob[:, :, qo:qo + 64], tmpS[:].rearrange("p (h q) -> p h q", h=8),
                                                           tmpP[:].rearrange("p (h q) -> p h q", h=8), ALU.add),
                          reads=[tk_tmpS, tk_tmpP], writes=[tk_ob])

            import os
            NAS = int(os.environ.get("NAS", "3"))
            if NAS >= 1:
                emit_S(rvs[0][0], rvs[0][1])
            for n, (i, rs, kind) in enumerate(rvs):
                if NAS >= 2:
                    emit_soft(i, rs)
                if n + 1 < len(rvs) and NAS >= 1:
                    emit_S(rvs[n + 1][0], rvs[n + 1][1])
                if NAS >= 3:
                    emit_pv(i, rs, kind)
            sc.op("pool", (lambda e, r0=r0: e.dma_start(out=ynaT[:, :, r0 * 64:r0 * 64 + 512], in_=ob[:])),
                  reads=[tk_ob], writes=[tk_ynaT], owner=tk_ob)

    def zero_fill(dst, tk_dst, npart):
        sc.op("pool", lambda e: e.memset(zt[:], 0.0), writes=[tk_zt])
        for t in range(NT):
            if npart == 128:
                sc.op("pool", (lambda e, t=t: e.dma_start(out=dst[:, :, t * 512:(t + 1) * 512], in_=zt[:])),
                      reads=[tk_zt], writes=[tk_dst], owner=tk_zt)
            else:
                for hh2 in range(2):
                    sc.op("pool", (lambda e, t=t, hh2=hh2: e.dma_start(out=dst[:, hh2 * 4:hh2 * 4 + 4, t * 512:(t + 1) * 512], in_=zt[:64])),
                          reads=[tk_zt], writes=[tk_dst], owner=tk_zt)

    TT_ = 256
    NTR = T // TT_
    SC = 0.606531
    rwp_d = {nm: dram_in(nm, shp) for nm, shp in [
        ("rw_mu", [L, 1920]), ("rw_w0", [L, 2, 512]), ("rw_w_up", [L, 2, 64, 512]), ("rw_a0", [L, 2, 512]),
        ("rw_a_up", [L, 2, 64, 512]), ("rw_g_up", [L, 128, 512]), ("rw_k_k", [L, 512]), ("rw_k_a", [L, 512]),
        ("rw_r_k", [L, 512]), ("rw_lnx_w", [L, 512]), ("rw_lnx_b", [L, 512])]}
    rwmask_d = dram_in("rwmask", [128, 4, 128])
    yf_d = dram_tmp("yf_d", [128, T // 64, 4, 64], F32); tk_yfd = sc.tk("yfd")
    tkR = carver()
    uA0 = tkR([128, 4, TT_ + 2]); uA = [uA0, uA0]; tk_uA0 = sc.tk("uA0"); tk_uA = [tk_uA0, tk_uA0]
    shb = tkR([128, 4, TT_]); tk_sh = sc.tk("sh"); o_sh = tkR.last
    xr = tkR([128, 4, TT_]); xk = tkR([128, 4, TT_]); xv = tkR([128, 4, TT_])
    tk_x = [sc.tk("xr"), sc.tk("xk"), sc.tk("xv")]
    xs3 = [xr, xk, xv]
    uB = tkR([64, 4, TT_ + 2]); tk_uB = sc.tk("uB")
    xB = tkR([64, 4, TT_]); tk_xB = sc.tk("xB")
    uC = tkR([128, 1, TT_ + 2]); tk_uC = sc.tk("uC")
    xCf = tkR([128, 1, TT_]); tk_xCf = sc.tk("xCf")
    xCb = tkR([128, TT_], BF16); tk_xCb = sc.tk("xCb")
    twb = tkR([64, TT_], BF16); tk_twb = sc.tk("twb")
    alb = tkR([64, 2, TT_], BF16); tk_alb = sc.tk("alb")
    sqkb = tkR([128, 4, TT_], BF16); tk_sqkb = sc.tk("sqkb")
    kkn = tkR([128, 4, TT_]); tk_kkn = sc.tk("kkn"); o_kkn = tkR.last
    sig = tkR([128, 4, TT_]); tk_sig = sc.tk("sig"); o_sig = tkR.last
    cs = tkR([128, 4, TT_]); tk_cs = sc.tk("cs"); o_cs = tkR.last
    t1 = tkR([128, 4, TT_]); tk_t1 = sc.tk("t1"); o_t1 = tkR.last
    t2 = tkR([128, 4, TT_]); tk_t2 = sc.tk("t2"); o_t2 = tkR.last
    e0 = tkR([128, 4, TT_]); tk_e0 = sc.tk("e0"); o_e0 = tkR.last
    e1 = tkR([128, 4, TT_]); tk_e1 = sc.tk("e1")
    asg = tkR([128, 4, TT_]); tk_asg = sc.tk("asg"); o_asg = tkR.last
    asf = tkR.at(o_sh, [128, 4, TT_]); tk_asf = tk_sh
    bb = tkR([128, 4, TT_]); tk_bb = sc.tk("bb"); o_bb = tkR.last
    kd = tkR([128, 4, TT_]); tk_kd = sc.tk("kd"); o_kd = tkR.last
    bonus = tkR.at(o_asg, [128, 4, TT_]); tk_bonus = tk_asg
    gbuf = tkR.at(o_kkn, [128, 4, TT_]); tk_g = tk_kkn
    NCH = TT_ // 64
    ARbd = tkR([128, 4, NCH, 256], BF16); tk_AR = sc.tk("AR")
    Btbd = tkR([128, 4, NCH, 128], BF16); tk_Bt = sc.tk("Bt")
    Ktbd = tkR([128, 4, NCH, 128], BF16); tk_Kt = sc.tk("Kt")
    Bhbd = tkR([128, 4, NCH, 128], BF16); tk_Bh = sc.tk("Bh")
    Khbd = tkR([128, 4, NCH, 128], BF16); tk_Kh = sc.tk("Kh")
    Vbd = tkR([128, 4, NCH, 128], BF16); tk_Vbd = sc.tk("Vbd")
    Sb = [tkR([128, 4, 128], BF16) for _ in range(4)]; tk_Sb = [sc.tk("Sb%d" % i) for i in range(4)]
    STb = [tkR([128, 4, 128], BF16) for _ in range(4)]; tk_STb = [sc.tk("STb%d" % i) for i in range(4)]
    TTb = [tkR([128, 4, 128], BF16) for _ in range(4)]; tk_TTb = [sc.tk("TTb%d" % i) for i in range(4)]
    M1s = [tkR([128, 4, 256], BF16) for _ in range(4)]; tk_M1s = [sc.tk("M1_%d" % i) for i in range(4)]
    M2s = [tkR([128, 4, 256], BF16) for _ in range(4)]; tk_M2s = [sc.tk("M2_%d" % i) for i in range(4)]
    Vtms = [tkR([128, 4, 128], BF16) for _ in range(2)]; tk_Vtms = [sc.tk("Vtm%d" % i) for i in range(2)]
    Bhtms = [tkR([128, 4, 128], BF16) for _ in range(2)]; tk_Bhtms = [sc.tk("Bhtm%d" % i) for i in range(2)]
    Khtms = [tkR([128, 4, 128], BF16) for _ in range(2)]; tk_Khtms = [sc.tk("Khtm%d" % i) for i in range(2)]
    Wb = tkR([128, 4, 128], BF16); tk_Wb = sc.tk("Wb")
    Ub = tkR([128, 4, 128], BF16); tk_Ub = sc.tk("Ub")
    Hs = tkR([128, 4, 128]); tk_H = sc.tk("H")
    Hb = tkR([128, 4, 128], BF16); tk_Hb = sc.tk("Hb")
    Yt = tkR.at(o_sig, [128, NCH, 4, 64]); tk_Yt = tk_sig
    Yf = tkR.at(o_e0, [128, NCH, 4, 64]); tk_Yf = tk_e0
    cen = tkR.at(o_t1, [128, NCH, 4, 64]); tk_cen = tk_t1
    ynbd = tkR([128, NCH, 4, 128], BF16); tk_ynbd = sc.tk("ynbd")
    ynT = tkR.at(o_t2, [128, 4, TT_]); tk_ynT = tk_t2
    orw = tkR([128, 4, TT_], BF16); tk_orw = sc.tk("orw")
    st1 = tkR([128, 16]); st2 = tkR([128, 16]); pcb = tkR([128, 16]); tk_st = sc.tk("st")
    tk_pc = sc.tk("pc")
    mask01 = tkR([128, 4 * TT_]); tk_m01 = sc.tk("m01")
    rwm = tkR([128, 4, 128]); tk_rwm = sc.tk("rwm")
    mAf = tkR([128, 256]); mAb = tkR([128, 256]); tk_mA = sc.tk("mA")
    bones = tkR([128, 128], BF16); tk_bones = sc.tk("bones")
    wupf = tkR.at(o_cs, [64, 2, 512]); aupf = tkR.at(o_kd, [64, 2, 512]); gupf = tkR.at(o_bb, [128, 512])
    wupb = tkR([64, 2, 512], BF16); aupb = tkR([64, 2, 512], BF16); gupb = tkR([128, 512], BF16)
    tk_wts = sc.tk("rwwts")
    tk_stage = [tk_cs, tk_kd, tk_bb]
    muA = tkR([128, 12]); muB = tkR([64, 4]); muC = tkR([128, 1])
    w0s = tkR([128, 8]); a0s = tkR([128, 8]); kks = tkR([128, 4]); kas = tkR([128, 4]); rks = tkR([128, 4])
    lws = tkR([128, 4]); lbs = tkR([128, 4])
    tk_par = sc.tk("rwpar")

    def bank2(i):
        return pall[:, i * 512:(i + 2) * 512]

    def seq(eng, fns, reads, writes):
        for fn in fns:
            sc.op(eng, fn, reads=list(reads) + list(writes), writes=writes)

    def bcast(ap2, shape):
        return ap2.unsqueeze(2).to_broadcast(shape)

    def rw_setup(l):
        def ldp(e):
            r = []
            nsc = dict(allow_slow_non_contiguous=True)
            r.append(e.dma_start(out=muA[:], in_=rwp_d["rw_mu"][l, 0:1536].rearrange("(c p) -> p c", p=128), **nsc))
            r.append(e.dma_start(out=muB[:], in_=rwp_d["rw_mu"][l, 1536:1792].rearrange("(c p) -> p c", p=64), **nsc))
            r.append(e.dma_start(out=muC[:], in_=rwp_d["rw_mu"][l, 1792:1920].rearrange("(c p) -> p c", p=128), **nsc))
            r.append(e.dma_start(out=w0s[:], in_=rwp_d["rw_w0"][l].rearrange("d (c p) -> p (d c)", p=128), **nsc))
            r.append(e.dma_start(out=a0s[:], in_=rwp_d["rw_a0"][l].rearrange("d (c p) -> p (d c)", p=128), **nsc))
            for dst, nm in ((kks, "rw_k_k"), (kas, "rw_k_a"), (rks, "rw_r_k"), (lws, "rw_lnx_w"), (lbs, "rw_lnx_b")):
                r.append(e.dma_start(out=dst[:], in_=rwp_d[nm][l].rearrange("(c p) -> p c", p=128), **nsc))
            return r
        sc.op("pool", ldp, writes=[tk_par], owner=tk_par, ndma=10)

        def ldw(e):
            r = [e.dma_start(out=wupf[:], in_=rwp_d["rw_w_up"][l].rearrange("d k n -> k d n")),
                 e.dma_start(out=aupf[:], in_=rwp_d["rw_a_up"][l].rearrange("d k n -> k d n")),
                 e.dma_start(out=gupf[:], in_=rwp_d["rw_g_up"][l]),
                 e.dma_start(out=rwm[:], in_=rwmask_d)]
            return r
        sc.op("pool", ldw, writes=[tk_wts, tk_rwm] + tk_stage, owner=tk_wts, ndma=4)

        def cvt(e):
            e.tensor_copy(wupb[:], wupf[:])
            e.tensor_copy(aupb[:], aupf[:])
            return e.tensor_copy(gupb[:], gupf[:])
        sc.op("dve", cvt, reads=[tk_wts] + tk_stage, writes=[tk_wts])

        def mkmasks(e):
            e.tensor_copy(mAf[:, 0:128], rwm[:, 1, :])
            e.tensor_copy(mAf[:, 128:256], rwm[:, 3, :])
            e.tensor_copy(mAb[:, 0:128], rwm[:, 0, :])
            return e.tensor_copy(mAb[:, 128:256], rwm[:, 2, :])
        sc.op("dve", mkmasks, reads=[tk_rwm], writes=[tk_mA])

        seq("pool", [lambda e: e.memset(mask01[:], 1.0),
                     lambda e: e.memset(mask01[:].rearrange("p (a b) -> p a b", b=64)[:, :, 0:1], 0.0)], [], [tk_m01])
        seq("pool", [lambda e: e.memset(bones[:], 0.0),
                     lambda e: e.memset(bones[0:64, 0:64], 1.0),
                     lambda e: e.memset(bones[64:128, 64:128], 1.0)], [], [tk_bones])
        for bdt, tkb in ((ARbd, tk_AR), (Btbd, tk_Bt), (Ktbd, tk_Kt), (Bhbd, tk_Bh), (Khbd, tk_Kh), (Vbd, tk_Vbd), (ynbd, tk_ynbd)):
            sc.op("pool", (lambda e, bdt=bdt: e.memset(bdt[:], 0.0)), writes=[tkb])
        sc.op("pool", lambda e: e.memset(Hs[:], 0.0), writes=[tk_H])

    def load_shift(src3, P, ncol, ub, tk_ub, mu, out_ap, tk_out, t0, eng="pool"):
        tlo = max(t0 - 1, 0)
        thi = min(t0 + TT_ + 1, T)
        off = tlo - (t0 - 1)
        n = thi - tlo

        def ld(e):
            return e.dma_start(out=ub[:, :, off:off + n], in_=src3[:, :, tlo:thi])
        sc.op("pool", ld, reads=[tk_projT], writes=[tk_ub], owner=tk_ub)
        if t0 == 0:
            sc.op(eng, lambda e: e.memset(ub[:, :, 0:1], 0.0), writes=[tk_ub])
        elif t0 % SEGLEN == 0:
            sc.op(eng, lambda e: e.tensor_scalar(ub[:, :, 0:1], ub[:, :, 0:1], flags[:P, 0:1], None, ALU.mult),
                  reads=[tk_flags], writes=[tk_ub])
        if t0 + TT_ == T:
            sc.op(eng, lambda e: e.memset(ub[:, :, TT_ + 1:TT_ + 2], 0.0), writes=[tk_ub])
        elif (t0 + TT_) % SEGLEN == 0:
            sc.op(eng, lambda e: e.tensor_scalar(ub[:, :, TT_ + 1:TT_ + 2], ub[:, :, TT_ + 1:TT_ + 2], flags[:P, 0:1], None, ALU.mult),
                  reads=[tk_flags], writes=[tk_ub])
        sh = shb[:P, :ncol, :]
        u1 = ub[:, :, 1:TT_ + 1]

        seq(eng, [lambda e: e.tensor_tensor(sh, ub[:, :, 0:TT_], ub[:, :, 2:TT_ + 2], ALU.add),
                  lambda e: e.tensor_scalar(sh, sh, 0.5, None, ALU.mult),
                  lambda e: e.tensor_tensor(sh, sh, u1, ALU.subtract),
                  lambda e: e.tensor_tensor(sh, sh, bcast(mu, [P, ncol, TT_]), ALU.mult),
                  lambda e: e.tensor_tensor(out_ap, sh, u1, ALU.add)], [tk_ub, tk_par], [tk_sh, tk_out])

    def bd_write(eng, dst4, col0, fn, reads, tk_dst):
        for hh in range(2):
            ov = dst4[hh * 64:(hh + 1) * 64, :, :, col0 + hh * 64:col0 + (hh + 1) * 64]
            sc.op(eng, (lambda e, ov=ov, hh=hh: fn(e, ov, hh)), reads=reads, writes=[tk_dst])

    def half4(ap3, hh):
        return ap3[hh * 64:(hh + 1) * 64].rearrange("p a (c q) -> p a c q", q=64)

    import os as _os
    RWS = float(_os.environ.get("RWS", "9"))

    def rw_shift(ti):
        t0 = ti * TT_
        for grp in range(3):
            load_shift(projT[4 * grp:4 * grp + 4].rearrange("c p t -> p c t"), 128, 4, uA[grp % 2], tk_uA[grp % 2],
                       muA[:, 4 * grp:4 * grp + 4], xs3[grp][:], tk_x[grp], t0)
        load_shift(projT[12:14].rearrange("c (two p) t -> p (c two) t", two=2), 64, 4, uB, tk_uB, muB[:], xB[:], tk_xB, t0)

    def rw_tile(l, d, ti, nxt):
        t0 = ti * TT_
        last = (d == 1)
        if RWS < 2:
            return
        sc.op("dve", lambda e: e.tensor_tensor(kkn[:], xk[:], bcast(kks[:], [128, 4, TT_]), ALU.mult),
              reads=[tk_x[1], tk_par], writes=[tk_kkn])
        sc.op("dve", lambda e: e.tensor_tensor(sqkb[:], kkn[:], kkn[:], ALU.mult), reads=[tk_kkn], writes=[tk_sqkb])

        def mmss(e):
            r = None
            for hp in range(4):
                r = e.matmul(bank2(0)[:, hp * TT_:(hp + 1) * TT_], bones[:], sqkb[:, hp, :], start=True, stop=True)
            return r
        sc.op("pe", mmss, reads=[tk_sqkb, tk_bones], writes=[tk_pb[0], tk_pb[1]])
        t1f = t1[:].rearrange("p a b -> p (a b)")
        sc.op("act", lambda e: e.activation(t1f, bank2(0), AF.Sqrt), reads=[tk_pb[0], tk_pb[1]], writes=[tk_t1])

        seq("dve", [lambda e: e.tensor_scalar(t1f, t1f, 1e-12, None, ALU.max),
                    lambda e: e.reciprocal(t1f, t1f),
                    lambda e: e.tensor_tensor(kkn[:], kkn[:], t1[:], ALU.mult)], [], [tk_t1, tk_kkn])
        if RWS < 3:
            return
        sc.op("act", lambda e: e.activation(twb[:], xB[:, d, :], AF.Tanh), reads=[tk_xB], writes=[tk_twb])
        sc.op("act", lambda e: e.activation(alb[:], xB[:, 2:4, :], AF.Copy), reads=[tk_xB], writes=[tk_alb])

        def mmz(e):
            r = None
            for hp in range(4):
                r = e.matmul(bank2(2)[:, hp * TT_:(hp + 1) * TT_], wupb[:, d, hp * 128:(hp + 1) * 128], twb[:], start=True, stop=True)
            return r
        sc.op("pe", mmz, reads=[tk_twb, tk_wts], writes=[tk_pb[2], tk_pb[3]])

        def mma(dd, bk):
            def f(e):
                r = None
                for hp in range(4):
                    r = e.matmul(bank2(bk)[:, hp * TT_:(hp + 1) * TT_], aupb[:, dd, hp * 128:(hp + 1) * 128], alb[:, dd, :], start=True, stop=True)
                return r
            return f
        sc.op("pe", mma(d, 4), reads=[tk_alb, tk_wts], writes=[tk_pb[4], tk_pb[5]])

        def sigz(e):
            r = None
            for hp in range(4):
                r = e.activation(sig[:, hp, :], bank2(2)[:, hp * TT_:(hp + 1) * TT_], AF.Sigmoid, bias=w0s[:, d * 4 + hp:d * 4 + hp + 1])
            return r
        sc.op("act", sigz, reads=[tk_pb[2], tk_pb[3], tk_par], writes=[tk_sig])

        def siga(dst, dd, bk):
            def f(e):
                r = None
                for hp in range(4):
                    r = e.activation(dst[:, hp, :], bank2(bk)[:, hp * TT_:(hp + 1) * TT_], AF.Sigmoid, bias=a0s[:, dd * 4 + hp:dd * 4 + hp + 1])
                return r
            return f
        sc.op("act", siga(asg, d, 4), reads=[tk_pb[4], tk_pb[5], tk_par], writes=[tk_asg])
        sc.op("dve", lambda e: e.tensor_tensor(bb[:], kkn[:], asg[:], ALU.mult), reads=[tk_kkn, tk_asg], writes=[tk_bb])

        seq("dve", [lambda e: e.scalar_tensor_tensor(out=kd[:], in0=asg[:], scalar=-1.0, in1=bcast(kas[:], [128, 4, TT_]), op0=ALU.add, op1=ALU.mult),
                    lambda e: e.scalar_tensor_tensor(out=kd[:], in0=kd[:], scalar=1.0, in1=xk[:], op0=ALU.add, op1=ALU.mult)],
            [tk_asg, tk_par, tk_x[1]], [tk_kd])
        if RWS < 4:
            return
        csf = cs[:].rearrange("p a b -> p (a b)")
        sc.op("dve", lambda e: e.tensor_tensor_scan(csf, mask01[:], sig[:].rearrange("p a b -> p (a b)"), 0.0, ALU.mult, ALU.add),
              reads=[tk_sig, tk_m01], writes=[tk_cs])
        cs16 = cs[:].rearrange("p a (c q) -> p (a c) q", q=64)
        totb = cs16[:, :, 63:64].to_broadcast([128, 16, 64])
        as16 = lambda ap: ap[:].rearrange("p a (c q) -> p (a c) q", q=64)
        sc.op("act", lambda e: e.activation(pcb[:], cs16[:, :, 63], AF.Exp, scale=-SC), reads=[tk_cs], writes=[tk_pc])
        if d == 0:
            inc, tk_inc = cs, tk_cs
            sc.op("dve", lambda e: e.tensor_tensor(t1[:], cs[:], sig[:], ALU.subtract), reads=[tk_cs, tk_sig], writes=[tk_t1])
            exc, tk_exc = t1, tk_t1
            sc.op("dve", lambda e: e.tensor_tensor(as16(t2), totb, cs16, ALU.subtract), reads=[tk_cs], writes=[tk_t2])
            rem, tk_rem = t2, tk_t2
        else:
            sc.op("dve", lambda e: e.tensor_tensor(as16(t2), totb, cs16, ALU.subtract), reads=[tk_cs], writes=[tk_t2])
            exc, tk_exc = t2, tk_t2
            sc.op("dve", lambda e: e.tensor_tensor(t1[:], t2[:], sig[:], ALU.add), reads=[tk_t2, tk_sig], writes=[tk_t1])
            inc, tk_inc = t1, tk_t1
            sc.op("dve", lambda e: e.tensor_tensor(sig[:], cs[:], sig[:], ALU.subtract), reads=[tk_cs, tk_sig], writes=[tk_sig])
            rem, tk_rem = sig, tk_sig
        if RWS < 4.2:
            return
        sc.op("act", lambda e: e.activation(e0[:], inc[:], AF.Exp, scale=-SC), reads=[tk_inc], writes=[tk_e0])
        bd_write("dve", ARbd, 128, lambda e, ov, hh: e.tensor_tensor(ov, half4(xr, hh), half4(e0, hh), ALU.mult),
                 [tk_x[0], tk_e0], tk_AR)
        if RWS < 4.4:
            return
        sc.op("act", lambda e: e.activation(e1[:], inc[:], AF.Exp, scale=SC), reads=[tk_inc], writes=[tk_e1])
        bd_write("dve", Btbd, 0, lambda e, ov, hh: e.tensor_tensor(ov, half4(bb, hh), half4(e1, hh), ALU.mult),
                 [tk_bb, tk_e1], tk_Bt)
        bd_write("dve", Ktbd, 0, lambda e, ov, hh: e.tensor_tensor(ov, half4(kd, hh), half4(e1, hh), ALU.mult),
                 [tk_kd, tk_e1], tk_Kt)
        if RWS < 4.6:
            return
        sc.op("act", lambda e: e.activation(e0[:], exc[:], AF.Exp, scale=-SC), reads=[tk_exc], writes=[tk_e0])
        bd_write("dve", ARbd, 0, lambda e, ov, hh: e.scalar_tensor_tensor(out=ov, in0=half4(kkn, hh), scalar=-1.0, in1=half4(e0, hh),
                                                                          op0=ALU.mult, op1=ALU.mult),
                 [tk_kkn, tk_e0], tk_AR)
        sc.op("act", lambda e: e.activation(e1[:], rem[:], AF.Exp, scale=-SC), reads=[tk_rem], writes=[tk_e1])
        bd_write("dve", Bhbd, 0, lambda e, ov, hh: e.tensor_tensor(ov, half4(bb, hh), half4(e1, hh), ALU.mult),
                 [tk_bb, tk_e1], tk_Bh)
        bd_write("dve", Khbd, 0, lambda e, ov, hh: e.tensor_tensor(ov, half4(kd, hh), half4(e1, hh), ALU.mult),
                 [tk_kd, tk_e1], tk_Kh)
        bd_write("act", Vbd, 0, lambda e, ov, hh: e.activation(ov, half4(xv, hh), AF.Copy), [tk_x[2]], tk_Vbd)
        if last:
            sc.op("pe", mma(0, 6), reads=[tk_alb, tk_wts], writes=[tk_pb[6], tk_pb[7]])
            sc.op("act", siga(asf, 0, 6), reads=[tk_pb[6], tk_pb[7], tk_par], writes=[tk_asf])

            seq("dve", [lambda e: e.tensor_tensor(asf[:], asf[:], asg[:], ALU.add),
                        lambda e: e.scalar_tensor_tensor(out=asf[:], in0=asf[:], scalar=-2.0, in1=bcast(kas[:], [128, 4, TT_]), op0=ALU.add, op1=ALU.mult),
                        lambda e: e.scalar_tensor_tensor(out=asf[:], in0=asf[:], scalar=2.0, in1=xk[:], op0=ALU.add, op1=ALU.mult),
                        lambda e: e.tensor_tensor(asf[:], asf[:], xr[:], ALU.mult),
                        lambda e: e.tensor_tensor(sqkb[:], asf[:], bcast(rks[:], [128, 4, TT_]), ALU.mult)],
                [tk_asg, tk_par, tk_x[0], tk_x[1]], [tk_asf, tk_sqkb])

            def mmbd(e):
                r = None
                for hp in range(4):
                    r = e.matmul(bank2(6)[:, hp * TT_:(hp + 1) * TT_], bones[:], sqkb[:, hp, :], start=True, stop=True)
                return r
            sc.op("pe", mmbd, reads=[tk_sqkb, tk_bones], writes=[tk_pb[6], tk_pb[7]])
            sc.op("dve", lambda e: e.tensor_tensor(bonus[:].rearrange("p a b -> p (a b)"), bank2(6), xv[:].rearrange("p a b -> p (a b)"), ALU.mult),
                  reads=[tk_pb[6], tk_pb[7], tk_x[2]], writes=[tk_bonus])
            load_shift(projT[14:15].rearrange("c p t -> p c t"), 128, 1, uC, tk_uC, muC[:], xCf[:], tk_xCf, t0)
            sc.op("act", lambda e: e.activation(xCb[:], xCf[:, 0, :], AF.Sigmoid), reads=[tk_xCf], writes=[tk_xCb])

            def mmg(e):
                r = None
                for hp in range(4):
                    r = e.matmul(bank2(6)[:, hp * TT_:(hp + 1) * TT_], gupb[:, hp * 128:(hp + 1) * 128], xCb[:], start=True, stop=True)
                return r
            sc.op("pe", mmg, reads=[tk_xCb, tk_wts], writes=[tk_pb[6], tk_pb[7]])
            sc.op("act", lambda e: e.activation(gbuf[:].rearrange("p a b -> p (a b)"), bank2(6), AF.Copy),
                  reads=[tk_pb[6], tk_pb[7]], writes=[tk_g])
            sc.op("pool", (lambda e: e.dma_start(out=Yf[:], in_=yf_d[:, ti * NCH:(ti + 1) * NCH, :, :])),
                  reads=[tk_yfd], writes=[tk_Yf], owner=tk_Yf)
        if RWS < 5:
            return
        bnd = (t0 > 0 and t0 % SEGLEN == 0) if d == 0 else (t0 + TT_ < T and (t0 + TT_) % SEGLEN == 0)
        if bnd:
            sc.op("dve", lambda e: e.tensor_scalar(Hs[:], Hs[:], flags[:, 0:1], None, ALU.mult), reads=[tk_flags], writes=[tk_H])
        first_tile = (ti == 0) if d == 0 else (ti == NTR - 1)
        if bnd or first_tile:
            sc.op("act", lambda e: e.activation(Hb[:], Hs[:], AF.Copy), reads=[tk_H], writes=[tk_Hb])
        if nxt is not None:
            rw_shift(nxt)
        mA = mAf if d == 0 else mAb
        mB = rwm[:, 0, :] if d == 0 else rwm[:, 1, :]
        mA4 = mA[:].unsqueeze(1).to_broadcast([128, 4, 256])
        mB4 = mB.unsqueeze(1).to_broadcast([128, 4, 128])
        idb4 = identb[:].unsqueeze(1).to_broadcast([128, 4, 128])
        chs = list(range(NCH)) if d == 0 else list(range(NCH - 1, -1, -1))
        fl = lambda ap3: ap3[:].rearrange("p a b -> p (a b)")
        v4 = lambda ap2: ap2.rearrange("p (a b) -> p a b", a=4)
        for ch in chs:
            def prods(e, ch=ch):
                r = None
                for hp in range(4):
                    e.matmul(bank2(0)[:, hp * 256:(hp + 1) * 256], Btbd[:, hp, ch, :], ARbd[:, hp, ch, :], start=True, stop=True)
                    e.matmul(bank2(2)[:, hp * 256:(hp + 1) * 256], Ktbd[:, hp, ch, :], ARbd[:, hp, ch, :], start=True, stop=True)
                    r = e.matmul(pb[4][:, hp * 128:(hp + 1) * 128], ARbd[:, hp, ch, 0:128], Btbd[:, hp, ch, :], start=True, stop=True)
                return r
            sc.op("pe", prods, reads=[tk_AR, tk_Bt, tk_Kt], writes=[tk_pb[0], tk_pb[1], tk_pb[2], tk_pb[3], tk_pb[4]])
            sc.op("dve", (lambda e, ch=ch: e.tensor_tensor(M1s[ch][:], v4(bank2(0)), mA4, ALU.mult)),
                  reads=[tk_pb[0], tk_pb[1], tk_mA], writes=[tk_M1s[ch]])
            sc.op("dve", (lambda e, ch=ch: e.tensor_tensor(M2s[ch][:], v4(bank2(2)), mA4, ALU.mult)),
                  reads=[tk_pb[2], tk_pb[3], tk_mA], writes=[tk_M2s[ch]])
            sc.op("dve", (lambda e, ch=ch: e.tensor_tensor(Sb[ch][:], v4(pb[4]), mB4, ALU.mult)),
                  reads=[tk_pb[4], tk_rwm], writes=[tk_Sb[ch]])
        for lev in range(1, 6):
            for i, ch in enumerate(chs):
                STc = M1s[ch][:, :, 0:128] if lev == 1 else STb[ch]
                tkST = tk_M1s[ch] if lev == 1 else tk_STb[ch]

                def sq1(e, ch=ch, i=i, STc=STc):
                    r = None
                    for hp in range(4):
                        r = e.matmul(pb[2 * i][:, hp * 128:(hp + 1) * 128], STc[:, hp, :], Sb[ch][:, hp, :], start=True, stop=True)
                    return r
                sc.op("pe", sq1, reads=[tkST, tk_Sb[ch]], writes=[tk_pb[2 * i]])
                if lev < 5:
                    def sq2(e, ch=ch, i=i, STc=STc):
                        r = None
                        for hp in range(4):
                            r = e.matmul(pb[2 * i + 1][:, hp * 128:(hp + 1) * 128], Sb[ch][:, hp, :], STc[:, hp, :], start=True, stop=True)
                        return r
                    sc.op("pe", sq2, reads=[tkST, tk_Sb[ch]], writes=[tk_pb[2 * i + 1]])
            for i, ch in enumerate(chs):
                sc.op("act", (lambda e, ch=ch, i=i: e.activation(fl(Sb[ch]), pb[2 * i], AF.Copy)), reads=[tk_pb[2 * i]], writes=[tk_Sb[ch]])
                if lev < 5:
                    sc.op("dve", (lambda e, ch=ch, i=i: e.tensor_copy(fl(STb[ch]), pb[2 * i + 1])), reads=[tk_pb[2 * i + 1]], writes=[tk_STb[ch]])
            for i, ch in enumerate(chs):
                def ttu(e, ch=ch, i=i):
                    r = None
                    for hp in range(4):
                        o = pb[i][:, hp * 128:(hp + 1) * 128]
                        e.matmul(o, Sb[ch][:, hp, :], identb[:], start=True, stop=False)
                        r = e.matmul(o, Sb[ch][:, hp, :], M1s[ch][:, hp, 0:128], start=False, stop=True)
                    return r
                sc.op("pe", ttu, reads=[tk_Sb[ch], tk_M1s[ch], tk_identb], writes=[tk_pb[i]])
            for i, ch in enumerate(chs):
                sc.op("dve", (lambda e, ch=ch, i=i: e.tensor_tensor(M1s[ch][:, :, 0:128], M1s[ch][:, :, 0:128], v4(pb[i]), ALU.add)),
                      reads=[tk_pb[i]], writes=[tk_M1s[ch]])

        def emit_tm(ch, k):
            def trs(e):
                r = None
                for hp in range(4):
                    e.matmul(pb[5][:, hp * 128:(hp + 1) * 128], Vbd[:, hp, ch, :], identb[:], start=True, stop=True)
                    e.matmul(pb[6][:, hp * 128:(hp + 1) * 128], Bhbd[:, hp, ch, :], identb[:], start=True, stop=True)
                    r = e.matmul(pb[7][:, hp * 128:(hp + 1) * 128], Khbd[:, hp, ch, :], identb[:], start=True, stop=True)
                return r
            sc.op("pe", trs, reads=[tk_Vbd, tk_Bh, tk_Kh, tk_identb], writes=[tk_pb[5], tk_pb[6], tk_pb[7]])
            sc.op("act", lambda e: e.activation(fl(Vtms[k]), pb[5], AF.Copy), reads=[tk_pb[5]], writes=[tk_Vtms[k]])
            sc.op("dve", lambda e: e.tensor_copy(fl(Bhtms[k]), pb[6]), reads=[tk_pb[6]], writes=[tk_Bhtms[k]])
            sc.op("act", lambda e: e.activation(fl(Khtms[k]), pb[7], AF.Copy), reads=[tk_pb[7]], writes=[tk_Khtms[k]])

        emit_tm(chs[0], 0)
        for idx, ch in enumerate(chs):
            k = idx % 2
            if idx + 1 < NCH:
                emit_tm(chs[idx + 1], (idx + 1) % 2)
            M1, tk_M1, M2, tk_M2 = M1s[ch], tk_M1s[ch], M2s[ch], tk_M2s[ch]
            Vtm, tk_Vtm, Bhtm, tk_Bhtm, Khtm, tk_Khtm = Vtms[k], tk_Vtms[k], Bhtms[k], tk_Bhtms[k], Khtms[k], tk_Khtms[k]
            TTf, tk_TTf = TTb[ch], tk_TTb[ch]

            def mmw(e, ch=ch, M2=M2, Vtm=Vtm):
                r = None
                for hp in range(4):
                    e.matmul(pb[0][:, hp * 128:(hp + 1) * 128], ARbd[:, hp, ch, 0:128], Hb[:, hp, :], start=True, stop=False)
                    r = e.matmul(pb[0][:, hp * 128:(hp + 1) * 128], M2[:, hp, 0:128], Vtm[:, hp, :], start=False, stop=True)
                return r
            sc.op("pe", mmw, reads=[tk_AR, tk_Hb, tk_M2, tk_Vtm], writes=[tk_pb[0]])
            sc.op("act", lambda e: e.activation(fl(Wb), pb[0], AF.Copy), reads=[tk_pb[0]], writes=[tk_Wb])

            def mmu(e, M1=M1):
                r = None
                for hp in range(4):
                    o = pb[1][:, hp * 128:(hp + 1) * 128]
                    e.matmul(o, identb[:], Wb[:, hp, :], start=True, stop=False)
                    r = e.matmul(o, M1[:, hp, 0:128], Wb[:, hp, :], start=False, stop=True)
                return r
            sc.op("pe", mmu, reads=[tk_M1, tk_Wb, tk_identb], writes=[tk_pb[1]])
            sc.op("act", lambda e: e.activation(fl(Ub), pb[1], AF.Copy), reads=[tk_pb[1]], writes=[tk_Ub])

            def mmh(e, Bhtm=Bhtm, Khtm=Khtm, Vtm=Vtm):
                r = None
                for hp in range(4):
                    o = pb[3][:, hp * 128:(hp + 1) * 128]
                    e.matmul(o, Bhtm[:, hp, :], Ub[:, hp, :], start=True, stop=False)
                    r = e.matmul(o, Khtm[:, hp, :], Vtm[:, hp, :], start=False, stop=True)
                return r
            sc.op("pe", mmh, reads=[tk_Bhtm, tk_Ub, tk_Khtm, tk_Vtm], writes=[tk_pb[3]])

            def mmy(e, ch=ch, M1=M1, M2=M2, Vtm=Vtm):
                r = None
                for hp in range(4):
                    o = pb[2][:, hp * 128:(hp + 1) * 128]
                    e.matmul(o, ARbd[:, hp, ch, 128:256], Hb[:, hp, :], start=True, stop=False)
                    e.matmul(o, M1[:, hp, 128:256], Ub[:, hp, :], start=False, stop=False)
                    r = e.matmul(o, M2[:, hp, 128:256], Vtm[:, hp, :], start=False, stop=True)
                return r
            sc.op("pe", mmy, reads=[tk_AR, tk_Hb, tk_M1, tk_Ub, tk_M2, tk_Vtm], writes=[tk_pb[2]])
            pcv = pcb[:].rearrange("p (a c) -> p a c", c=NCH)[:, :, ch:ch + 1].to_broadcast([128, 4, 128])
            seq("dve", [(lambda e, pcv=pcv: e.tensor_tensor(Hs[:], Hs[:], pcv, ALU.mult)),
                        lambda e: e.tensor_tensor(Hs[:], Hs[:], v4(pb[3]), ALU.add)],
                [tk_pb[3], tk_pc], [tk_H])
            sc.op("act", lambda e: e.activation(Hb[:], Hs[:], AF.Copy), reads=[tk_H], writes=[tk_Hb])
            for hh in range(2):
                src = v4(pb[2][hh * 64:(hh + 1) * 64, :])[:, :, hh * 64:(hh + 1) * 64]
                dst = Yt[hh * 64:(hh + 1) * 64, ch, :, :]
                if last:
                    yfv = Yf[hh * 64:(hh + 1) * 64, ch, :, :]
                    sc.op("dve", (lambda e, src=src, dst=dst, yfv=yfv: e.tensor_tensor(dst, src, yfv, ALU.add)),
                          reads=[tk_pb[2], tk_Yf], writes=[tk_Yt])
                else:
                    sc.op("act", (lambda e, src=src, dst=dst: e.activation(dst, src, AF.Copy)), reads=[tk_pb[2]], writes=[tk_Yt])
        if RWS < 9:
            return
        if not last:
            sc.op("pool", (lambda e: e.dma_start(out=yf_d[:, ti * NCH:(ti + 1) * NCH, :, :], in_=Yt[:])),
                  reads=[tk_Yt], writes=[tk_yfd], owner=tk_Yt)
            return
        Y16 = Yt[:].rearrange("p c a v -> p (c a) v")
        cen16 = cen[:].rearrange("p c a v -> p (c a) v")

        seq("dve", [lambda e: e.tensor_reduce(st1[:], Y16, AX.X, ALU.add),
                    lambda e: e.tensor_scalar(st1[:], st1[:], -1.0 / 64, None, ALU.mult),
                    lambda e: e.tensor_tensor(cen16, Y16, st1[:].unsqueeze(2).to_broadcast([128, 16, 64]), ALU.add),
                    lambda e: e.tensor_tensor(Y16, cen16, cen16, ALU.mult),
                    lambda e: e.tensor_reduce(st2[:], Y16, AX.X, ALU.add)], [], [tk_Yt, tk_cen, tk_st])
        sc.op("act", lambda e: e.activation(st2[:], st2[:], AF.Sqrt, bias=64e-5, scale=1.0 / 64), reads=[tk_st], writes=[tk_st])
        sc.op("dve", lambda e: e.reciprocal(st2[:], st2[:]), reads=[tk_st], writes=[tk_st])
        for hh in range(2):
            ov = ynbd[hh * 64:(hh + 1) * 64, :, :, hh * 64:(hh + 1) * 64]
            iv = cen[hh * 64:(hh + 1) * 64]
            rv = st2[hh * 64:(hh + 1) * 64, :].rearrange("p (c a) -> p c a", a=4).unsqueeze(3).to_broadcast([64, NCH, 4, 64])
            sc.op("dve", (lambda e, ov=ov, iv=iv, rv=rv: e.tensor_tensor(ov, iv, rv, ALU.mult)), reads=[tk_cen, tk_st], writes=[tk_ynbd])
        for ch in range(NCH):
            def try_(e, ch=ch):
                r = None
                for hp in range(4):
                    r = e.matmul(pb[0][:, hp * 128:(hp + 1) * 128], ynbd[:, ch, hp, :], identb[:], start=True, stop=True)
                return r
            sc.op("pe", try_, reads=[tk_ynbd, tk_identb], writes=[tk_pb[0]])
            for hh in range(2):
                src = pb[0][hh * 64:(hh + 1) * 64, :].rearrange("p (a b) -> p a b", a=4)[:, :, hh * 64:(hh + 1) * 64]
                dst = ynT[hh * 64:(hh + 1) * 64, :, ch * 64:(ch + 1) * 64]
                if hh == 0:
                    sc.op("act", (lambda e, src=src, dst=dst: e.activation(dst, src, AF.Copy)), reads=[tk_pb[0]], writes=[tk_ynT])
                else:
                    sc.op("dve", (lambda e, src=src, dst=dst: e.tensor_copy(dst, src)), reads=[tk_pb[0]], writes=[tk_ynT])

        seq("dve", [lambda e: e.tensor_tensor(ynT[:], ynT[:], bcast(lws[:], [128, 4, TT_]), ALU.mult),
                    lambda e: e.tensor_tensor(ynT[:], ynT[:], bcast(lbs[:], [128, 4, TT_]), ALU.add),
                    lambda e: e.tensor_tensor(ynT[:], ynT[:], bonus[:], ALU.add),
                    lambda e: e.tensor_tensor(orw[:], ynT[:], gbuf[:], ALU.mult)], [tk_par, tk_bonus, tk_g], [tk_ynT, tk_orw])
        sc.op("pool", (lambda e: e.dma_start(out=yrwT[:, :, t0:t0 + TT_], in_=orw[:])), reads=[tk_orw], writes=[tk_yrwT], owner=tk_orw)

    def rw_phase(l):
        rw_setup(l)
        rw_shift(0)
        for ti in range(NTR):
            rw_tile(l, 0, ti, ti + 1 if ti + 1 < NTR else NTR - 1)
        sc.op("pool", lambda e: e.memset(Hs[:], 0.0), writes=[tk_H])
        for ti in range(NTR - 1, -1, -1):
            rw_tile(l, 1, ti, ti - 1 if ti > 0 else None)


    def mixers(l):
        sc.barrier()
        if mode in ("full", "na"):
            na_phase(l)
        else:
            zero_fill(ynaT, tk_ynaT, 64)
        sc.barrier()
        if mode in ("full", "rw"):
            rw_phase(l)
        else:
            zero_fill(yrwT, tk_yrwT, 128)
        sc.barrier()

    for t in range(NT):
        load_x_tile(t)
        phase_a(0, t)
    for l in range(LRUN):
        if do_mix:
            mixers(l)
        for t in range(NT):
            phase_c(l, t)
            if l + 1 < LRUN:
                phase_a(l + 1, t)
            else:
                store_out_tile(t)
    sc.finish("pool")
    sc.emit()
    return nc, sc


def arrange(W, kp):
    Kd, Nd = W.shape
    KC = Kd // kp
    NJ = Nd // 128
    return np.ascontiguousarray(W.reshape(KC, kp, NJ, 128).transpose(2, 1, 0, 3)).reshape(NJ, kp, KC * 128)


def host_weights(inp):
    out = {}

    def st(fn):
        return np.stack([fn(l) for l in range(L)], 0)
    out["w_wg1"] = st(lambda l: arrange(inp["ffn1_wg"][l], 128))
    out["w_wu1"] = st(lambda l: arrange(inp["ffn1_wu"][l], 128))
    out["w_wd1"] = st(lambda l: arrange(inp["ffn1_wd"][l], 128))
    out["w_win"] = st(lambda l: arrange(inp["w_in"][l], 128))
    out["w_wona"] = st(lambda l: arrange(inp["w_out"][l][:512], 64))
    out["w_worw"] = st(lambda l: arrange(inp["w_out"][l][512:], 128))
    out["w_wg2"] = st(lambda l: arrange(inp["ffn2_wg"][l], 128))
    out["w_wu2"] = st(lambda l: arrange(inp["ffn2_wu"][l], 128))
    out["w_wd2"] = st(lambda l: arrange(inp["ffn2_wd"][l], 128))
    out["w_pgate"] = st(lambda l: arrange(inp["ple_gate"][l], 128))
    out["w_pup"] = st(lambda l: arrange(inp["ple_up"][l], 128))
    for nm in NORMS:
        out[nm] = np.ascontiguousarray(inp[nm], dtype=np.float32)
    out["final_norm"] = np.ascontiguousarray(inp["final_norm"], dtype=np.float32)
    return out


def host_extra(inp):
    rpb = np.asarray(inp["na_rpb"], dtype=np.float32)
    tb = np.full((L, 2, 64, 8, 14, 64), NEG, np.float32)
    j = np.arange(64)
    cs = np.clip(j - 8, 0, 48)
    for par in range(2):
        for m in range(14):
            if m + par > 14:
                continue
            for cp in range(64):
                ok = (cp >= cs) & (cp < cs + 16)
                jj = j[ok]
                tb[:, par, cp, :, m, jj] = np.transpose(rpb[:, :, m + par, cp - jj + 15], (2, 0, 1))
    out = {"tb": np.ascontiguousarray(tb.reshape(L, 128, 8 * 14 * 64))}
    p = np.arange(128)[:, None]
    f = np.arange(128)[None, :]
    same = (p // 64) == (f // 64)
    pl, fl = p % 64, f % 64
    rwm = np.stack([same & (fl < pl), same & (fl > pl), same & (fl <= pl), same & (fl >= pl)], 1).astype(np.float32)
    out["rwmask"] = np.ascontiguousarray(rwm)
    for nm in ("rw_mu", "rw_w0", "rw_w_up", "rw_a0", "rw_a_up", "rw_g_up", "rw_k_k", "rw_k_a", "rw_lnx_w", "rw_lnx_b"):
        out[nm] = np.ascontiguousarray(inp[nm], dtype=np.float32)
    out["rw_r_k"] = np.ascontiguousarray(inp["rw_r_k"], dtype=np.float32).reshape(L, 512)
    return out


def kernel(**inputs):
    NSEG, SEGLEN = 4, 4096
    TC = NSEG * SEGLEN
    inp = {k: np.asarray(v) for k, v in inputs.items()}
    nc, sc = build_program(NSEG, SEGLEN, "full")
    hw = host_weights(inp)
    hw.update(host_extra(inp))
    xp = np.asarray(inp["x_prompt"], dtype=np.float32)
    xs = np.asarray(inp["x_sample"], dtype=np.float32)
    pp = np.asarray(inp["p_prompt"], dtype=np.float32)
    ps = np.asarray(inp["p_sample"], dtype=np.float32)
    zx = np.zeros((TC, D), np.float32)
    zp = np.zeros((L, TC, PLE), np.float32)
    in_maps = []
    for c in range(8):
        m = dict(hw)
        fl = np.zeros((128, 2), np.float32)
        if c < 4:
            m["xin"] = np.ascontiguousarray(xp[4 * c:4 * c + 4].reshape(TC, D))
            m["pin"] = np.ascontiguousarray(pp[:, 4 * c:4 * c + 4].reshape(L, TC, PLE))
            fl[:, 1] = 1.0
        elif c < 6:
            m["xin"] = np.ascontiguousarray(xs[c - 4])
            m["pin"] = np.ascontiguousarray(ps[:, c - 4])
            fl[:, 0] = 1.0
        else:
            m["xin"] = zx
            m["pin"] = zp
            fl[:, 1] = 1.0
        m["flags"] = fl
        in_maps.append(m)
    res = run_bass_kernel_spmd(nc, in_maps, core_ids=list(range(8)))
    outs = [np.asarray(r["yout"], dtype=np.float32) for r in res.results]
    y_prompt = np.concatenate([outs[c].reshape(4, SEGLEN, D) for c in range(4)], axis=0)
    y_sample = np.stack([outs[4], outs[5]], axis=0)
    return (y_prompt, y_sample)
```

```python
import numpy as np
import concourse.bass as bass
import concourse.mybir as mybir
from concourse.bass_utils import run_bass_kernel_spmd

F32 = mybir.dt.float32
BF16 = mybir.dt.bfloat16
AF = mybir.ActivationFunctionType
ALU = mybir.AluOpType
AX = mybir.AxisListType

D = 1024
DFF = 2816
NJF = DFF // 128
PW = 3456
NJP = PW // 128
PLE = 256
L = 2
NEG = -30000.0
ENGS = ("pe", "act", "dve", "pool", "sp")


class Tk:
    __slots__ = ("name", "w", "r", "sem", "semval", "last_dma")

    def __init__(self, name):
        self.name = name
        self.w = None
        self.r = {}
        self.sem = None
        self.semval = 0
        self.last_dma = None


class Sched:
    def __init__(self, nc):
        self.nc = nc
        self.q = {e: [] for e in ENGS}
        self.esem = {e: nc.alloc_semaphore("es_" + e) for e in ENGS}
        self.ecount = {e: 0 for e in ENGS}
        self.waited = {e: {} for e in ENGS}
        self.nops = 0
        self.nsem = 0
        self.owners = []

    def tk(self, name):
        return Tk(name)

    def _need(self, eng, ev):
        if ev is None:
            return
        sem, val = ev
        k = id(sem)
        if self.waited[eng].get(k, 0) < val:
            self.waited[eng][k] = val
            self.q[eng].append(("w", sem, val))

    def op(self, eng, fn, reads=(), writes=(), owner=None, ndma=1):
        for t in reads:
            self._need(eng, t.w)
        for t in writes:
            self._need(eng, t.w)
            for ev in t.r.values():
                self._need(eng, ev)
        if owner is not None:
            if owner.sem is None:
                owner.sem = self.nc.alloc_semaphore("ds%d" % self.nsem)
                self.nsem += 1
                self.owners.append(owner)
            self._need(eng, owner.last_dma)
            owner.semval += 16 * ndma
            ev = (owner.sem, owner.semval)
            owner.last_dma = ev
            self.q[eng].append(("d", fn, owner.sem))
        else:
            self.ecount[eng] += 1
            ev = (self.esem[eng], self.ecount[eng])
            self.q[eng].append(("o", fn, self.esem[eng]))
        for t in reads:
            t.r[id(ev[0])] = ev
        for t in writes:
            t.w = ev
            t.r = {}
        self.nops += 1
        return ev

    def barrier(self):
        evs = [(self.esem[e], self.ecount[e]) for e in ENGS if self.ecount[e]]
        evs += [o.last_dma for o in self.owners]
        for e in ENGS:
            for ev in evs:
                self._need(e, ev)

    def finish(self, eng="pool"):
        for e in ENGS:
            if self.ecount[e]:
                self._need(eng, (self.esem[e], self.ecount[e]))
        for o in self.owners:
            self._need(eng, o.last_dma)

    def emit(self):
        nc = self.nc
        q = self.q

        def run(e, name):
            for it in q[name]:
                if it[0] == "w":
                    e.wait_ge(it[1], it[2])
                elif it[0] == "o":
                    ins = it[1](e)
                    ins.then_inc(it[2], 1)
                else:
                    r = it[1](e)
                    if isinstance(r, (list, tuple)):
                        for ins in r:
                            ins.then_inc(it[2], 16)
                    else:
                        r.then_inc(it[2], 16)

        with nc.Block() as block:
            @block.tensor
            def _(e):
                run(e, "pe")

            @block.scalar
            def _(e):
                run(e, "act")

            @block.vector
            def _(e):
                run(e, "dve")

            @block.gpsimd
            def _(e):
                run(e, "pool")

            @block.sync
            def _(e):
                run(e, "sp")


WSPEC = {
    "wg1": (128, 8, NJF), "wu1": (128, 8, NJF), "wd1": (128, NJF, 8),
    "win": (128, 8, NJP), "wona": (64, 8, 8), "worw": (128, 4, 8),
    "wg2": (128, 8, NJF), "wu2": (128, 8, NJF), "wd2": (128, NJF, 8),
    "pgate": (128, 8, 8), "pup": (128, 2, 8),
}
NORMS = ("ffn1_norm", "mix_norm", "ffn2_norm", "ple_norm")


def build_program(NSEG, SEGLEN, mode="full"):
    do_mix = mode != "nomix"
    T = NSEG * SEGLEN
    NT = T // 512
    nc = bass.Bass("TRN2", target_bir_lowering=False)
    sc = Sched(nc)

    def dram_in(name, shape, dt=F32):
        return nc.dram_tensor(name, list(shape), dt, kind="ExternalInput").ap()

    def dram_tmp(name, shape, dt):
        return nc.dram_tensor(name, list(shape), dt, kind="Internal").ap()

    xin = dram_in("xin", [T, D])
    pin = dram_in("pin", [L, T, PLE])
    flags_d = dram_in("flags", [128, 2])
    yout = nc.dram_tensor("yout", [T, D], F32, kind="ExternalOutput").ap()
    wsrc = {}
    wbf = {}
    for nm, (kp, KC, NJ) in WSPEC.items():
        wsrc[nm] = dram_in("w_" + nm, [L, NJ, kp, KC * 128])
        wbf[nm] = dram_tmp("b_" + nm, [L, NJ, kp, KC * 128], BF16)
    norms_d = {nm: dram_in(nm, [L, D]) for nm in NORMS}
    fnorm_d = dram_in("final_norm", [D])
    h1T = dram_tmp("h1T", [8, 128, T], F32)
    projT = dram_tmp("projT", [15, 128, T], F32)
    qkvT = dram_tmp("qkvT", [12, 128, T], BF16)
    tk_qkvT = sc.tk("qkvT")
    ynaT = dram_tmp("ynaT", [64, 8, T], BF16)
    import os as _os0
    DBG = int(_os0.environ.get("DBG", "0"))
    LRUN = int(_os0.environ.get("LRUN", str(L)))
    if DBG:
        yrwT = nc.dram_tensor("yrwT", [128, 4, T], BF16, kind="ExternalOutput").ap()
    else:
        yrwT = dram_tmp("yrwT", [128, 4, T], BF16)
    tk_h1T = sc.tk("h1T")
    tk_projT = sc.tk("projT")
    tk_ynaT = sc.tk("ynaT")
    tk_yrwT = sc.tk("yrwT")

    def sb(name, shape, dt=F32):
        return nc.alloc_sbuf_tensor(name, list(shape), dt)

    ident = sb("ident", [128, 128]); tk_ident = sc.tk("ident")
    identb = sb("identb", [128, 128], BF16); tk_identb = sc.tk("identb")
    onesb = sb("onesb", [128, 128], BF16); tk_ones = sc.tk("ones")
    gains = sb("gains", [128, L * 4 + 1, 8]); tk_gains = sc.tk("gains")
    flags = sb("flags_s", [128, 2]); tk_flags = sc.tk("flags")
    NRING = 4
    RING_EL = NJF * 128
    ring_i = [0]
    ARENA = 165 * 1024
    arena = sb("arena", [128, ARENA // 4])

    def carver():
        off = [0]

        def view(o, shape, dt):
            esz = 4 if dt == F32 else 2
            n = 1
            for d_ in shape[1:]:
                n *= d_
            nbytes = (n * esz + 63) // 64 * 64
            assert o + nbytes <= ARENA, (o, nbytes)
            ap = arena[:shape[0], o // 4:(o + nbytes) // 4]
            if dt != F32:
                ap = ap.bitcast(dt)
            ap = ap[:, :n]
            if len(shape) == 3:
                ap = ap.rearrange("p (a b) -> p a b", a=shape[1])
            elif len(shape) == 4:
                ap = ap.rearrange("p (a b c) -> p a b c", a=shape[1], b=shape[2])
            return ap, nbytes

        def take(shape, dt=F32):
            ap, nbytes = view(off[0], shape, dt)
            take.last = off[0]
            off[0] += nbytes
            return ap

        def at(o, shape, dt=F32):
            return view(o, shape, dt)[0]
        take.at = at
        take.off = off
        return take

    tkW = carver()
    cst = [tkW([128, 4096]) for i in range(2)]; tk_cst = [sc.tk("cst%d" % i) for i in range(2)]
    cstb = [tkW([128, 4096], BF16) for i in range(2)]; tk_cstb = [sc.tk("cstb%d" % i) for i in range(2)]
    tkA = carver()
    h = tkA([128, 8, 512]); tk_h = [sc.tk("h%d" % c) for c in range(8)]
    nb = tkA([128, 8, 512], BF16); tk_n = sc.tk("n")
    sq = tkA([128, 8, 512], BF16); tk_sq = sc.tk("sq")
    act = tkA([128, NJF, 512], BF16); tk_act = [sc.tk("act%d" % j) for j in range(NJF)]
    rstd = tkA([128, 512]); tk_rstd = sc.tk("rstd")
    sg = [tkA([128, 512]) for i in range(2)]; tk_sg = [sc.tk("sg%d" % i) for i in range(2)]
    ev_f = [tkA([128, 512]) for i in range(3)]; tk_evf = [sc.tk("evf%d" % i) for i in range(3)]
    ev_b = [tkA([128, 512], BF16) for i in range(3)]; tk_evb = [sc.tk("evb%d" % i) for i in range(3)]
    tokbuf = tkA([128, 4, D]); tk_tok = sc.tk("tokbuf")
    pbuf = tkA([128, 4, PLE]); tk_pbuf = sc.tk("pbuf")
    pT = tkA([128, 2, 512], BF16); tk_pT = sc.tk("pT")
    ynat = tkA([64, 8, 512], BF16); tk_ynat = sc.tk("ynat")
    yrwt = tkA([128, 4, 512], BF16); tk_yrwt = sc.tk("yrwt")
    sqf = tkA([128, 8, 512]); tk_sqf = sc.tk("sqf")
    ring = [tkA([128, RING_EL], BF16) for i in range(NRING)]
    tk_ring = [sc.tk("ring%d" % i) for i in range(NRING)]

    pall = nc.alloc_psum_tensor("pall", [128, 4096], F32)
    pb = [pall[:, i * 512:(i + 1) * 512] for i in range(8)]
    tk_pb = [sc.tk("pb%d" % i) for i in range(8)]
    rr = [0]

    def next_bank():
        i = 1 + (rr[0] % 6)
        rr[0] += 1
        return pb[i], tk_pb[i]

    sc.op("pool", lambda e: e.memset(ident[:], 1.0), writes=[tk_ident])
    sc.op("pool", lambda e: e.affine_select(out=ident[:], in_=ident[:], pattern=[[-1, 128]],
                                            compare_op=ALU.is_equal, fill=0.0, base=0, channel_multiplier=1),
          reads=[tk_ident], writes=[tk_ident])
    sc.op("pool", lambda e: e.tensor_copy(identb[:], ident[:]), reads=[tk_ident], writes=[tk_identb])
    sc.op("pool", lambda e: e.memset(onesb[:], 1.0), writes=[tk_ones])
    sc.op("pool", lambda e: e.dma_start(out=flags[:], in_=flags_d), writes=[tk_flags], owner=tk_flags)

    def ld_gains(e):
        r = []
        for l in range(L):
            for i, nm in enumerate(NORMS):
                r.append(e.dma_start(out=gains[:, l * 4 + i, :],
                                     in_=norms_d[nm][l].rearrange("(c p) -> p c", p=128),
                                     allow_slow_non_contiguous=True))
        r.append(e.dma_start(out=gains[:, L * 4, :], in_=fnorm_d.rearrange("(c p) -> p c", p=128),
                             allow_slow_non_contiguous=True))
        return r
    sc.op("pool", ld_gains, writes=[tk_gains], owner=tk_gains, ndma=L * 4 + 1)

    cvt_i = [0]

    def convert(src_flat, dst_flat, nper):
        nblk = (nper + 4095) // 4096
        while nper % nblk:
            nblk += 1
        F = nper // nblk
        for b in range(nblk):
            i = cvt_i[0] % 2
            cvt_i[0] += 1
            s_ap = src_flat[:, b * F:(b + 1) * F]
            d_ap = dst_flat[:, b * F:(b + 1) * F]
            sc.op("sp", (lambda e, i=i, s_ap=s_ap, F=F: e.dma_start(out=cst[i][:, :F], in_=s_ap)),
                  writes=[tk_cst[i]], owner=tk_cst[i])
            eng = ("dve", "act")[i]
            if eng == "dve":
                sc.op("dve", (lambda e, i=i, F=F: e.tensor_copy(cstb[i][:, :F], cst[i][:, :F])),
                      reads=[tk_cst[i]], writes=[tk_cstb[i]])
            else:
                sc.op("act", (lambda e, i=i, F=F: e.activation(cstb[i][:, :F], cst[i][:, :F], AF.Copy)),
                      reads=[tk_cst[i]], writes=[tk_cstb[i]])
            sc.op("pool", (lambda e, i=i, d_ap=d_ap, F=F: e.dma_start(out=d_ap, in_=cstb[i][:, :F])),
                  reads=[tk_cstb[i]], owner=tk_cstb[i])

    tk_wbf = {}
    for nm, (kp, KC, NJ) in WSPEC.items():
        tot = L * NJ * kp * KC * 128
        nper = tot // 128
        s = wsrc[nm].rearrange("l j k x -> (l j k x)").rearrange("(p n) -> p n", p=128)
        d = wbf[nm].rearrange("l j k x -> (l j k x)").rearrange("(p n) -> p n", p=128)
        convert(s, d, nper)
        tk_wbf[nm] = sc.tk("wbf_" + nm)
    sc.barrier()

    def wload(nm, l, j):
        kp, KC, NJ = WSPEC[nm]
        s = ring_i[0] % NRING
        ring_i[0] += 1
        n = KC * 128
        src = wbf[nm][l, j]
        sc.op("sp", (lambda e, s=s, kp=kp, n=n, src=src: e.dma_start(out=ring[s][:kp, :n], in_=src)),
              writes=[tk_ring[s]], owner=tk_ring[s])
        return ring[s], tk_ring[s]

    def rmsnorm(gidx, out_fp32=None):
        sc.op("act", lambda e: e.activation(sq[:], h[:], AF.Square), reads=tk_h, writes=[tk_sq])

        def mm(e):
            r = None
            for c in range(8):
                r = e.matmul(pb[0][:], onesb[:], sq[:, c, :], start=(c == 0), stop=(c == 7))
            return r
        sc.op("pe", mm, reads=[tk_sq, tk_ones], writes=[tk_pb[0]])
        sc.op("act", lambda e: e.activation(rstd[:], pb[0][:], AF.Sqrt, bias=1e-6, scale=1.0 / D),
              reads=[tk_pb[0]], writes=[tk_rstd])
        sc.op("dve", lambda e: e.reciprocal(rstd[:], rstd[:]), reads=[tk_rstd], writes=[tk_rstd])
        dst = nb if out_fp32 is None else out_fp32

        def nrm(e):
            r = None
            for c in range(8):
                r = e.scalar_tensor_tensor(out=dst[:, c, :], in0=h[:, c, :], scalar=gains[:, gidx, c:c + 1],
                                           in1=rstd[:], op0=ALU.mult, op1=ALU.mult)
            return r
        return nrm

    def norm_to_n(gidx):
        nrm = rmsnorm(gidx)
        sc.op("dve", nrm, reads=tk_h + [tk_rstd, tk_gains], writes=[tk_n])

    def ffn(l, wg, wu, wd):
        for j in range(NJF):
            gbuf, gtk = wload(wg, l, j)
            ubuf, utk = wload(wu, l, j)
            pg, tpg = next_bank()
            pu, tpu = next_bank()

            def mmg(e, gbuf=gbuf, pg=pg):
                r = None
                for c in range(8):
                    r = e.matmul(pg[:], gbuf[:, c * 128:(c + 1) * 128], nb[:, c, :], start=(c == 0), stop=(c == 7))
                return r

            def mmu(e, ubuf=ubuf, pu=pu):
                r = None
                for c in range(8):
                    r = e.matmul(pu[:], ubuf[:, c * 128:(c + 1) * 128], nb[:, c, :], start=(c == 0), stop=(c == 7))
                return r
            sc.op("pe", mmg, reads=[gtk, tk_n], writes=[tpg])
            sc.op("pe", mmu, reads=[utk, tk_n], writes=[tpu])
            si = j % 2
            sc.op("act", (lambda e, si=si, pg=pg: e.activation(sg[si][:], pg[:], AF.Silu)),
                  reads=[tpg], writes=[tk_sg[si]])
            sc.op("dve", (lambda e, si=si, pu=pu, j=j: e.tensor_tensor(act[:, j, :], sg[si][:], pu[:], ALU.mult)),
                  reads=[tk_sg[si], tpu], writes=[tk_act[j]])
        for c in range(8):
            dbuf, dtk = wload(wd, l, c)
            po, tpo = next_bank()

            def mmd(e, dbuf=dbuf, po=po):
                r = None
                for j in range(NJF):
                    r = e.matmul(po[:], dbuf[:, j * 128:(j + 1) * 128], act[:, j, :], start=(j == 0), stop=(j == NJF - 1))
                return r
            sc.op("pe", mmd, reads=[dtk] + tk_act, writes=[tpo])
            sc.op("dve", (lambda e, po=po, c=c: e.scalar_tensor_tensor(out=h[:, c, :], in0=po[:], scalar=0.5, in1=h[:, c, :],
                                                                    op0=ALU.mult, op1=ALU.add)),
                  reads=[tpo, tk_h[c]], writes=[tk_h[c]])

    def load_x_tile(t):
        t0 = t * 512
        sc.op("pool", lambda e: e.dma_start(out=tokbuf[:], in_=xin[t0:t0 + 512, :].rearrange("(b p) d -> p b d", p=128)),
              writes=[tk_tok], owner=tk_tok)
        for c in range(8):
            def tr(e, c=c):
                r = None
                for b in range(4):
                    r = e.transpose(pb[7][:, b * 128:(b + 1) * 128], tokbuf[:, b, c * 128:(c + 1) * 128], ident[:])
                return r
            sc.op("pe", tr, reads=[tk_tok, tk_ident], writes=[tk_pb[7]])
            if c % 2 == 0:
                sc.op("act", (lambda e, c=c: e.activation(h[:, c, :], pb[7][:], AF.Copy)), reads=[tk_pb[7]], writes=[tk_h[c]])
            else:
                sc.op("dve", (lambda e, c=c: e.tensor_copy(h[:, c, :], pb[7][:])), reads=[tk_pb[7]], writes=[tk_h[c]])

    def store_out_tile(t):
        t0 = t * 512
        nrm = rmsnorm(L * 4, out_fp32=sqf)
        sc.op("dve", nrm, reads=tk_h + [tk_rstd, tk_gains], writes=[tk_sqf])
        for c in range(8):
            def tr(e, c=c):
                r = None
                for b in range(4):
                    r = e.transpose(pb[7][:, b * 128:(b + 1) * 128], sqf[:, c, b * 128:(b + 1) * 128], ident[:])
                return r
            sc.op("pe", tr, reads=[tk_sqf, tk_ident], writes=[tk_pb[7]])
            dst = tokbuf[:, :, c * 128:(c + 1) * 128]
            src = pb[7][:].rearrange("p (b q) -> p b q", b=4)
            if c % 2 == 0:
                sc.op("act", (lambda e, dst=dst, src=src: e.activation(dst, src, AF.Copy)), reads=[tk_pb[7]], writes=[tk_tok])
            else:
                sc.op("dve", (lambda e, dst=dst, src=src: e.tensor_copy(dst, src)), reads=[tk_pb[7]], writes=[tk_tok])
        sc.op("pool", lambda e: e.dma_start(out=yout[t0:t0 + 512, :].rearrange("(b p) d -> p b d", p=128), in_=tokbuf[:]),
              reads=[tk_tok], owner=tk_tok)


    def phase_a(l, t):
        t0 = t * 512
        norm_to_n(l * 4 + 0)
        ffn(l, "wg1", "wu1", "wd1")
        sc.op("pool", lambda e: e.dma_start(out=h1T[:, :, t0:t0 + 512].rearrange("c p t -> p c t"), in_=h[:]),
              reads=tk_h, writes=[tk_h1T], owner=tk_h[0])
        if not do_mix:
            return
        norm_to_n(l * 4 + 1)
        for j in range(NJP):
            wbuf, wtk = wload("win", l, j)
            po, tpo = next_bank()

            def mmp(e, wbuf=wbuf, po=po):
                r = None
                for c in range(8):
                    r = e.matmul(po[:], wbuf[:, c * 128:(c + 1) * 128], nb[:, c, :], start=(c == 0), stop=(c == 7))
                return r
            sc.op("pe", mmp, reads=[wtk, tk_n], writes=[tpo])
            k = j % 3
            scale = 0.125 if j < 4 else 1.0
            dstb = ev_b[k] if j < 12 else ev_f[k]
            dtk = tk_evb[k] if j < 12 else tk_evf[k]
            if j % 2 == 0:
                sc.op("act", (lambda e, dstb=dstb, po=po, scale=scale: e.activation(dstb[:], po[:], AF.Copy, scale=scale)),
                      reads=[tpo], writes=[dtk])
            else:
                sc.op("dve", (lambda e, dstb=dstb, po=po, scale=scale: e.tensor_scalar(dstb[:], po[:], scale, None, ALU.mult)),
                      reads=[tpo], writes=[dtk])
            if j < 12:
                sc.op("pool", (lambda e, dstb=dstb, j=j: e.dma_start(out=qkvT[j, :, t0:t0 + 512], in_=dstb[:])),
                      reads=[dtk], writes=[tk_qkvT], owner=dtk)
            else:
                sc.op("pool", (lambda e, dstb=dstb, j=j: e.dma_start(out=projT[j - 12, :, t0:t0 + 512], in_=dstb[:])),
                      reads=[dtk], writes=[tk_projT], owner=dtk)

    def phase_c(l, t):
        t0 = t * 512
        sc.op("pool", lambda e: e.dma_start(out=h[:], in_=h1T[:, :, t0:t0 + 512].rearrange("c p t -> p c t")),
              reads=[tk_h1T], writes=tk_h, owner=tk_h[0])
        if do_mix:
            sc.op("pool", lambda e: e.dma_start(out=ynat[:], in_=ynaT[:, :, t0:t0 + 512]),
                  reads=[tk_ynaT], writes=[tk_ynat], owner=tk_ynat)
            sc.op("pool", lambda e: e.dma_start(out=yrwt[:], in_=yrwT[:, :, t0:t0 + 512]),
                  reads=[tk_yrwT], writes=[tk_yrwt], owner=tk_yrwt)
            for j in range(8):
                w1, w1tk = wload("wona", l, j)
                w2, w2tk = wload("worw", l, j)
                po, tpo = next_bank()

                def mmo(e, w1=w1, w2=w2, po=po):
                    for hh in range(8):
                        e.matmul(po[:], w1[:64, hh * 128:(hh + 1) * 128], ynat[:, hh, :], start=(hh == 0), stop=False)
                    r = None
                    for c in range(4):
                        r = e.matmul(po[:], w2[:, c * 128:(c + 1) * 128], yrwt[:, c, :], start=False, stop=(c == 3))
                    return r
                sc.op("pe", mmo, reads=[w1tk, w2tk, tk_ynat, tk_yrwt], writes=[tpo])
                sc.op("dve", (lambda e, po=po, j=j: e.tensor_tensor(h[:, j, :], h[:, j, :], po[:], ALU.add)),
                      reads=[tpo, tk_h[j]], writes=[tk_h[j]])
        norm_to_n(l * 4 + 2)
        ffn(l, "wg2", "wu2", "wd2")
        sc.op("pool", lambda e: e.dma_start(out=pbuf[:], in_=pin[l, t0:t0 + 512, :].rearrange("(b p) d -> p b d", p=128)),
              writes=[tk_pbuf], owner=tk_pbuf)
        for c in range(2):
            def tr(e, c=c):
                r = None
                for b in range(4):
                    r = e.transpose(pb[7][:, b * 128:(b + 1) * 128], pbuf[:, b, c * 128:(c + 1) * 128], ident[:])
                return r
            sc.op("pe", tr, reads=[tk_pbuf, tk_ident], writes=[tk_pb[7]])
            sc.op("act", (lambda e, c=c: e.activation(pT[:, c, :], pb[7][:], AF.Copy)), reads=[tk_pb[7]], writes=[tk_pT])
        norm_to_n(l * 4 + 3)
        for j in range(8):
            wgt, wgtk = wload("pgate", l, j)
            wup, wuptk = wload("pup", l, j)
            pg, tpg = next_bank()
            pu, tpu = next_bank()

            def mmg(e, wgt=wgt, pg=pg):
                r = None
                for c in range(8):
                    r = e.matmul(pg[:], wgt[:, c * 128:(c + 1) * 128], nb[:, c, :], start=(c == 0), stop=(c == 7))
                return r

            def mmu(e, wup=wup, pu=pu):
                r = None
                for c in range(2):
                    r = e.matmul(pu[:], wup[:, c * 128:(c + 1) * 128], pT[:, c, :], start=(c == 0), stop=(c == 1))
                return r
            sc.op("pe", mmg, reads=[wgtk, tk_n], writes=[tpg])
            sc.op("pe", mmu, reads=[wuptk, tk_pT], writes=[tpu])
            si = j % 2
            sc.op("act", (lambda e, si=si, pg=pg: e.activation(sg[si][:], pg[:], AF.Sigmoid)), reads=[tpg], writes=[tk_sg[si]])
            sc.op("dve", (lambda e, si=si, pu=pu: e.tensor_tensor(sg[si][:], sg[si][:], pu[:], ALU.mult)),
                  reads=[tk_sg[si], tpu], writes=[tk_sg[si]])
            sc.op("dve", (lambda e, si=si, j=j: e.tensor_tensor(h[:, j, :], h[:, j, :], sg[si][:], ALU.add)),
                  reads=[tk_sg[si], tk_h[j]], writes=[tk_h[j]])


    R_ = T // 64
    RS = SEGLEN // 64
    tb_d = dram_in("tb", [L, 128, 8 * 14 * 64])
    tkN = carver()
    KW = 22 * 64
    kT = tkN([64, 8, KW], BF16); tk_kT = sc.tk("kT")
    vT = tkN([128, 4, KW], BF16); tk_vT = sc.tk("vT")
    qT = tkN([64, 8, 512], BF16); tk_qT = sc.tk("qT")
    Vb = tkN([128, 21, 512], BF16); tk_Vbs = [sc.tk("Vb%d" % i) for i in range(21)]
    Tb = tkN([128, 8, 7, 2, 64][:4] if False else [128, 8 * 14 * 64]); tk_Tb = sc.tk("Tb")
    Tb5 = Tb.rearrange("p (h m2 par j) -> p h m2 par j", h=8, m2=7, par=2)
    Ssb = tkN([128, 8, 4, 64]); tk_Ssb = sc.tk("Ssb")
    Eb = tkN([128, 8, 4, 64], BF16); tk_E = sc.tk("E")
    rden = tkN([64, 512]); tk_rden = sc.tk("rden")
    tmpS = tkN([64, 512]); tk_tmpS = sc.tk("tmpS")
    tmpP = tkN([64, 512]); tk_tmpP = sc.tk("tmpP")
    ob = tkN([64, 8, 512], BF16); tk_ob = sc.tk("ob")
    zt = tkN([128, 4, 512], BF16); tk_zt = sc.tk("zt")
    Sps = pall[:, 512:2560].rearrange("p (h k q) -> p h k q", h=8, k=4)
    tk_S = tk_pb[1:5]
    pb7b = pb[7].bitcast(BF16)
    pbTs = [pb[7].bitcast(BF16), pb[0].bitcast(BF16)]
    tk_pbTs = [tk_pb[7], tk_pb[0]]

    def na_phase(l):
        sc.op("pool", lambda e: e.dma_start(out=Tb[:], in_=tb_d[l]), writes=[tk_Tb], owner=tk_Tb)
        for blk in range(R_ // 8):
            r0 = blk * 8
            klo = max(0, r0 - 7)
            khi = min(R_, r0 + 15)
            nk = (khi - klo) * 64
            sc.op("pool", (lambda e, klo=klo, khi=khi, nk=nk: e.dma_start(
                out=kT[:, :, :nk], in_=qkvT[4:8, :, klo * 64:khi * 64].rearrange("c (two p) t -> p (c two) t", two=2))),
                reads=[tk_qkvT], writes=[tk_kT], owner=tk_kT)
            sc.op("pool", (lambda e, klo=klo, khi=khi, nk=nk: e.dma_start(
                out=vT[:, :, :nk], in_=qkvT[8:12, :, klo * 64:khi * 64].rearrange("c p t -> p c t"))),
                reads=[tk_qkvT], writes=[tk_vT], owner=tk_vT)
            sc.op("pool", (lambda e, r0=r0: e.dma_start(
                out=qT[:], in_=qkvT[0:4, :, r0 * 64:r0 * 64 + 512].rearrange("c (two p) t -> p (c two) t", two=2))),
                reads=[tk_qkvT], writes=[tk_qT], owner=tk_qT)
            for o in range(klo, khi - 1):
                oo = o - klo

                pbt = pbTs[oo % 2]
                tkt = tk_pbTs[oo % 2]

                def trv(e, oo=oo, pbt=pbt):
                    r = None
                    for hp in range(4):
                        r = e.transpose(pbt[:, hp * 128:(hp + 1) * 128], vT[:, hp, oo * 64:oo * 64 + 128], identb[:])
                    return r
                sc.op("pe", trv, reads=[tk_vT, tk_identb], writes=[tkt])
                if oo % 2 == 0:
                    sc.op("act", (lambda e, oo=oo, pbt=pbt: e.activation(Vb[:, oo, :], pbt[:, 0:512], AF.Copy)),
                          reads=[tkt], writes=[tk_Vbs[oo]])
                else:
                    sc.op("dve", (lambda e, oo=oo, pbt=pbt: e.tensor_copy(Vb[:, oo, :], pbt[:, 0:512])),
                          reads=[tkt], writes=[tk_Vbs[oo]])
            rvs = []
            for i in range(r0, r0 + 8):
                seg = i // RS
                il = i % RS
                rsP = seg * RS + min(max(il - 4, 0), RS - 8)
                rsS = min(max(i - 4, 0), R_ - 8)
                if rsP == rsS:
                    rvs.append((i, rsP, 0))
                else:
                    rvs.append((i, rsS, 1))
                    rvs.append((i, rsP, 2))

            def emit_S(i, rs, r0=r0, klo=klo):
                qo = (i - r0) * 64

                def mms(e):
                    r = None
                    for hh in range(8):
                        for kc in range(4):
                            ks = (rs + 2 * kc - klo) * 64
                            r = e.matmul(Sps[:, hh, kc, :], kT[:, hh, ks:ks + 128], qT[:, hh, qo:qo + 64],
                                         start=True, stop=True)
                    return r
                sc.op("pe", mms, reads=[tk_kT, tk_qT], writes=tk_S)

            def emit_soft(i, rs):
                cl = i - rs
                s0 = 7 - cl
                tbv = Tb5[:, :, s0 // 2:s0 // 2 + 4, s0 % 2, :]
                sc.op("dve", lambda e: e.tensor_tensor(Ssb[:], Sps, tbv, ALU.add), reads=tk_S + [tk_Tb], writes=[tk_Ssb])
                sc.op("act", lambda e: e.activation(Eb[:], Ssb[:], AF.Exp), reads=[tk_Ssb], writes=[tk_E])

            def emit_pv(i, rs, kind, r0=r0, klo=klo):
                qo = (i - r0) * 64

                def mmpv(e):
                    r = None
                    for hh in range(8):
                        for kc in range(4):
                            r = e.matmul(pb[5][:64, hh * 64:(hh + 1) * 64], Vb[:, rs + 2 * kc - klo, hh * 64:(hh + 1) * 64],
                                         Eb[:, hh, kc, :], start=(kc == 0), stop=(kc == 3))
                    return r
                sc.op("pe", mmpv, reads=tk_Vbs + [tk_E], writes=[tk_pb[5]])

                def mmden(e):
                    r = None
                    for kc in range(4):
                        r = e.matmul(pb[6][:64, :].rearrange("p (h q) -> p h q", h=8), onesb[:, :64], Eb[:, :, kc, :],
                                     start=(kc == 0), stop=(kc == 3))
                    return r
                sc.op("pe", mmden, reads=[tk_E, tk_ones], writes=[tk_pb[6]])
                sc.op("dve", lambda e: e.reciprocal(rden[:], pb[6][:64, :]), reads=[tk_pb[6]], writes=[tk_rden])
                o3 = pb[5][:64, :].rearrange("p (h q) -> p h q", h=8)
                r3 = rden[:].rearrange("p (h q) -> p h q", h=8)
                if kind == 0:
                    sc.op("dve", lambda e: e.tensor_tensor(ob[:, :, qo:qo + 64], o3, r3, ALU.mult),
                          reads=[tk_pb[5], tk_rden], writes=[tk_ob])
                elif kind == 1:
                    sc.op("dve", lambda e: e.scalar_tensor_tensor(out=tmpS[:], in0=pb[5][:64, :], scalar=flags[:64, 0:1], in1=rden[:],
                                                                  op0=ALU.mult, op1=ALU.mult),
                          reads=[tk_pb[5], tk_rden, tk_flags], writes=[tk_tmpS])
                else:
                    sc.op("dve", lambda e: e.scalar_tensor_tensor(out=tmpP[:], in0=pb[5][:64, :], scalar=flags[:64, 1:2], in1=rden[:],
                                                                  op0=ALU.mult, op1=ALU.mult),
                          reads=[tk_pb[5], tk_rden, tk_flags], writes=[tk_tmpP])
                    sc.op("dve", lambda e: e.tensor_tensor(ob[:, :, qo:qo + 64], tmpS[:].rearrange("p (h q) -> p h q", h=8),
                                                           tmpP[:].rearrange("p (h q) -> p h q", h=8), ALU.add),
                          reads=[tk_tmpS, tk_tmpP], writes=[tk_ob])

            import os
            NAS = int(os.environ.get("NAS", "3"))
            if NAS >= 1:
                emit_S(rvs[0][0], rvs[0][1])
            for n, (i, rs, kind) in enumerate(rvs):
                if NAS >= 2:
                    emit_soft(i, rs)
                if n + 1 < len(rvs) and NAS >= 1:
                    emit_S(rvs[n + 1][0], rvs[n + 1][1])
                if NAS >= 3:
                    emit_pv(i, rs, kind)
            sc.op("pool", (lambda e, r0=r0: e.dma_start(out=ynaT[:, :, r0 * 64:r0 * 64 + 512], in_=ob[:])),
                  reads=[tk_ob], writes=[tk_ynaT], owner=tk_ob)

    def zero_fill(dst, tk_dst, npart):
        sc.op("pool", lambda e: e.memset(zt[:], 0.0), writes=[tk_zt])
        for t in range(NT):
            if npart == 128:
                sc.op("pool", (lambda e, t=t: e.dma_start(out=dst[:, :, t * 512:(t + 1) * 512], in_=zt[:])),
                      reads=[tk_zt], writes=[tk_dst], owner=tk_zt)
            else:
                for hh2 in range(2):
                    sc.op("pool", (lambda e, t=t, hh2=hh2: e.dma_start(out=dst[:, hh2 * 4:hh2 * 4 + 4, t * 512:(t + 1) * 512], in_=zt[:64])),
                          reads=[tk_zt], writes=[tk_dst], owner=tk_zt)

    TT_ = 256
    NTR = T // TT_
    SC = 0.606531
    rwp_d = {nm: dram_in(nm, shp) for nm, shp in [
        ("rw_mu", [L, 1920]), ("rw_w0", [L, 2, 512]), ("rw_w_up", [L, 2, 64, 512]), ("rw_a0", [L, 2, 512]),
        ("rw_a_up", [L, 2, 64, 512]), ("rw_g_up", [L, 128, 512]), ("rw_k_k", [L, 512]), ("rw_k_a", [L, 512]),
        ("rw_r_k", [L, 512]), ("rw_lnx_w", [L, 512]), ("rw_lnx_b", [L, 512])]}
    rwmask_d = dram_in("rwmask", [128, 4, 128])
    yf_d = dram_tmp("yf_d", [128, T // 64, 4, 64], F32); tk_yfd = sc.tk("yfd")
    tkR = carver()
    uA0 = tkR([128, 4, TT_ + 2]); uA = [uA0, uA0]; tk_uA0 = sc.tk("uA0"); tk_uA = [tk_uA0, tk_uA0]
    shb = tkR([128, 4, TT_]); tk_sh = sc.tk("sh"); o_sh = tkR.last
    xr = tkR([128, 4, TT_]); xk = tkR([128, 4, TT_]); xv = tkR([128, 4, TT_])
    tk_x = [sc.tk("xr"), sc.tk("xk"), sc.tk("xv")]
    xs3 = [xr, xk, xv]
    uB = tkR([64, 4, TT_ + 2]); tk_uB = sc.tk("uB")
    xB = tkR([64, 4, TT_]); tk_xB = sc.tk("xB")
    uC = tkR([128, 1, TT_ + 2]); tk_uC = sc.tk("uC")
    xCf = tkR([128, 1, TT_]); tk_xCf = sc.tk("xCf")
    xCb = tkR([128, TT_], BF16); tk_xCb = sc.tk("xCb")
    twb = tkR([64, TT_], BF16); tk_twb = sc.tk("twb")
    alb = tkR([64, 2, TT_], BF16); tk_alb = sc.tk("alb")
    sqkb = tkR([128, 4, TT_], BF16); tk_sqkb = sc.tk("sqkb")
    kkn = tkR([128, 4, TT_]); tk_kkn = sc.tk("kkn"); o_kkn = tkR.last
    sig = tkR([128, 4, TT_]); tk_sig = sc.tk("sig"); o_sig = tkR.last
    cs = tkR([128, 4, TT_]); tk_cs = sc.tk("cs"); o_cs = tkR.last
    t1 = tkR([128, 4, TT_]); tk_t1 = sc.tk("t1"); o_t1 = tkR.last
    t2 = tkR([128, 4, TT_]); tk_t2 = sc.tk("t2"); o_t2 = tkR.last
    e0 = tkR([128, 4, TT_]); tk_e0 = sc.tk("e0"); o_e0 = tkR.last
    e1 = tkR([128, 4, TT_]); tk_e1 = sc.tk("e1")
    asg = tkR([128, 4, TT_]); tk_asg = sc.tk("asg"); o_asg = tkR.last
    asf = tkR.at(o_sh, [128, 4, TT_]); tk_asf = tk_sh
    bb = tkR([128, 4, TT_]); tk_bb = sc.tk("bb"); o_bb = tkR.last
    kd = tkR([128, 4, TT_]); tk_kd = sc.tk("kd"); o_kd = tkR.last
    bonus = tkR.at(o_asg, [128, 4, TT_]); tk_bonus = tk_asg
    gbuf = tkR.at(o_kkn, [128, 4, TT_]); tk_g = tk_kkn
    NCH = TT_ // 64
    ARbd = tkR([128, 4, NCH, 256], BF16); tk_AR = sc.tk("AR")
    Btbd = tkR([128, 4, NCH, 128], BF16); tk_Bt = sc.tk("Bt")
    Ktbd = tkR([128, 4, NCH, 128], BF16); tk_Kt = sc.tk("Kt")
    Bhbd = tkR([128, 4, NCH, 128], BF16); tk_Bh = sc.tk("Bh")
    Khbd = tkR([128, 4, NCH, 128], BF16); tk_Kh = sc.tk("Kh")
    Vbd = tkR([128, 4, NCH, 128], BF16); tk_Vbd = sc.tk("Vbd")
    Sb = [tkR([128, 4, 128], BF16) for _ in range(4)]; tk_Sb = [sc.tk("Sb%d" % i) for i in range(4)]
    STb = [tkR([128, 4, 128], BF16) for _ in range(4)]; tk_STb = [sc.tk("STb%d" % i) for i in range(4)]
    TTb = [tkR([128, 4, 128], BF16) for _ in range(4)]; tk_TTb = [sc.tk("TTb%d" % i) for i in range(4)]
    M1s = [tkR([128, 4, 256], BF16) for _ in range(4)]; tk_M1s = [sc.tk("M1_%d" % i) for i in range(4)]
    M2s = [tkR([128, 4, 256], BF16) for _ in range(4)]; tk_M2s = [sc.tk("M2_%d" % i) for i in range(4)]
    Vtms = [tkR([128, 4, 128], BF16) for _ in range(2)]; tk_Vtms = [sc.tk("Vtm%d" % i) for i in range(2)]
    Bhtms = [tkR([128, 4, 128], BF16) for _ in range(2)]; tk_Bhtms = [sc.tk("Bhtm%d" % i) for i in range(2)]
    Khtms = [tkR([128, 4, 128], BF16) for _ in range(2)]; tk_Khtms = [sc.tk("Khtm%d" % i) for i in range(2)]
    Wb = tkR([128, 4, 128], BF16); tk_Wb = sc.tk("Wb")
    Ub = tkR([128, 4, 128], BF16); tk_Ub = sc.tk("Ub")
    Hs = tkR([128, 4, 128]); tk_H = sc.tk("H")
    Hb = tkR([128, 4, 128], BF16); tk_Hb = sc.tk("Hb")
    Yt = tkR.at(o_sig, [128, NCH, 4, 64]); tk_Yt = tk_sig
    Yf = tkR.at(o_e0, [128, NCH, 4, 64]); tk_Yf = tk_e0
    cen = tkR.at(o_t1, [128, NCH, 4, 64]); tk_cen = tk_t1
    ynbd = tkR([128, NCH, 4, 128], BF16); tk_ynbd = sc.tk("ynbd")
    ynT = tkR.at(o_t2, [128, 4, TT_]); tk_ynT = tk_t2
    orw = tkR([128, 4, TT_], BF16); tk_orw = sc.tk("orw")
    st1 = tkR([128, 16]); st2 = tkR([128, 16]); pcb = tkR([128, 16]); tk_st = sc.tk("st")
    tk_pc = sc.tk("pc")
    mask01 = tkR([128, 4 * TT_]); tk_m01 = sc.tk("m01")
    rwm = tkR([128, 4, 128]); tk_rwm = sc.tk("rwm")
    mAf = tkR([128, 256]); mAb = tkR([128, 256]); tk_mA = sc.tk("mA")
    bones = tkR([128, 128], BF16); tk_bones = sc.tk("bones")
    wupf = tkR.at(o_cs, [64, 2, 512]); aupf = tkR.at(o_kd, [64, 2, 512]); gupf = tkR.at(o_bb, [128, 512])
    wupb = tkR([64, 2, 512], BF16); aupb = tkR([64, 2, 512], BF16); gupb = tkR([128, 512], BF16)
    tk_wts = sc.tk("rwwts")
    tk_stage = [tk_cs, tk_kd, tk_bb]
    muA = tkR([128, 12]); muB = tkR([64, 4]); muC = tkR([128, 1])
    w0s = tkR([128, 8]); a0s = tkR([128, 8]); kks = tkR([128, 4]); kas = tkR([128, 4]); rks = tkR([128, 4])
    lws = tkR([128, 4]); lbs = tkR([128, 4])
    tk_par = sc.tk("rwpar")

    def bank2(i):
        return pall[:, i * 512:(i + 2) * 512]

    def seq(eng, fns, reads, writes):
        for fn in fns:
            sc.op(eng, fn, reads=list(reads) + list(writes), writes=writes)

    def bcast(ap2, shape):
        return ap2.unsqueeze(2).to_broadcast(shape)

    def rw_setup(l):
        def ldp(e):
            r = []
            nsc = dict(allow_slow_non_contiguous=True)
            r.append(e.dma_start(out=muA[:], in_=rwp_d["rw_mu"][l, 0:1536].rearrange("(c p) -> p c", p=128), **nsc))
            r.append(e.dma_start(out=muB[:], in_=rwp_d["rw_mu"][l, 1536:1792].rearrange("(c p) -> p c", p=64), **nsc))
            r.append(e.dma_start(out=muC[:], in_=rwp_d["rw_mu"][l, 1792:1920].rearrange("(c p) -> p c", p=128), **nsc))
            r.append(e.dma_start(out=w0s[:], in_=rwp_d["rw_w0"][l].rearrange("d (c p) -> p (d c)", p=128), **nsc))
            r.append(e.dma_start(out=a0s[:], in_=rwp_d["rw_a0"][l].rearrange("d (c p) -> p (d c)", p=128), **nsc))
            for dst, nm in ((kks, "rw_k_k"), (kas, "rw_k_a"), (rks, "rw_r_k"), (lws, "rw_lnx_w"), (lbs, "rw_lnx_b")):
                r.append(e.dma_start(out=dst[:], in_=rwp_d[nm][l].rearrange("(c p) -> p c", p=128), **nsc))
            return r
        sc.op("pool", ldp, writes=[tk_par], owner=tk_par, ndma=10)

        def ldw(e):
            r = [e.dma_start(out=wupf[:], in_=rwp_d["rw_w_up"][l].rearrange("d k n -> k d n")),
                 e.dma_start(out=aupf[:], in_=rwp_d["rw_a_up"][l].rearrange("d k n -> k d n")),
                 e.dma_start(out=gupf[:], in_=rwp_d["rw_g_up"][l]),
                 e.dma_start(out=rwm[:], in_=rwmask_d)]
            return r
        sc.op("pool", ldw, writes=[tk_wts, tk_rwm] + tk_stage, owner=tk_wts, ndma=4)

        def cvt(e):
            e.tensor_copy(wupb[:], wupf[:])
            e.tensor_copy(aupb[:], aupf[:])
            return e.tensor_copy(gupb[:], gupf[:])
        sc.op("dve", cvt, reads=[tk_wts] + tk_stage, writes=[tk_wts])

        def mkmasks(e):
            e.tensor_copy(mAf[:, 0:128], rwm[:, 1, :])
            e.tensor_copy(mAf[:, 128:256], rwm[:, 3, :])
            e.tensor_copy(mAb[:, 0:128], rwm[:, 0, :])
            return e.tensor_copy(mAb[:, 128:256], rwm[:, 2, :])
        sc.op("dve", mkmasks, reads=[tk_rwm], writes=[tk_mA])

        seq("pool", [lambda e: e.memset(mask01[:], 1.0),
                     lambda e: e.memset(mask01[:].rearrange("p (a b) -> p a b", b=64)[:, :, 0:1], 0.0)], [], [tk_m01])
        seq("pool", [lambda e: e.memset(bones[:], 0.0),
                     lambda e: e.memset(bones[0:64, 0:64], 1.0),
                     lambda e: e.memset(bones[64:128, 64:128], 1.0)], [], [tk_bones])
        for bdt, tkb in ((ARbd, tk_AR), (Btbd, tk_Bt), (Ktbd, tk_Kt), (Bhbd, tk_Bh), (Khbd, tk_Kh), (Vbd, tk_Vbd), (ynbd, tk_ynbd)):
            sc.op("pool", (lambda e, bdt=bdt: e.memset(bdt[:], 0.0)), writes=[tkb])
        sc.op("pool", lambda e: e.memset(Hs[:], 0.0), writes=[tk_H])

    def load_shift(src3, P, ncol, ub, tk_ub, mu, out_ap, tk_out, t0, eng="pool"):
        tlo = max(t0 - 1, 0)
        thi = min(t0 + TT_ + 1, T)
        off = tlo - (t0 - 1)
        n = thi - tlo

        def ld(e):
            return e.dma_start(out=ub[:, :, off:off + n], in_=src3[:, :, tlo:thi])
        sc.op("pool", ld, reads=[tk_projT], writes=[tk_ub], owner=tk_ub)
        if t0 == 0:
            sc.op(eng, lambda e: e.memset(ub[:, :, 0:1], 0.0), writes=[tk_ub])
        elif t0 % SEGLEN == 0:
            sc.op(eng, lambda e: e.tensor_scalar(ub[:, :, 0:1], ub[:, :, 0:1], flags[:P, 0:1], None, ALU.mult),
                  reads=[tk_flags], writes=[tk_ub])
        if t0 + TT_ == T:
            sc.op(eng, lambda e: e.memset(ub[:, :, TT_ + 1:TT_ + 2], 0.0), writes=[tk_ub])
        elif (t0 + TT_) % SEGLEN == 0:
            sc.op(eng, lambda e: e.tensor_scalar(ub[:, :, TT_ + 1:TT_ + 2], ub[:, :, TT_ + 1:TT_ + 2], flags[:P, 0:1], None, ALU.mult),
                  reads=[tk_flags], writes=[tk_ub])
        sh = shb[:P, :ncol, :]
        u1 = ub[:, :, 1:TT_ + 1]

        seq(eng, [lambda e: e.tensor_tensor(sh, ub[:, :, 0:TT_], ub[:, :, 2:TT_ + 2], ALU.add),
                  lambda e: e.tensor_scalar(sh, sh, 0.5, None, ALU.mult),
                  lambda e: e.tensor_tensor(sh, sh, u1, ALU.subtract),
                  lambda e: e.tensor_tensor(sh, sh, bcast(mu, [P, ncol, TT_]), ALU.mult),
                  lambda e: e.tensor_tensor(out_ap, sh, u1, ALU.add)], [tk_ub, tk_par], [tk_sh, tk_out])

    def bd_write(eng, dst4, col0, fn, reads, tk_dst):
        for hh in range(2):
            ov = dst4[hh * 64:(hh + 1) * 64, :, :, col0 + hh * 64:col0 + (hh + 1) * 64]
            sc.op(eng, (lambda e, ov=ov, hh=hh: fn(e, ov, hh)), reads=reads, writes=[tk_dst])

    def half4(ap3, hh):
        return ap3[hh * 64:(hh + 1) * 64].rearrange("p a (c q) -> p a c q", q=64)

    import os as _os
    RWS = float(_os.environ.get("RWS", "9"))

    def rw_shift(ti):
        t0 = ti * TT_
        for grp in range(3):
            load_shift(projT[4 * grp:4 * grp + 4].rearrange("c p t -> p c t"), 128, 4, uA[grp % 2], tk_uA[grp % 2],
                       muA[:, 4 * grp:4 * grp + 4], xs3[grp][:], tk_x[grp], t0)
        load_shift(projT[12:14].rearrange("c (two p) t -> p (c two) t", two=2), 64, 4, uB, tk_uB, muB[:], xB[:], tk_xB, t0)

    def rw_tile(l, d, ti, nxt):
        t0 = ti * TT_
        last = (d == 1)
        if RWS < 2:
            return
        sc.op("dve", lambda e: e.tensor_tensor(kkn[:], xk[:], bcast(kks[:], [128, 4, TT_]), ALU.mult),
              reads=[tk_x[1], tk_par], writes=[tk_kkn])
        sc.op("dve", lambda e: e.tensor_tensor(sqkb[:], kkn[:], kkn[:], ALU.mult), reads=[tk_kkn], writes=[tk_sqkb])

        def mmss(e):
            r = None
            for hp in range(4):
                r = e.matmul(bank2(0)[:, hp * TT_:(hp + 1) * TT_], bones[:], sqkb[:, hp, :], start=True, stop=True)
            return r
        sc.op("pe", mmss, reads=[tk_sqkb, tk_bones], writes=[tk_pb[0], tk_pb[1]])
        t1f = t1[:].rearrange("p a b -> p (a b)")
        sc.op("act", lambda e: e.activation(t1f, bank2(0), AF.Sqrt), reads=[tk_pb[0], tk_pb[1]], writes=[tk_t1])

        seq("dve", [lambda e: e.tensor_scalar(t1f, t1f, 1e-12, None, ALU.max),
                    lambda e: e.reciprocal(t1f, t1f),
                    lambda e: e.tensor_tensor(kkn[:], kkn[:], t1[:], ALU.mult)], [], [tk_t1, tk_kkn])
        if RWS < 3:
            return
        sc.op("act", lambda e: e.activation(twb[:], xB[:, d, :], AF.Tanh), reads=[tk_xB], writes=[tk_twb])
        sc.op("act", lambda e: e.activation(alb[:], xB[:, 2:4, :], AF.Copy), reads=[tk_xB], writes=[tk_alb])

        def mmz(e):
            r = None
            for hp in range(4):
                r = e.matmul(bank2(2)[:, hp * TT_:(hp + 1) * TT_], wupb[:, d, hp * 128:(hp + 1) * 128], twb[:], start=True, stop=True)
            return r
        sc.op("pe", mmz, reads=[tk_twb, tk_wts], writes=[tk_pb[2], tk_pb[3]])

        def mma(dd, bk):
            def f(e):
                r = None
                for hp in range(4):
                    r = e.matmul(bank2(bk)[:, hp * TT_:(hp + 1) * TT_], aupb[:, dd, hp * 128:(hp + 1) * 128], alb[:, dd, :], start=True, stop=True)
                return r
            return f
        sc.op("pe", mma(d, 4), reads=[tk_alb, tk_wts], writes=[tk_pb[4], tk_pb[5]])

        def sigz(e):
            r = None
            for hp in range(4):
                r = e.activation(sig[:, hp, :], bank2(2)[:, hp * TT_:(hp + 1) * TT_], AF.Sigmoid, bias=w0s[:, d * 4 + hp:d * 4 + hp + 1])
            return r
        sc.op("act", sigz, reads=[tk_pb[2], tk_pb[3], tk_par], writes=[tk_sig])

        def siga(dst, dd, bk):
            def f(e):
                r = None
                for hp in range(4):
                    r = e.activation(dst[:, hp, :], bank2(bk)[:, hp * TT_:(hp + 1) * TT_], AF.Sigmoid, bias=a0s[:, dd * 4 + hp:dd * 4 + hp + 1])
                return r
            return f
        sc.op("act", siga(asg, d, 4), reads=[tk_pb[4], tk_pb[5], tk_par], writes=[tk_asg])
        sc.op("dve", lambda e: e.tensor_tensor(bb[:], kkn[:], asg[:], ALU.mult), reads=[tk_kkn, tk_asg], writes=[tk_bb])

        seq("dve", [lambda e: e.scalar_tensor_tensor(out=kd[:], in0=asg[:], scalar=-1.0, in1=bcast(kas[:], [128, 4, TT_]), op0=ALU.add, op1=ALU.mult),
                    lambda e: e.scalar_tensor_tensor(out=kd[:], in0=kd[:], scalar=1.0, in1=xk[:], op0=ALU.add, op1=ALU.mult)],
            [tk_asg, tk_par, tk_x[1]], [tk_kd])
        if RWS < 4:
            return
        csf = cs[:].rearrange("p a b -> p (a b)")
        sc.op("dve", lambda e: e.tensor_tensor_scan(csf, mask01[:], sig[:].rearrange("p a b -> p (a b)"), 0.0, ALU.mult, ALU.add),
              reads=[tk_sig, tk_m01], writes=[tk_cs])
        cs16 = cs[:].rearrange("p a (c q) -> p (a c) q", q=64)
        totb = cs16[:, :, 63:64].to_broadcast([128, 16, 64])
        as16 = lambda ap: ap[:].rearrange("p a (c q) -> p (a c) q", q=64)
        sc.op("act", lambda e: e.activation(pcb[:], cs16[:, :, 63], AF.Exp, scale=-SC), reads=[tk_cs], writes=[tk_pc])
        if d == 0:
            inc, tk_inc = cs, tk_cs
            sc.op("dve", lambda e: e.tensor_tensor(t1[:], cs[:], sig[:], ALU.subtract), reads=[tk_cs, tk_sig], writes=[tk_t1])
            exc, tk_exc = t1, tk_t1
            sc.op("dve", lambda e: e.tensor_tensor(as16(t2), totb, cs16, ALU.subtract), reads=[tk_cs], writes=[tk_t2])
            rem, tk_rem = t2, tk_t2
        else:
            sc.op("dve", lambda e: e.tensor_tensor(as16(t2), totb, cs16, ALU.subtract), reads=[tk_cs], writes=[tk_t2])
            exc, tk_exc = t2, tk_t2
            sc.op("dve", lambda e: e.tensor_tensor(t1[:], t2[:], sig[:], ALU.add), reads=[tk_t2, tk_sig], writes=[tk_t1])
            inc, tk_inc = t1, tk_t1
            sc.op("dve", lambda e: e.tensor_tensor(sig[:], cs[:], sig[:], ALU.subtract), reads=[tk_cs, tk_sig], writes=[tk_sig])
            rem, tk_rem = sig, tk_sig
        if RWS < 4.2:
            return
        sc.op("act", lambda e: e.activation(e0[:], inc[:], AF.Exp, scale=-SC), reads=[tk_inc], writes=[tk_e0])
        bd_write("dve", ARbd, 128, lambda e, ov, hh: e.tensor_tensor(ov, half4(xr, hh), half4(e0, hh), ALU.mult),
                 [tk_x[0], tk_e0], tk_AR)
        if RWS < 4.4:
            return
        sc.op("act", lambda e: e.activation(e1[:], inc[:], AF.Exp, scale=SC), reads=[tk_inc], writes=[tk_e1])
        bd_write("dve", Btbd, 0, lambda e, ov, hh: e.tensor_tensor(ov, half4(bb, hh), half4(e1, hh), ALU.mult),
                 [tk_bb, tk_e1], tk_Bt)
        bd_write("dve", Ktbd, 0, lambda e, ov, hh: e.tensor_tensor(ov, half4(kd, hh), half4(e1, hh), ALU.mult),
                 [tk_kd, tk_e1], tk_Kt)
        if RWS < 4.6:
            return
        sc.op("act", lambda e: e.activation(e0[:], exc[:], AF.Exp, scale=-SC), reads=[tk_exc], writes=[tk_e0])
        bd_write("dve", ARbd, 0, lambda e, ov, hh: e.scalar_tensor_tensor(out=ov, in0=half4(kkn, hh), scalar=-1.0, in1=half4(e0, hh),
                                                                          op0=ALU.mult, op1=ALU.mult),
                 [tk_kkn, tk_e0], tk_AR)
        sc.op("act", lambda e: e.activation(e1[:], rem[:], AF.Exp, scale=-SC), reads=[tk_rem], writes=[tk_e1])
        bd_write("dve", Bhbd, 0, lambda e, ov, hh: e.tensor_tensor(ov, half4(bb, hh), half4(e1, hh), ALU.mult),
                 [tk_bb, tk_e1], tk_Bh)
        bd_write("dve", Khbd, 0, lambda e, ov, hh: e.tensor_tensor(ov, half4(kd, hh), half4(e1, hh), ALU.mult),
                 [tk_kd, tk_e1], tk_Kh)
        bd_write("act", Vbd, 0, lambda e, ov, hh: e.activation(ov, half4(xv, hh), AF.Copy), [tk_x[2]], tk_Vbd)
        if last:
            sc.op("pe", mma(0, 6), reads=[tk_alb, tk_wts], writes=[tk_pb[6], tk_pb[7]])
            sc.op("act", siga(asf, 0, 6), reads=[tk_pb[6], tk_pb[7], tk_par], writes=[tk_asf])

            seq("dve", [lambda e: e.tensor_tensor(asf[:], asf[:], asg[:], ALU.add),
                        lambda e: e.scalar_tensor_tensor(out=asf[:], in0=asf[:], scalar=-2.0, in1=bcast(kas[:], [128, 4, TT_]), op0=ALU.add, op1=ALU.mult),
                        lambda e: e.scalar_tensor_tensor(out=asf[:], in0=asf[:], scalar=2.0, in1=xk[:], op0=ALU.add, op1=ALU.mult),
                        lambda e: e.tensor_tensor(asf[:], asf[:], xr[:], ALU.mult),
                        lambda e: e.tensor_tensor(sqkb[:], asf[:], bcast(rks[:], [128, 4, TT_]), ALU.mult)],
                [tk_asg, tk_par, tk_x[0], tk_x[1]], [tk_asf, tk_sqkb])

            def mmbd(e):
                r = None
                for hp in range(4):
                    r = e.matmul(bank2(6)[:, hp * TT_:(hp + 1) * TT_], bones[:], sqkb[:, hp, :], start=True, stop=True)
                return r
            sc.op("pe", mmbd, reads=[tk_sqkb, tk_bones], writes=[tk_pb[6], tk_pb[7]])
            sc.op("dve", lambda e: e.tensor_tensor(bonus[:].rearrange("p a b -> p (a b)"), bank2(6), xv[:].rearrange("p a b -> p (a b)"), ALU.mult),
                  reads=[tk_pb[6], tk_pb[7], tk_x[2]], writes=[tk_bonus])
            load_shift(projT[14:15].rearrange("c p t -> p c t"), 128, 1, uC, tk_uC, muC[:], xCf[:], tk_xCf, t0)
            sc.op("act", lambda e: e.activation(xCb[:], xCf[:, 0, :], AF.Sigmoid), reads=[tk_xCf], writes=[tk_xCb])

            def mmg(e):
                r = None
                for hp in range(4):
                    r = e.matmul(bank2(6)[:, hp * TT_:(hp + 1) * TT_], gupb[:, hp * 128:(hp + 1) * 128], xCb[:], start=True, stop=True)
                return r
            sc.op("pe", mmg, reads=[tk_xCb, tk_wts], writes=[tk_pb[6], tk_pb[7]])
            sc.op("act", lambda e: e.activation(gbuf[:].rearrange("p a b -> p (a b)"), bank2(6), AF.Copy),
                  reads=[tk_pb[6], tk_pb[7]], writes=[tk_g])
            sc.op("pool", (lambda e: e.dma_start(out=Yf[:], in_=yf_d[:, ti * NCH:(ti + 1) * NCH, :, :])),
                  reads=[tk_yfd], writes=[tk_Yf], owner=tk_Yf)
        if RWS < 5:
            return
        bnd = (t0 > 0 and t0 % SEGLEN == 0) if d == 0 else (t0 + TT_ < T and (t0 + TT_) % SEGLEN == 0)
        if bnd:
            sc.op("dve", lambda e: e.tensor_scalar(Hs[:], Hs[:], flags[:, 0:1], None, ALU.mult), reads=[tk_flags], writes=[tk_H])
        first_tile = (ti == 0) if d == 0 else (ti == NTR - 1)
        if bnd or first_tile:
            sc.op("act", lambda e: e.activation(Hb[:], Hs[:], AF.Copy), reads=[tk_H], writes=[tk_Hb])
        if nxt is not None:
            rw_shift(nxt)
        mA = mAf if d == 0 else mAb
        mB = rwm[:, 0, :] if d == 0 else rwm[:, 1, :]
        mA4 = mA[:].unsqueeze(1).to_broadcast([128, 4, 256])
        mB4 = mB.unsqueeze(1).to_broadcast([128, 4, 128])
        idb4 = identb[:].unsqueeze(1).to_broadcast([128, 4, 128])
        chs = list(range(NCH)) if d == 0 else list(range(NCH - 1, -1, -1))
        fl = lambda ap3: ap3[:].rearrange("p a b -> p (a b)")
        v4 = lambda ap2: ap2.rearrange("p (a b) -> p a b", a=4)
        for ch in chs:
            def prods(e, ch=ch):
                r = None
                for hp in range(4):
                    e.matmul(bank2(0)[:, hp * 256:(hp + 1) * 256], Btbd[:, hp, ch, :], ARbd[:, hp, ch, :], start=True, stop=True)
                    e.matmul(bank2(2)[:, hp * 256:(hp + 1) * 256], Ktbd[:, hp, ch, :], ARbd[:, hp, ch, :], start=True, stop=True)
                    r = e.matmul(pb[4][:, hp * 128:(hp + 1) * 128], ARbd[:, hp, ch, 0:128], Btbd[:, hp, ch, :], start=True, stop=True)
                return r
            sc.op("pe", prods, reads=[tk_AR, tk_Bt, tk_Kt], writes=[tk_pb[0], tk_pb[1], tk_pb[2], tk_pb[3], tk_pb[4]])
            sc.op("dve", (lambda e, ch=ch: e.tensor_tensor(M1s[ch][:], v4(bank2(0)), mA4, ALU.mult)),
                  reads=[tk_pb[0], tk_pb[1], tk_mA], writes=[tk_M1s[ch]])
            sc.op("dve", (lambda e, ch=ch: e.tensor_tensor(M2s[ch][:], v4(bank2(2)), mA4, ALU.mult)),
                  reads=[tk_pb[2], tk_pb[3], tk_mA], writes=[tk_M2s[ch]])
            sc.op("dve", (lambda e, ch=ch: e.tensor_tensor(Sb[ch][:], v4(pb[4]), mB4, ALU.mult)),
                  reads=[tk_pb[4], tk_rwm], writes=[tk_Sb[ch]])
        for lev in range(1, 6):
            for i, ch in enumerate(chs):
                STc = M1s[ch][:, :, 0:128] if lev == 1 else STb[ch]
                tkST = tk_M1s[ch] if lev == 1 else tk_STb[ch]

                def sq1(e, ch=ch, i=i, STc=STc):
                    r = None
                    for hp in range(4):
                        r = e.matmul(pb[2 * i][:, hp * 128:(hp + 1) * 128], STc[:, hp, :], Sb[ch][:, hp, :], start=True, stop=True)
                    return r
                sc.op("pe", sq1, reads=[tkST, tk_Sb[ch]], writes=[tk_pb[2 * i]])
                if lev < 5:
                    def sq2(e, ch=ch, i=i, STc=STc):
                        r = None
                        for hp in range(4):
                            r = e.matmul(pb[2 * i + 1][:, hp * 128:(hp + 1) * 128], Sb[ch][:, hp, :], STc[:, hp, :], start=True, stop=True)
                        return r
                    sc.op("pe", sq2, reads=[tkST, tk_Sb[ch]], writes=[tk_pb[2 * i + 1]])
            for i, ch in enumerate(chs):
                sc.op("act", (lambda e, ch=ch, i=i: e.activation(fl(Sb[ch]), pb[2 * i], AF.Copy)), reads=[tk_pb[2 * i]], writes=[tk_Sb[ch]])
                if lev < 5:
                    sc.op("dve", (lambda e, ch=ch, i=i: e.tensor_copy(fl(STb[ch]), pb[2 * i + 1])), reads=[tk_pb[2 * i + 1]], writes=[tk_STb[ch]])
            for i, ch in enumerate(chs):
                def ttu(e, ch=ch, i=i):
                    r = None
                    for hp in range(4):
                        o = pb[i][:, hp * 128:(hp + 1) * 128]
                        e.matmul(o, Sb[ch][:, hp, :], identb[:], start=True, stop=False)
                        r = e.matmul(o, Sb[ch][:, hp, :], M1s[ch][:, hp, 0:128], start=False, stop=True)
                    return r
                sc.op("pe", ttu, reads=[tk_Sb[ch], tk_M1s[ch], tk_identb], writes=[tk_pb[i]])
            for i, ch in enumerate(chs):
                sc.op("dve", (lambda e, ch=ch, i=i: e.tensor_tensor(M1s[ch][:, :, 0:128], M1s[ch][:, :, 0:128], v4(pb[i]), ALU.add)),
                      reads=[tk_pb[i]], writes=[tk_M1s[ch]])

        def emit_tm(ch, k):
            def trs(e):
                r = None
                for hp in range(4):
                    e.matmul(pb[5][:, hp * 128:(hp + 1) * 128], Vbd[:, hp, ch, :], identb[:], start=True, stop=True)
                    e.matmul(pb[6][:, hp * 128:(hp + 1) * 128], Bhbd[:, hp, ch, :], identb[:], start=True, stop=True)
                    r = e.matmul(pb[7][:, hp * 128:(hp + 1) * 128], Khbd[:, hp, ch, :], identb[:], start=True, stop=True)
                return r
            sc.op("pe", trs, reads=[tk_Vbd, tk_Bh, tk_Kh, tk_identb], writes=[tk_pb[5], tk_pb[6], tk_pb[7]])
            sc.op("act", lambda e: e.activation(fl(Vtms[k]), pb[5], AF.Copy), reads=[tk_pb[5]], writes=[tk_Vtms[k]])
            sc.op("dve", lambda e: e.tensor_copy(fl(Bhtms[k]), pb[6]), reads=[tk_pb[6]], writes=[tk_Bhtms[k]])
            sc.op("act", lambda e: e.activation(fl(Khtms[k]), pb[7], AF.Copy), reads=[tk_pb[7]], writes=[tk_Khtms[k]])

        emit_tm(chs[0], 0)
        for idx, ch in enumerate(chs):
            k = idx % 2
            if idx + 1 < NCH:
                emit_tm(chs[idx + 1], (idx + 1) % 2)
            M1, tk_M1, M2, tk_M2 = M1s[ch], tk_M1s[ch], M2s[ch], tk_M2s[ch]
            Vtm, tk_Vtm, Bhtm, tk_Bhtm, Khtm, tk_Khtm = Vtms[k], tk_Vtms[k], Bhtms[k], tk_Bhtms[k], Khtms[k], tk_Khtms[k]
            TTf, tk_TTf = TTb[ch], tk_TTb[ch]

            def mmw(e, ch=ch, M2=M2, Vtm=Vtm):
                r = None
                for hp in range(4):
                    e.matmul(pb[0][:, hp * 128:(hp + 1) * 128], ARbd[:, hp, ch, 0:128], Hb[:, hp, :], start=True, stop=False)
                    r = e.matmul(pb[0][:, hp * 128:(hp + 1) * 128], M2[:, hp, 0:128], Vtm[:, hp, :], start=False, stop=True)
                return r
            sc.op("pe", mmw, reads=[tk_AR, tk_Hb, tk_M2, tk_Vtm], writes=[tk_pb[0]])
            sc.op("act", lambda e: e.activation(fl(Wb), pb[0], AF.Copy), reads=[tk_pb[0]], writes=[tk_Wb])

            def mmu(e, M1=M1):
                r = None
                for hp in range(4):
                    o = pb[1][:, hp * 128:(hp + 1) * 128]
                    e.matmul(o, identb[:], Wb[:, hp, :], start=True, stop=False)
                    r = e.matmul(o, M1[:, hp, 0:128], Wb[:, hp, :], start=False, stop=True)
                return r
            sc.op("pe", mmu, reads=[tk_M1, tk_Wb, tk_identb], writes=[tk_pb[1]])
            sc.op("act", lambda e: e.activation(fl(Ub), pb[1], AF.Copy), reads=[tk_pb[1]], writes=[tk_Ub])

            def mmh(e, Bhtm=Bhtm, Khtm=Khtm, Vtm=Vtm):
                r = None
                for hp in range(4):
                    o = pb[3][:, hp * 128:(hp + 1) * 128]
                    e.matmul(o, Bhtm[:, hp, :], Ub[:, hp, :], start=True, stop=False)
                    r = e.matmul(o, Khtm[:, hp, :], Vtm[:, hp, :], start=False, stop=True)
                return r
            sc.op("pe", mmh, reads=[tk_Bhtm, tk_Ub, tk_Khtm, tk_Vtm], writes=[tk_pb[3]])

            def mmy(e, ch=ch, M1=M1, M2=M2, Vtm=Vtm):
                r = None
                for hp in range(4):
                    o = pb[2][:, hp * 128:(hp + 1) * 128]
                    e.matmul(o, ARbd[:, hp, ch, 128:256], Hb[:, hp, :], start=True, stop=False)
                    e.matmul(o, M1[:, hp, 128:256], Ub[:, hp, :], start=False, stop=False)
                    r = e.matmul(o, M2[:, hp, 128:256], Vtm[:, hp, :], start=False, stop=True)
                return r
            sc.op("pe", mmy, reads=[tk_AR, tk_Hb, tk_M1, tk_Ub, tk_M2, tk_Vtm], writes=[tk_pb[2]])
            pcv = pcb[:].rearrange("p (a c) -> p a c", c=NCH)[:, :, ch:ch + 1].to_broadcast([128, 4, 128])
            seq("dve", [(lambda e, pcv=pcv: e.tensor_tensor(Hs[:], Hs[:], pcv, ALU.mult)),
                        lambda e: e.tensor_tensor(Hs[:], Hs[:], v4(pb[3]), ALU.add)],
                [tk_pb[3], tk_pc], [tk_H])
            sc.op("act", lambda e: e.activation(Hb[:], Hs[:], AF.Copy), reads=[tk_H], writes=[tk_Hb])
            for hh in range(2):
                src = v4(pb[2][hh * 64:(hh + 1) * 64, :])[:, :, hh * 64:(hh + 1) * 64]
                dst = Yt[hh * 64:(hh + 1) * 64, ch, :, :]
                if last:
                    yfv = Yf[hh * 64:(hh + 1) * 64, ch, :, :]
                    sc.op("dve", (lambda e, src=src, dst=dst, yfv=yfv: e.tensor_tensor(dst, src, yfv, ALU.add)),
                          reads=[tk_pb[2], tk_Yf], writes=[tk_Yt])
                else:
                    sc.op("act", (lambda e, src=src, dst=dst: e.activation(dst, src, AF.Copy)), reads=[tk_pb[2]], writes=[tk_Yt])
        if RWS < 9:
            return
        if not last:
            sc.op("pool", (lambda e: e.dma_start(out=yf_d[:, ti * NCH:(ti + 1) * NCH, :, :], in_=Yt[:])),
                  reads=[tk_Yt], writes=[tk_yfd], owner=tk_Yt)
            return
        Y16 = Yt[:].rearrange("p c a v -> p (c a) v")
        cen16 = cen[:].rearrange("p c a v -> p (c a) v")

        seq("dve", [lambda e: e.tensor_reduce(st1[:], Y16, AX.X, ALU.add),
                    lambda e: e.tensor_scalar(st1[:], st1[:], -1.0 / 64, None, ALU.mult),
                    lambda e: e.tensor_tensor(cen16, Y16, st1[:].unsqueeze(2).to_broadcast([128, 16, 64]), ALU.add),
                    lambda e: e.tensor_tensor(Y16, cen16, cen16, ALU.mult),
                    lambda e: e.tensor_reduce(st2[:], Y16, AX.X, ALU.add)], [], [tk_Yt, tk_cen, tk_st])
        sc.op("act", lambda e: e.activation(st2[:], st2[:], AF.Sqrt, bias=64e-5, scale=1.0 / 64), reads=[tk_st], writes=[tk_st])
        sc.op("dve", lambda e: e.reciprocal(st2[:], st2[:]), reads=[tk_st], writes=[tk_st])
        for hh in range(2):
            ov = ynbd[hh * 64:(hh + 1) * 64, :, :, hh * 64:(hh + 1) * 64]
            iv = cen[hh * 64:(hh + 1) * 64]
            rv = st2[hh * 64:(hh + 1) * 64, :].rearrange("p (c a) -> p c a", a=4).unsqueeze(3).to_broadcast([64, NCH, 4, 64])
            sc.op("dve", (lambda e, ov=ov, iv=iv, rv=rv: e.tensor_tensor(ov, iv, rv, ALU.mult)), reads=[tk_cen, tk_st], writes=[tk_ynbd])
        for ch in range(NCH):
            def try_(e, ch=ch):
                r = None
                for hp in range(4):
                    r = e.matmul(pb[0][:, hp * 128:(hp + 1) * 128], ynbd[:, ch, hp, :], identb[:], start=True, stop=True)
                return r
            sc.op("pe", try_, reads=[tk_ynbd, tk_identb], writes=[tk_pb[0]])
            for hh in range(2):
                src = pb[0][hh * 64:(hh + 1) * 64, :].rearrange("p (a b) -> p a b", a=4)[:, :, hh * 64:(hh + 1) * 64]
                dst = ynT[hh * 64:(hh + 1) * 64, :, ch * 64:(ch + 1) * 64]
                if hh == 0:
                    sc.op("act", (lambda e, src=src, dst=dst: e.activation(dst, src, AF.Copy)), reads=[tk_pb[0]], writes=[tk_ynT])
                else:
                    sc.op("dve", (lambda e, src=src, dst=dst: e.tensor_copy(dst, src)), reads=[tk_pb[0]], writes=[tk_ynT])

        seq("dve", [lambda e: e.tensor_tensor(ynT[:], ynT[:], bcast(lws[:], [128, 4, TT_]), ALU.mult),
                    lambda e: e.tensor_tensor(ynT[:], ynT[:], bcast(lbs[:], [128, 4, TT_]), ALU.add),
                    lambda e: e.tensor_tensor(ynT[:], ynT[:], bonus[:], ALU.add),
                    lambda e: e.tensor_tensor(orw[:], ynT[:], gbuf[:], ALU.mult)], [tk_par, tk_bonus, tk_g], [tk_ynT, tk_orw])
        sc.op("pool", (lambda e: e.dma_start(out=yrwT[:, :, t0:t0 + TT_], in_=orw[:])), reads=[tk_orw], writes=[tk_yrwT], owner=tk_orw)

    def rw_phase(l):
        rw_setup(l)
        rw_shift(0)
        for ti in range(NTR):
            rw_tile(l, 0, ti, ti + 1 if ti + 1 < NTR else NTR - 1)
        sc.op("pool", lambda e: e.memset(Hs[:], 0.0), writes=[tk_H])
        for ti in range(NTR - 1, -1, -1):
            rw_tile(l, 1, ti, ti - 1 if ti > 0 else None)


    def mixers(l):
        sc.barrier()
        if mode in ("full", "na"):
            na_phase(l)
        else:
            zero_fill(ynaT, tk_ynaT, 64)
        sc.barrier()
        if mode in ("full", "rw"):
            rw_phase(l)
        else:
            zero_fill(yrwT, tk_yrwT, 128)
        sc.barrier()

    for t in range(NT):
        load_x_tile(t)
        phase_a(0, t)
    for l in range(LRUN):
        if do_mix:
            mixers(l)
        for t in range(NT):
            phase_c(l, t)
            if l + 1 < LRUN:
                phase_a(l + 1, t)
            else:
                store_out_tile(t)
    sc.finish("pool")
    sc.emit()
    return nc, sc


def arrange(W, kp):
    Kd, Nd = W.shape
    KC = Kd // kp
    NJ = Nd // 128
    return np.ascontiguousarray(W.reshape(KC, kp, NJ, 128).transpose(2, 1, 0, 3)).reshape(NJ, kp, KC * 128)


def host_weights(inp):
    out = {}

    def st(fn):
        return np.stack([fn(l) for l in range(L)], 0)
    out["w_wg1"] = st(lambda l: arrange(inp["ffn1_wg"][l], 128))
    out["w_wu1"] = st(lambda l: arrange(inp["ffn1_wu"][l], 128))
    out["w_wd1"] = st(lambda l: arrange(inp["ffn1_wd"][l], 128))
    out["w_win"] = st(lambda l: arrange(inp["w_in"][l], 128))
    out["w_wona"] = st(lambda l: arrange(inp["w_out"][l][:512], 64))
    out["w_worw"] = st(lambda l: arrange(inp["w_out"][l][512:], 128))
    out["w_wg2"] = st(lambda l: arrange(inp["ffn2_wg"][l], 128))
    out["w_wu2"] = st(lambda l: arrange(inp["ffn2_wu"][l], 128))
    out["w_wd2"] = st(lambda l: arrange(inp["ffn2_wd"][l], 128))
    out["w_pgate"] = st(lambda l: arrange(inp["ple_gate"][l], 128))
    out["w_pup"] = st(lambda l: arrange(inp["ple_up"][l], 128))
    for nm in NORMS:
        out[nm] = np.ascontiguousarray(inp[nm], dtype=np.float32)
    out["final_norm"] = np.ascontiguousarray(inp["final_norm"], dtype=np.float32)
    return out


def host_extra(inp):
    rpb = np.asarray(inp["na_rpb"], dtype=np.float32)
    tb = np.full((L, 2, 64, 8, 14, 64), NEG, np.float32)
    j = np.arange(64)
    cs = np.clip(j - 8, 0, 48)
    for par in range(2):
        for m in range(14):
            if m + par > 14:
                continue
            for cp in range(64):
                ok = (cp >= cs) & (cp < cs + 16)
                jj = j[ok]
                tb[:, par, cp, :, m, jj] = np.transpose(rpb[:, :, m + par, cp - jj + 15], (2, 0, 1))
    out = {"tb": np.ascontiguousarray(tb.reshape(L, 128, 8 * 14 * 64))}
    p = np.arange(128)[:, None]
    f = np.arange(128)[None, :]
    same = (p // 64) == (f // 64)
    pl, fl = p % 64, f % 64
    rwm = np.stack([same & (fl < pl), same & (fl > pl), same & (fl <= pl), same & (fl >= pl)], 1).astype(np.float32)
    out["rwmask"] = np.ascontiguousarray(rwm)
    for nm in ("rw_mu", "rw_w0", "rw_w_up", "rw_a0", "rw_a_up", "rw_g_up", "rw_k_k", "rw_k_a", "rw_lnx_w", "rw_lnx_b"):
        out[nm] = np.ascontiguousarray(inp[nm], dtype=np.float32)
    out["rw_r_k"] = np.ascontiguousarray(inp["rw_r_k"], dtype=np.float32).reshape(L, 512)
    return out


def kernel(**inputs):
    NSEG, SEGLEN = 4, 4096
    TC = NSEG * SEGLEN
    inp = {k: np.asarray(v) for k, v in inputs.items()}
    nc, sc = build_program(NSEG, SEGLEN, "full")
    hw = host_weights(inp)
    hw.update(host_extra(inp))
    xp = np.asarray(inp["x_prompt"], dtype=np.float32)
    xs = np.asarray(inp["x_sample"], dtype=np.float32)
    pp = np.asarray(inp["p_prompt"], dtype=np.float32)
    ps = np.asarray(inp["p_sample"], dtype=np.float32)
    zx = np.zeros((TC, D), np.float32)
    zp = np.zeros((L, TC, PLE), np.float32)
    in_maps = []
    for c in range(8):
        m = dict(hw)
        fl = np.zeros((128, 2), np.float32)
        if c < 4:
            m["xin"] = np.ascontiguousarray(xp[4 * c:4 * c + 4].reshape(TC, D))
            m["pin"] = np.ascontiguousarray(pp[:, 4 * c:4 * c + 4].reshape(L, TC, PLE))
            fl[:, 1] = 1.0
        elif c < 6:
            m["xin"] = np.ascontiguousarray(xs[c - 4])
            m["pin"] = np.ascontiguousarray(ps[:, c - 4])
            fl[:, 0] = 1.0
        else:
            m["xin"] = zx
            m["pin"] = zp
            fl[:, 1] = 1.0
        m["flags"] = fl
        in_maps.append(m)
    res = run_bass_kernel_spmd(nc, in_maps, core_ids=list(range(8)))
    outs = [np.asarray(r["yout"], dtype=np.float32) for r in res.results]
    y_prompt = np.concatenate([outs[c].reshape(4, SEGLEN, D) for c in range(4)], axis=0)
    y_sample = np.stack([outs[4], outs[5]], axis=0)
    return (y_prompt, y_sample)
```

```python
import numpy as np
import concourse.bass as bass
import concourse.mybir as mybir
from concourse.bass_utils import run_bass_kernel_spmd

F32 = mybir.dt.float32
BF16 = mybir.dt.bfloat16
AF = mybir.ActivationFunctionType
ALU = mybir.AluOpType
AX = mybir.AxisListType

D = 1024
DFF = 2816
NJF = DFF // 128
PW = 3456
NJP = PW // 128
PLE = 256
L = 2
NEG = -30000.0
ENGS = ("pe", "act", "dve", "pool", "sp")


class Tk:
    __slots__ = ("name", "w", "r", "sem", "semval", "last_dma")

    def __init__(self, name):
        self.name = name
        self.w = None
        self.r = {}
        self.sem = None
        self.semval = 0
        self.last_dma = None


class Sched:
    def __init__(self, nc):
        self.nc = nc
        self.q = {e: [] for e in ENGS}
        self.esem = {e: nc.alloc_semaphore("es_" + e) for e in ENGS}
        self.ecount = {e: 0 for e in ENGS}
        self.waited = {e: {} for e in ENGS}
        self.nops = 0
        self.nsem = 0
        self.owners = []

    def tk(self, name):
        return Tk(name)

    def _need(self, eng, ev):
        if ev is None:
            return
        sem, val = ev
        k = id(sem)
        if self.waited[eng].get(k, 0) < val:
            self.waited[eng][k] = val
            self.q[eng].append(("w", sem, val))

    def op(self, eng, fn, reads=(), writes=(), owner=None, ndma=1):
        for t in reads:
            self._need(eng, t.w)
        for t in writes:
            self._need(eng, t.w)
            for ev in t.r.values():
                self._need(eng, ev)
        if owner is not None:
            if owner.sem is None:
                owner.sem = self.nc.alloc_semaphore("ds%d" % self.nsem)
                self.nsem += 1
                self.owners.append(owner)
            self._need(eng, owner.last_dma)
            owner.semval += 16 * ndma
            ev = (owner.sem, owner.semval)
            owner.last_dma = ev
            self.q[eng].append(("d", fn, owner.sem))
        else:
            self.ecount[eng] += 1
            ev = (self.esem[eng], self.ecount[eng])
            self.q[eng].append(("o", fn, self.esem[eng]))
        for t in reads:
            t.r[id(ev[0])] = ev
        for t in writes:
            t.w = ev
            t.r = {}
        self.nops += 1
        return ev

    def barrier(self):
        evs = [(self.esem[e], self.ecount[e]) for e in ENGS if self.ecount[e]]
        evs += [o.last_dma for o in self.owners]
        for e in ENGS:
            for ev in evs:
                self._need(e, ev)

    def finish(self, eng="pool"):
        for e in ENGS:
            if self.ecount[e]:
                self._need(eng, (self.esem[e], self.ecount[e]))
        for o in self.owners:
            self._need(eng, o.last_dma)

    def emit(self):
        nc = self.nc
        q = self.q

        def run(e, name):
            for it in q[name]:
                if it[0] == "w":
                    e.wait_ge(it[1], it[2])
                elif it[0] == "o":
                    ins = it[1](e)
                    ins.then_inc(it[2], 1)
                else:
                    r = it[1](e)
                    if isinstance(r, (list, tuple)):
                        for ins in r:
                            ins.then_inc(it[2], 16)
                    else:
                        r.then_inc(it[2], 16)

        with nc.Block() as block:
            @block.tensor
            def _(e):
                run(e, "pe")

            @block.scalar
            def _(e):
                run(e, "act")

            @block.vector
            def _(e):
                run(e, "dve")

            @block.gpsimd
            def _(e):
                run(e, "pool")

            @block.sync
            def _(e):
                run(e, "sp")


WSPEC = {
    "wg1": (128, 8, NJF), "wu1": (128, 8, NJF), "wd1": (128, NJF, 8),
    "win": (128, 8, NJP), "wona": (64, 8, 8), "worw": (128, 4, 8),
    "wg2": (128, 8, NJF), "wu2": (128, 8, NJF), "wd2": (128, NJF, 8),
    "pgate": (128, 8, 8), "pup": (128, 2, 8),
}
NORMS = ("ffn1_norm", "mix_norm", "ffn2_norm", "ple_norm")


def build_program(NSEG, SEGLEN, mode="full"):
    do_mix = mode != "nomix"
    T = NSEG * SEGLEN
    NT = T // 512
    nc = bass.Bass("TRN2", target_bir_lowering=False)
    sc = Sched(nc)

    def dram_in(name, shape, dt=F32):
        return nc.dram_tensor(name, list(shape), dt, kind="ExternalInput").ap()

    def dram_tmp(name, shape, dt):
        return nc.dram_tensor(name, list(shape), dt, kind="Internal").ap()

    xin = dram_in("xin", [T, D])
    pin = dram_in("pin", [L, T, PLE])
    flags_d = dram_in("flags", [128, 2])
    yout = nc.dram_tensor("yout", [T, D], F32, kind="ExternalOutput").ap()
    wsrc = {}
    wbf = {}
    for nm, (kp, KC, NJ) in WSPEC.items():
        wsrc[nm] = dram_in("w_" + nm, [L, NJ, kp, KC * 128])
        wbf[nm] = dram_tmp("b_" + nm, [L, NJ, kp, KC * 128], BF16)
    norms_d = {nm: dram_in(nm, [L, D]) for nm in NORMS}
    fnorm_d = dram_in("final_norm", [D])
    h1T = dram_tmp("h1T", [8, 128, T], F32)
    projT = dram_tmp("projT", [15, 128, T], F32)
    qkvT = dram_tmp("qkvT", [12, 128, T], BF16)
    tk_qkvT = sc.tk("qkvT")
    ynaT = dram_tmp("ynaT", [64, 8, T], BF16)
    import os as _os0
    DBG = int(_os0.environ.get("DBG", "0"))
    LRUN = int(_os0.environ.get("LRUN", str(L)))
    if DBG:
        yrwT = nc.dram_tensor("yrwT", [128, 4, T], BF16, kind="ExternalOutput").ap()
    else:
        yrwT = dram_tmp("yrwT", [128, 4, T], BF16)
    tk_h1T = sc.tk("h1T")
    tk_projT = sc.tk("projT")
    tk_ynaT = sc.tk("ynaT")
    tk_yrwT = sc.tk("yrwT")

    def sb(name, shape, dt=F32):
        return nc.alloc_sbuf_tensor(name, list(shape), dt)

    ident = sb("ident", [128, 128]); tk_ident = sc.tk("ident")
    identb = sb("identb", [128, 128], BF16); tk_identb = sc.tk("identb")
    onesb = sb("onesb", [128, 128], BF16); tk_ones = sc.tk("ones")
    gains = sb("gains", [128, L * 4 + 1, 8]); tk_gains = sc.tk("gains")
    flags = sb("flags_s", [128, 2]); tk_flags = sc.tk("flags")
    NRING = 4
    RING_EL = NJF * 128
    ring_i = [0]
    ARENA = 165 * 1024
    arena = sb("arena", [128, ARENA // 4])

    def carver():
        off = [0]

        def view(o, shape, dt):
            esz = 4 if dt == F32 else 2
            n = 1
            for d_ in shape[1:]:
                n *= d_
            nbytes = (n * esz + 63) // 64 * 64
            assert o + nbytes <= ARENA, (o, nbytes)
            ap = arena[:shape[0], o // 4:(o + nbytes) // 4]
            if dt != F32:
                ap = ap.bitcast(dt)
            ap = ap[:, :n]
            if len(shape) == 3:
                ap = ap.rearrange("p (a b) -> p a b", a=shape[1])
            elif len(shape) == 4:
                ap = ap.rearrange("p (a b c) -> p a b c", a=shape[1], b=shape[2])
            return ap, nbytes

        def take(shape, dt=F32):
            ap, nbytes = view(off[0], shape, dt)
            take.last = off[0]
            off[0] += nbytes
            return ap

        def at(o, shape, dt=F32):
            return view(o, shape, dt)[0]
        take.at = at
        take.off = off
        return take

    tkW = carver()
    cst = [tkW([128, 4096]) for i in range(2)]; tk_cst = [sc.tk("cst%d" % i) for i in range(2)]
    cstb = [tkW([128, 4096], BF16) for i in range(2)]; tk_cstb = [sc.tk("cstb%d" % i) for i in range(2)]
    tkA = carver()
    h = tkA([128, 8, 512]); tk_h = [sc.tk("h%d" % c) for c in range(8)]
    nb = tkA([128, 8, 512], BF16); tk_n = sc.tk("n")
    sq = tkA([128, 8, 512], BF16); tk_sq = sc.tk("sq")
    act = tkA([128, NJF, 512], BF16); tk_act = [sc.tk("act%d" % j) for j in range(NJF)]
    rstd = tkA([128, 512]); tk_rstd = sc.tk("rstd")
    sg = [tkA([128, 512]) for i in range(2)]; tk_sg = [sc.tk("sg%d" % i) for i in range(2)]
    ev_f = [tkA([128, 512]) for i in range(3)]; tk_evf = [sc.tk("evf%d" % i) for i in range(3)]
    ev_b = [tkA([128, 512], BF16) for i in range(3)]; tk_evb = [sc.tk("evb%d" % i) for i in range(3)]
    tokbuf = tkA([128, 4, D]); tk_tok = sc.tk("tokbuf")
    pbuf = tkA([128, 4, PLE]); tk_pbuf = sc.tk("pbuf")
    pT = tkA([128, 2, 512], BF16); tk_pT = sc.tk("pT")
    ynat = tkA([64, 8, 512], BF16); tk_ynat = sc.tk("ynat")
    yrwt = tkA([128, 4, 512], BF16); tk_yrwt = sc.tk("yrwt")
    sqf = tkA([128, 8, 512]); tk_sqf = sc.tk("sqf")
    ring = [tkA([128, RING_EL], BF16) for i in range(NRING)]
    tk_ring = [sc.tk("ring%d" % i) for i in range(NRING)]

    pall = nc.alloc_psum_tensor("pall", [128, 4096], F32)
    pb = [pall[:, i * 512:(i + 1) * 512] for i in range(8)]
    tk_pb = [sc.tk("pb%d" % i) for i in range(8)]
    rr = [0]

    def next_bank():
        i = 1 + (rr[0] % 6)
        rr[0] += 1
        return pb[i], tk_pb[i]

    sc.op("pool", lambda e: e.memset(ident[:], 1.0), writes=[tk_ident])
    sc.op("pool", lambda e: e.affine_select(out=ident[:], in_=ident[:], pattern=[[-1, 128]],
                                            compare_op=ALU.is_equal, fill=0.0, base=0, channel_multiplier=1),
          reads=[tk_ident], writes=[tk_ident])
    sc.op("pool", lambda e: e.tensor_copy(identb[:], ident[:]), reads=[tk_ident], writes=[tk_identb])
    sc.op("pool", lambda e: e.memset(onesb[:], 1.0), writes=[tk_ones])
    sc.op("pool", lambda e: e.dma_start(out=flags[:], in_=flags_d), writes=[tk_flags], owner=tk_flags)

    def ld_gains(e):
        r = []
        for l in range(L):
            for i, nm in enumerate(NORMS):
                r.append(e.dma_start(out=gains[:, l * 4 + i, :],
                                     in_=norms_d[nm][l].rearrange("(c p) -> p c", p=128),
                                     allow_slow_non_contiguous=True))
        r.append(e.dma_start(out=gains[:, L * 4, :], in_=fnorm_d.rearrange("(c p) -> p c", p=128),
                             allow_slow_non_contiguous=True))
        return r
    sc.op("pool", ld_gains, writes=[tk_gains], owner=tk_gains, ndma=L * 4 + 1)

    cvt_i = [0]

    def convert(src_flat, dst_flat, nper):
        nblk = (nper + 4095) // 4096
        while nper % nblk:
            nblk += 1
        F = nper // nblk
        for b in range(nblk):
            i = cvt_i[0] % 2
            cvt_i[0] += 1
            s_ap = src_flat[:, b * F:(b + 1) * F]
            d_ap = dst_flat[:, b * F:(b + 1) * F]
            sc.op("sp", (lambda e, i=i, s_ap=s_ap, F=F: e.dma_start(out=cst[i][:, :F], in_=s_ap)),
                  writes=[tk_cst[i]], owner=tk_cst[i])
            eng = ("dve", "act")[i]
            if eng == "dve":
                sc.op("dve", (lambda e, i=i, F=F: e.tensor_copy(cstb[i][:, :F], cst[i][:, :F])),
                      reads=[tk_cst[i]], writes=[tk_cstb[i]])
            else:
                sc.op("act", (lambda e, i=i, F=F: e.activation(cstb[i][:, :F], cst[i][:, :F], AF.Copy)),
                      reads=[tk_cst[i]], writes=[tk_cstb[i]])
            sc.op("pool", (lambda e, i=i, d_ap=d_ap, F=F: e.dma_start(out=d_ap, in_=cstb[i][:, :F])),
                  reads=[tk_cstb[i]], owner=tk_cstb[i])

    tk_wbf = {}
    for nm, (kp, KC, NJ) in WSPEC.items():
        tot = L * NJ * kp * KC * 128
        nper = tot // 128
        s = wsrc[nm].rearrange("l j k x -> (l j k x)").rearrange("(p n) -> p n", p=128)
        d = wbf[nm].rearrange("l j k x -> (l j k x)").rearrange("(p n) -> p n", p=128)
        convert(s, d, nper)
        tk_wbf[nm] = sc.tk("wbf_" + nm)
    sc.barrier()

    def wload(nm, l, j):
        kp, KC, NJ = WSPEC[nm]
        s = ring_i[0] % NRING
        ring_i[0] += 1
        n = KC * 128
        src = wbf[nm][l, j]
        sc.op("sp", (lambda e, s=s, kp=kp, n=n, src=src: e.dma_start(out=ring[s][:kp, :n], in_=src)),
              writes=[tk_ring[s]], owner=tk_ring[s])
        return ring[s], tk_ring[s]

    def rmsnorm(gidx, out_fp32=None):
        sc.op("act", lambda e: e.activation(sq[:], h[:], AF.Square), reads=tk_h, writes=[tk_sq])

        def mm(e):
            r = None
            for c in range(8):
                r = e.matmul(pb[0][:], onesb[:], sq[:, c, :], start=(c == 0), stop=(c == 7))
            return r
        sc.op("pe", mm, reads=[tk_sq, tk_ones], writes=[tk_pb[0]])
        sc.op("act", lambda e: e.activation(rstd[:], pb[0][:], AF.Sqrt, bias=1e-6, scale=1.0 / D),
              reads=[tk_pb[0]], writes=[tk_rstd])
        sc.op("dve", lambda e: e.reciprocal(rstd[:], rstd[:]), reads=[tk_rstd], writes=[tk_rstd])
        dst = nb if out_fp32 is None else out_fp32

        def nrm(e):
            r = None
            for c in range(8):
                r = e.scalar_tensor_tensor(out=dst[:, c, :], in0=h[:, c, :], scalar=gains[:, gidx, c:c + 1],
                                           in1=rstd[:], op0=ALU.mult, op1=ALU.mult)
            return r
        return nrm

    def norm_to_n(gidx):
        nrm = rmsnorm(gidx)
        sc.op("dve", nrm, reads=tk_h + [tk_rstd, tk_gains], writes=[tk_n])

    def ffn(l, wg, wu, wd):
        for j in range(NJF):
            gbuf, gtk = wload(wg, l, j)
            ubuf, utk = wload(wu, l, j)
            pg, tpg = next_bank()
            pu, tpu = next_bank()

            def mmg(e, gbuf=gbuf, pg=pg):
                r = None
                for c in range(8):
                    r = e.matmul(pg[:], gbuf[:, c * 128:(c + 1) * 128], nb[:, c, :], start=(c == 0), stop=(c == 7))
                return r

            def mmu(e, ubuf=ubuf, pu=pu):
                r = None
                for c in range(8):
                    r = e.matmul(pu[:], ubuf[:, c * 128:(c + 1) * 128], nb[:, c, :], start=(c == 0), stop=(c == 7))
                return r
            sc.op("pe", mmg, reads=[gtk, tk_n], writes=[tpg])
            sc.op("pe", mmu, reads=[utk, tk_n], writes=[tpu])
            si = j % 2
            sc.op("act", (lambda e, si=si, pg=pg: e.activation(sg[si][:], pg[:], AF.Silu)),
                  reads=[tpg], writes=[tk_sg[si]])
            sc.op("dve", (lambda e, si=si, pu=pu, j=j: e.tensor_tensor(act[:, j, :], sg[si][:], pu[:], ALU.mult)),
                  reads=[tk_sg[si], tpu], writes=[tk_act[j]])
        for c in range(8):
            dbuf, dtk = wload(wd, l, c)
            po, tpo = next_bank()

            def mmd(e, dbuf=dbuf, po=po):
                r = None
                for j in range(NJF):
                    r = e.matmul(po[:], dbuf[:, j * 128:(j + 1) * 128], act[:, j, :], start=(j == 0), stop=(j == NJF - 1))
                return r
            sc.op("pe", mmd, reads=[dtk] + tk_act, writes=[tpo])
            sc.op("dve", (lambda e, po=po, c=c: e.scalar_tensor_tensor(out=h[:, c, :], in0=po[:], scalar=0.5, in1=h[:, c, :],
                                                                    op0=ALU.mult, op1=ALU.add)),
                  reads=[tpo, tk_h[c]], writes=[tk_h[c]])

    def load_x_tile(t):
        t0 = t * 512
        sc.op("pool", lambda e: e.dma_start(out=tokbuf[:], in_=xin[t0:t0 + 512, :].rearrange("(b p) d -> p b d", p=128)),
              writes=[tk_tok], owner=tk_tok)
        for c in range(8):
            def tr(e, c=c):
                r = None
                for b in range(4):
                    r = e.transpose(pb[7][:, b * 128:(b + 1) * 128], tokbuf[:, b, c * 128:(c + 1) * 128], ident[:])
                return r
            sc.op("pe", tr, reads=[tk_tok, tk_ident], writes=[tk_pb[7]])
            if c % 2 == 0:
                sc.op("act", (lambda e, c=c: e.activation(h[:, c, :], pb[7][:], AF.Copy)), reads=[tk_pb[7]], writes=[tk_h[c]])
            else:
                sc.op("dve", (lambda e, c=c: e.tensor_copy(h[:, c, :], pb[7][:])), reads=[tk_pb[7]], writes=[tk_h[c]])

    def store_out_tile(t):
        t0 = t * 512
        nrm = rmsnorm(L * 4, out_fp32=sqf)
        sc.op("dve", nrm, reads=tk_h + [tk_rstd, tk_gains], writes=[tk_sqf])
        for c in range(8):
            def tr(e, c=c):
                r = None
                for b in range(4):
                    r = e.transpose(pb[7][:, b * 128:(b + 1) * 128], sqf[:, c, b * 128:(b + 1) * 128], ident[:])
                return r
            sc.op("pe", tr, reads=[tk_sqf, tk_ident], writes=[tk_pb[7]])
            dst = tokbuf[:, :, c * 128:(c + 1) * 128]
            src = pb[7][:].rearrange("p (b q) -> p b q", b=4)
            if c % 2 == 0:
                sc.op("act", (lambda e, dst=dst, src=src: e.activation(dst, src, AF.Copy)), reads=[tk_pb[7]], writes=[tk_tok])
            else:
                sc.op("dve", (lambda e, dst=dst, src=src: e.tensor_copy(dst, src)), reads=[tk_pb[7]], writes=[tk_tok])
        sc.op("pool", lambda e: e.dma_start(out=yout[t0:t0 + 512, :].rearrange("(b p) d -> p b d", p=128), in_=tokbuf[:]),
              reads=[tk_tok], owner=tk_tok)


    def phase_a(l, t):
        t0 = t * 512
        norm_to_n(l * 4 + 0)
        ffn(l, "wg1", "wu1", "wd1")
        sc.op("pool", lambda e: e.dma_start(out=h1T[:, :, t0:t0 + 512].rearrange("c p t -> p c t"), in_=h[:]),
              reads=tk_h, writes=[tk_h1T], owner=tk_h[0])
        if not do_mix:
            return
        norm_to_n(l * 4 + 1)
        for j in range(NJP):
            wbuf, wtk = wload("win", l, j)
            po, tpo = next_bank()

            def mmp(e, wbuf=wbuf, po=po):
                r = None
                for c in range(8):
                    r = e.matmul(po[:], wbuf[:, c * 128:(c + 1) * 128], nb[:, c, :], start=(c == 0), stop=(c == 7))
                return r
            sc.op("pe", mmp, reads=[wtk, tk_n], writes=[tpo])
            k = j % 3
            scale = 0.125 if j < 4 else 1.0
            dstb = ev_b[k] if j < 12 else ev_f[k]
            dtk = tk_evb[k] if j < 12 else tk_evf[k]
            if j % 2 == 0:
                sc.op("act", (lambda e, dstb=dstb, po=po, scale=scale: e.activation(dstb[:], po[:], AF.Copy, scale=scale)),
                      reads=[tpo], writes=[dtk])
            else:
                sc.op("dve", (lambda e, dstb=dstb, po=po, scale=scale: e.tensor_scalar(dstb[:], po[:], scale, None, ALU.mult)),
                      reads=[tpo], writes=[dtk])
            if j < 12:
                sc.op("pool", (lambda e, dstb=dstb, j=j: e.dma_start(out=qkvT[j, :, t0:t0 + 512], in_=dstb[:])),
                      reads=[dtk], writes=[tk_qkvT], owner=dtk)
            else:
                sc.op("pool", (lambda e, dstb=dstb, j=j: e.dma_start(out=projT[j - 12, :, t0:t0 + 512], in_=dstb[:])),
                      reads=[dtk], writes=[tk_projT], owner=dtk)

    def phase_c(l, t):
        t0 = t * 512
        sc.op("pool", lambda e: e.dma_start(out=h[:], in_=h1T[:, :, t0:t0 + 512].rearrange("c p t -> p c t")),
              reads=[tk_h1T], writes=tk_h, owner=tk_h[0])
        if do_mix:
            sc.op("pool", lambda e: e.dma_start(out=ynat[:], in_=ynaT[:, :, t0:t0 + 512]),
                  reads=[tk_ynaT], writes=[tk_ynat], owner=tk_ynat)
            sc.op("pool", lambda e: e.dma_start(out=yrwt[:], in_=yrwT[:, :, t0:t0 + 512]),
                  reads=[tk_yrwT], writes=[tk_yrwt], owner=tk_yrwt)
            for j in range(8):
                w1, w1tk = wload("wona", l, j)
                w2, w2tk = wload("worw", l, j)
                po, tpo = next_bank()

                def mmo(e, w1=w1, w2=w2, po=po):
                    for hh in range(8):
                        e.matmul(po[:], w1[:64, hh * 128:(hh + 1) * 128], ynat[:, hh, :], start=(hh == 0), stop=False)
                    r = None
                    for c in range(4):
                        r = e.matmul(po[:], w2[:, c * 128:(c + 1) * 128], yrwt[:, c, :], start=False, stop=(c == 3))
                    return r
                sc.op("pe", mmo, reads=[w1tk, w2tk, tk_ynat, tk_yrwt], writes=[tpo])
                sc.op("dve", (lambda e, po=po, j=j: e.tensor_tensor(h[:, j, :], h[:, j, :], po[:], ALU.add)),
                      reads=[tpo, tk_h[j]], writes=[tk_h[j]])
        norm_to_n(l * 4 + 2)
        ffn(l, "wg2", "wu2", "wd2")
        sc.op("pool", lambda e: e.dma_start(out=pbuf[:], in_=pin[l, t0:t0 + 512, :].rearrange("(b p) d -> p b d", p=128)),
              writes=[tk_pbuf], owner=tk_pbuf)
        for c in range(2):
            def tr(e, c=c):
                r = None
                for b in range(4):
                    r = e.transpose(pb[7][:, b * 128:(b + 1) * 128], pbuf[:, b, c * 128:(c + 1) * 128], ident[:])
                return r
            sc.op("pe", tr, reads=[tk_pbuf, tk_ident], writes=[tk_pb[7]])
            sc.op("act", (lambda e, c=c: e.activation(pT[:, c, :], pb[7][:], AF.Copy)), reads=[tk_pb[7]], writes=[tk_pT])
        norm_to_n(l * 4 + 3)
        for j in range(8):
            wgt, wgtk = wload("pgate", l, j)
            wup, wuptk = wload("pup", l, j)
            pg, tpg = next_bank()
            pu, tpu = next_bank()

            def mmg(e, wgt=wgt, pg=pg):
                r = None
                for c in range(8):
                    r = e.matmul(pg[:], wgt[:, c * 128:(c + 1) * 128], nb[:, c, :], start=(c == 0), stop=(c == 7))
                return r

            def mmu(e, wup=wup, pu=pu):
                r = None
                for c in range(2):
                    r = e.matmul(pu[:], wup[:, c * 128:(c + 1) * 128], pT[:, c, :], start=(c == 0), stop=(c == 1))
                return r
            sc.op("pe", mmg, reads=[wgtk, tk_n], writes=[tpg])
            sc.op("pe", mmu, reads=[wuptk, tk_pT], writes=[tpu])
            si = j % 2
            sc.op("act", (lambda e, si=si, pg=pg: e.activation(sg[si][:], pg[:], AF.Sigmoid)), reads=[tpg], writes=[tk_sg[si]])
            sc.op("dve", (lambda e, si=si, pu=pu: e.tensor_tensor(sg[si][:], sg[si][:], pu[:], ALU.mult)),
                  reads=[tk_sg[si], tpu], writes=[tk_sg[si]])
            sc.op("dve", (lambda e, si=si, j=j: e.tensor_tensor(h[:, j, :], h[:, j, :], sg[si][:], ALU.add)),
                  reads=[tk_sg[si], tk_h[j]], writes=[tk_h[j]])


    R_ = T // 64
    RS = SEGLEN // 64
    tb_d = dram_in("tb", [L, 128, 8 * 14 * 64])
    tkN = carver()
    KW = 22 * 64
    kT = tkN([64, 8, KW], BF16); tk_kT = sc.tk("kT")
    vT = tkN([128, 4, KW], BF16); tk_vT = sc.tk("vT")
    qT = tkN([64, 8, 512], BF16); tk_qT = sc.tk("qT")
    Vb = tkN([128, 21, 512], BF16); tk_Vbs = [sc.tk("Vb%d" % i) for i in range(21)]
    Tb = tkN([128, 8, 7, 2, 64][:4] if False else [128, 8 * 14 * 64]); tk_Tb = sc.tk("Tb")
    Tb5 = Tb.rearrange("p (h m2 par j) -> p h m2 par j", h=8, m2=7, par=2)
    Ssb = tkN([128, 8, 4, 64]); tk_Ssb = sc.tk("Ssb")
    Eb = tkN([128, 8, 4, 64], BF16); tk_E = sc.tk("E")
    rden = tkN([64, 512]); tk_rden = sc.tk("rden")
    tmpS = tkN([64, 512]); tk_tmpS = sc.tk("tmpS")
    tmpP = tkN([64, 512]); tk_tmpP = sc.tk("tmpP")
    ob = tkN([64, 8, 512], BF16); tk_ob = sc.tk("ob")
    zt = tkN([128, 4, 512], BF16); tk_zt = sc.tk("zt")
    Sps = pall[:, 512:2560].rearrange("p (h k q) -> p h k q", h=8, k=4)
    tk_S = tk_pb[1:5]
    pb7b = pb[7].bitcast(BF16)
    pbTs = [pb[7].bitcast(BF16), pb[0].bitcast(BF16)]
    tk_pbTs = [tk_pb[7], tk_pb[0]]

    def na_phase(l):
        sc.op("pool", lambda e: e.dma_start(out=Tb[:], in_=tb_d[l]), writes=[tk_Tb], owner=tk_Tb)
        for blk in range(R_ // 8):
            r0 = blk * 8
            klo = max(0, r0 - 7)
            khi = min(R_, r0 + 15)
            nk = (khi - klo) * 64
            sc.op("pool", (lambda e, klo=klo, khi=khi, nk=nk: e.dma_start(
                out=kT[:, :, :nk], in_=qkvT[4:8, :, klo * 64:khi * 64].rearrange("c (two p) t -> p (c two) t", two=2))),
                reads=[tk_qkvT], writes=[tk_kT], owner=tk_kT)
            sc.op("pool", (lambda e, klo=klo, khi=khi, nk=nk: e.dma_start(
                out=vT[:, :, :nk], in_=qkvT[8:12, :, klo * 64:khi * 64].rearrange("c p t -> p c t"))),
                reads=[tk_qkvT], writes=[tk_vT], owner=tk_vT)
            sc.op("pool", (lambda e, r0=r0: e.dma_start(
                out=qT[:], in_=qkvT[0:4, :, r0 * 64:r0 * 64 + 512].rearrange("c (two p) t -> p (c two) t", two=2))),
                reads=[tk_qkvT], writes=[tk_qT], owner=tk_qT)
            for o in range(klo, khi - 1):
                oo = o - klo

                pbt = pbTs[oo % 2]
                tkt = tk_pbTs[oo % 2]

                def trv(e, oo=oo, pbt=pbt):
                    r = None
                    for hp in range(4):
                        r = e.transpose(pbt[:, hp * 128:(hp + 1) * 128], vT[:, hp, oo * 64:oo * 64 + 128], identb[:])
                    return r
                sc.op("pe", trv, reads=[tk_vT, tk_identb], writes=[tkt])
                if oo % 2 == 0:
                    sc.op("act", (lambda e, oo=oo, pbt=pbt: e.activation(Vb[:, oo, :], pbt[:, 0:512], AF.Copy)),
                          reads=[tkt], writes=[tk_Vbs[oo]])
                else:
                    sc.op("dve", (lambda e, oo=oo, pbt=pbt: e.tensor_copy(Vb[:, oo, :], pbt[:, 0:512])),
                          reads=[tkt], writes=[tk_Vbs[oo]])
            rvs = []
            for i in range(r0, r0 + 8):
                seg = i // RS
                il = i % RS
                rsP = seg * RS + min(max(il - 4, 0), RS - 8)
                rsS = min(max(i - 4, 0), R_ - 8)
                if rsP == rsS:
                    rvs.append((i, rsP, 0))
                else:
                    rvs.append((i, rsS, 1))
                    rvs.append((i, rsP, 2))

            def emit_S(i, rs, r0=r0, klo=klo):
                qo = (i - r0) * 64

                def mms(e):
                    r = None
                    for hh in range(8):
                        for kc in range(4):
                            ks = (rs + 2 * kc - klo) * 64
                            r = e.matmul(Sps[:, hh, kc, :], kT[:, hh, ks:ks + 128], qT[:, hh, qo:qo + 64],
                                         start=True, stop=True)
                    return r
                sc.op("pe", mms, reads=[tk_kT, tk_qT], writes=tk_S)

            def emit_soft(i, rs):
                cl = i - rs
                s0 = 7 - cl
                tbv = Tb5[:, :, s0 // 2:s0 // 2 + 4, s0 % 2, :]
                sc.op("dve", lambda e: e.tensor_tensor(Ssb[:], Sps, tbv, ALU.add), reads=tk_S + [tk_Tb], writes=[tk_Ssb])
                sc.op("act", lambda e: e.activation(Eb[:], Ssb[:], AF.Exp), reads=[tk_Ssb], writes=[tk_E])

            def emit_pv(i, rs, kind, r0=r0, klo=klo):
                qo = (i - r0) * 64

                def mmpv(e):
                    r = None
                    for hh in range(8):
                        for kc in range(4):
                            r = e.matmul(pb[5][:64, hh * 64:(hh + 1) * 64], Vb[:, rs + 2 * kc - klo, hh * 64:(hh + 1) * 64],
                                         Eb[:, hh, kc, :], start=(kc == 0), stop=(kc == 3))
                    return r
                sc.op("pe", mmpv, reads=tk_Vbs + [tk_E], writes=[tk_pb[5]])

                def mmden(e):
                    r = None
                    for kc in range(4):
                        r = e.matmul(pb[6][:64, :].rearrange("p (h q) -> p h q", h=8), onesb[:, :64], Eb[:, :, kc, :],
                                     start=(kc == 0), stop=(kc == 3))
                    return r
                sc.op("pe", mmden, reads=[tk_E, tk_ones], writes=[tk_pb[6]])
                sc.op("dve", lambda e: e.reciprocal(rden[:], pb[6][:64, :]), reads=[tk_pb[6]], writes=[tk_rden])
                o3 = pb[5][:64, :].rearrange("p (h q) -> p h q", h=8)
                r3 = rden[:].rearrange("p (h q) -> p h q", h=8)
                if kind == 0:
                    sc.op("dve", lambda e: e.tensor_tensor(ob[:, :, qo:qo + 64], o3, r3, ALU.mult),
                          reads=[tk_pb[5], tk_rden], writes=[tk_ob])
                elif kind == 1:
                    sc.op("dve", lambda e: e.scalar_tensor_tensor(out=tmpS[:], in0=pb[5][:64, :], scalar=flags[:64, 0:1], in1=rden[:],
                                                                  op0=ALU.mult, op1=ALU.mult),
                          reads=[tk_pb[5], tk_rden, tk_flags], writes=[tk_tmpS])
                else:
                    sc.op("dve", lambda e: e.scalar_tensor_tensor(out=tmpP[:], in0=pb[5][:64, :], scalar=flags[:64, 1:2], in1=rden[:],
                                                                  op0=ALU.mult, op1=ALU.mult),
                          reads=[tk_pb[5], tk_rden, tk_flags], writes=[tk_tmpP])
                    sc.op("dve", lambda e: e.tensor_tensor(ob[:, :, qo:qo + 64], tmpS[:].rearrange("p (h q) -> p h q", h=8),
                                                           tmpP[:].rearrange("p (h q) -> p h q", h=8), ALU.add),
                          reads=[tk_tmpS, tk_tmpP], writes=[tk_ob])

            import os
            NAS = int(os.environ.get("NAS", "3"))
            if NAS >= 1:
                emit_S(rvs[0][0], rvs[0][1])
            for n, (i, rs, kind) in enumerate(rvs):
                if NAS >= 2:
                    emit_soft(i, rs)
                if n + 1 < len(rvs) and NAS >= 1:
                    emit_S(rvs[n + 1][0], rvs[n + 1][1])
                if NAS >= 3:
                    emit_pv(i, rs, kind)
            sc.op("pool", (lambda e, r0=r0: e.dma_start(out=ynaT[:, :, r0 * 64:r0 * 64 + 512], in_=ob[:])),
                  reads=[tk_ob], writes=[tk_ynaT], owner=tk_ob)

    def zero_fill(dst, tk_dst, npart):
        sc.op("pool", lambda e: e.memset(zt[:], 0.0), writes=[tk_zt])
        for t in range(NT):
            if npart == 128:
                sc.op("pool", (lambda e, t=t: e.dma_start(out=dst[:, :, t * 512:(t + 1) * 512], in_=zt[:])),
                      reads=[tk_zt], writes=[tk_dst], owner=tk_zt)
            else:
                for hh2 in range(2):
                    sc.op("pool", (lambda e, t=t, hh2=hh2: e.dma_start(out=dst[:, hh2 * 4:hh2 * 4 + 4, t * 512:(t + 1) * 512], in_=zt[:64])),
                          reads=[tk_zt], writes=[tk_dst], owner=tk_zt)

    TT_ = 256
    NTR = T // TT_
    SC = 0.606531
    rwp_d = {nm: dram_in(nm, shp) for nm, shp in [
        ("rw_mu", [L, 1920]), ("rw_w0", [L, 2, 512]), ("rw_w_up", [L, 2, 64, 512]), ("rw_a0", [L, 2, 512]),
        ("rw_a_up", [L, 2, 64, 512]), ("rw_g_up", [L, 128, 512]), ("rw_k_k", [L, 512]), ("rw_k_a", [L, 512]),
        ("rw_r_k", [L, 512]), ("rw_lnx_w", [L, 512]), ("rw_lnx_b", [L, 512])]}
    rwmask_d = dram_in("rwmask", [128, 4, 128])
    yf_d = dram_tmp("yf_d", [128, T // 64, 4, 64], F32); tk_yfd = sc.tk("yfd")
    tkR = carver()
    uA0 = tkR([128, 4, TT_ + 2]); uA = [uA0, uA0]; tk_uA0 = sc.tk("uA0"); tk_uA = [tk_uA0, tk_uA0]
    shb = tkR([128, 4, TT_]); tk_sh = sc.tk("sh"); o_sh = tkR.last
    xr = tkR([128, 4, TT_]); xk = tkR([128, 4, TT_]); xv = tkR([128, 4, TT_])
    tk_x = [sc.tk("xr"), sc.tk("xk"), sc.tk("xv")]
    xs3 = [xr, xk, xv]
    uB = tkR([64, 4, TT_ + 2]); tk_uB = sc.tk("uB")
    xB = tkR([64, 4, TT_]); tk_xB = sc.tk("xB")
    uC = tkR([128, 1, TT_ + 2]); tk_uC = sc.tk("uC")
    xCf = tkR([128, 1, TT_]); tk_xCf = sc.tk("xCf")
    xCb = tkR([128, TT_], BF16); tk_xCb = sc.tk("xCb")
    twb = tkR([64, TT_], BF16); tk_twb = sc.tk("twb")
    alb = tkR([64, 2, TT_], BF16); tk_alb = sc.tk("alb")
    sqkb = tkR([128, 4, TT_], BF16); tk_sqkb = sc.tk("sqkb")
    kkn = tkR([128, 4, TT_]); tk_kkn = sc.tk("kkn"); o_kkn = tkR.last
    sig = tkR([128, 4, TT_]); tk_sig = sc.tk("sig"); o_sig = tkR.last
    cs = tkR([128, 4, TT_]); tk_cs = sc.tk("cs"); o_cs = tkR.last
    t1 = tkR([128, 4, TT_]); tk_t1 = sc.tk("t1"); o_t1 = tkR.last
    t2 = tkR([128, 4, TT_]); tk_t2 = sc.tk("t2"); o_t2 = tkR.last
    e0 = tkR([128, 4, TT_]); tk_e0 = sc.tk("e0"); o_e0 = tkR.last
    e1 = tkR([128, 4, TT_]); tk_e1 = sc.tk("e1")
    asg = tkR([128, 4, TT_]); tk_asg = sc.tk("asg"); o_asg = tkR.last
    asf = tkR.at(o_sh, [128, 4, TT_]); tk_asf = tk_sh
    bb = tkR([128, 4, TT_]); tk_bb = sc.tk("bb"); o_bb = tkR.last
    kd = tkR([128, 4, TT_]); tk_kd = sc.tk("kd"); o_kd = tkR.last
    bonus = tkR.at(o_asg, [128, 4, TT_]); tk_bonus = tk_asg
    gbuf = tkR.at(o_kkn, [128, 4, TT_]); tk_g = tk_kkn
    NCH = TT_ // 64
    ARbd = tkR([128, 4, NCH, 256], BF16); tk_AR = sc.tk("AR")
    Btbd = tkR([128, 4, NCH, 128], BF16); tk_Bt = sc.tk("Bt")
    Ktbd = tkR([128, 4, NCH, 128], BF16); tk_Kt = sc.tk("Kt")
    Bhbd = tkR([128, 4, NCH, 128], BF16); tk_Bh = sc.tk("Bh")
    Khbd = tkR([128, 4, NCH, 128], BF16); tk_Kh = sc.tk("Kh")
    Vbd = tkR([128, 4, NCH, 128], BF16); tk_Vbd = sc.tk("Vbd")
    Sb = [tkR([128, 4, 128], BF16) for _ in range(4)]; tk_Sb = [sc.tk("Sb%d" % i) for i in range(4)]
    STb = [tkR([128, 4, 128], BF16) for _ in range(4)]; tk_STb = [sc.tk("STb%d" % i) for i in range(4)]
    TTb = [tkR([128, 4, 128], BF16) for _ in range(4)]; tk_TTb = [sc.tk("TTb%d" % i) for i in range(4)]
    M1s = [tkR([128, 4, 256], BF16) for _ in range(4)]; tk_M1s = [sc.tk("M1_%d" % i) for i in range(4)]
    M2s = [tkR([128, 4, 256], BF16) for _ in range(4)]; tk_M2s = [sc.tk("M2_%d" % i) for i in range(4)]
    Vtms = [tkR([128, 4, 128], BF16) for _ in range(2)]; tk_Vtms = [sc.tk("Vtm%d" % i) for i in range(2)]
    Bhtms = [tkR([128, 4, 128], BF16) for _ in range(2)]; tk_Bhtms = [sc.tk("Bhtm%d" % i) for i in range(2)]
    Khtms = [tkR([128, 4, 128], BF16) for _ in range(2)]; tk_Khtms = [sc.tk("Khtm%d" % i) for i in range(2)]
    Wb = tkR([128, 4, 128], BF16); tk_Wb = sc.tk("Wb")
    Ub = tkR([128, 4, 128], BF16); tk_Ub = sc.tk("Ub")
    Hs = tkR([128, 4, 128]); tk_H = sc.tk("H")
    Hb = tkR([128, 4, 128], BF16); tk_Hb = sc.tk("Hb")
    Yt = tkR.at(o_sig, [128, NCH, 4, 64]); tk_Yt = tk_sig
    Yf = tkR.at(o_e0, [128, NCH, 4, 64]); tk_Yf = tk_e0
    cen = tkR.at(o_t1, [128, NCH, 4, 64]); tk_cen = tk_t1
    ynbd = tkR([128, NCH, 4, 128], BF16); tk_ynbd = sc.tk("ynbd")
    ynT = tkR.at(o_t2, [128, 4, TT_]); tk_ynT = tk_t2
    orw = tkR([128, 4, TT_], BF16); tk_orw = sc.tk("orw")
    st1 = tkR([128, 16]); st2 = tkR([128, 16]); pcb = tkR([128, 16]); tk_st = sc.tk("st")
    tk_pc = sc.tk("pc")
    mask01 = tkR([128, 4 * TT_]); tk_m01 = sc.tk("m01")
    rwm = tkR([128, 4, 128]); tk_rwm = sc.tk("rwm")
    mAf = tkR([128, 256]); mAb = tkR([128, 256]); tk_mA = sc.tk("mA")
    bones = tkR([128, 128], BF16); tk_bones = sc.tk("bones")
    wupf = tkR.at(o_cs, [64, 2, 512]); aupf = tkR.at(o_kd, [64, 2, 512]); gupf = tkR.at(o_bb, [128, 512])
    wupb = tkR([64, 2, 512], BF16); aupb = tkR([64, 2, 512], BF16); gupb = tkR([128, 512], BF16)
    tk_wts = sc.tk("rwwts")
    tk_stage = [tk_cs, tk_kd, tk_bb]
    muA = tkR([128, 12]); muB = tkR([64, 4]); muC = tkR([128, 1])
    w0s = tkR([128, 8]); a0s = tkR([128, 8]); kks = tkR([128, 4]); kas = tkR([128, 4]); rks = tkR([128, 4])
    lws = tkR([128, 4]); lbs = tkR([128, 4])
    tk_par = sc.tk("rwpar")

    def bank2(i):
        return pall[:, i * 512:(i + 2) * 512]

    def seq(eng, fns, reads, writes):
        for fn in fns:
            sc.op(eng, fn, reads=list(reads) + list(writes), writes=writes)

    def bcast(ap2, shape):
        return ap2.unsqueeze(2).to_broadcast(shape)

    def rw_setup(l):
        def ldp(e):
            r = []
            nsc = dict(allow_slow_non_contiguous=True)
            r.append(e.dma_start(out=muA[:], in_=rwp_d["rw_mu"][l, 0:1536].rearrange("(c p) -> p c", p=128), **nsc))
            r.append(e.dma_start(out=muB[:], in_=rwp_d["rw_mu"][l, 1536:1792].rearrange("(c p) -> p c", p=64), **nsc))
            r.append(e.dma_start(out=muC[:], in_=rwp_d["rw_mu"][l, 1792:1920].rearrange("(c p) -> p c", p=128), **nsc))
            r.append(e.dma_start(out=w0s[:], in_=rwp_d["rw_w0"][l].rearrange("d (c p) -> p (d c)", p=128), **nsc))
            r.append(e.dma_start(out=a0s[:], in_=rwp_d["rw_a0"][l].rearrange("d (c p) -> p (d c)", p=128), **nsc))
            for dst, nm in ((kks, "rw_k_k"), (kas, "rw_k_a"), (rks, "rw_r_k"), (lws, "rw_lnx_w"), (lbs, "rw_lnx_b")):
                r.append(e.dma_start(out=dst[:], in_=rwp_d[nm][l].rearrange("(c p) -> p c", p=128), **nsc))
            return r
        sc.op("pool", ldp, writes=[tk_par], owner=tk_par, ndma=10)

        def ldw(e):
            r = [e.dma_start(out=wupf[:], in_=rwp_d["rw_w_up"][l].rearrange("d k n -> k d n")),
                 e.dma_start(out=aupf[:], in_=rwp_d["rw_a_up"][l].rearrange("d k n -> k d n")),
                 e.dma_start(out=gupf[:], in_=rwp_d["rw_g_up"][l]),
                 e.dma_start(out=rwm[:], in_=rwmask_d)]
            return r
        sc.op("pool", ldw, writes=[tk_wts, tk_rwm] + tk_stage, owner=tk_wts, ndma=4)

        def cvt(e):
            e.tensor_copy(wupb[:], wupf[:])
            e.tensor_copy(aupb[:], aupf[:])
            return e.tensor_copy(gupb[:], gupf[:])
        sc.op("dve", cvt, reads=[tk_wts] + tk_stage, writes=[tk_wts])

        def mkmasks(e):
            e.tensor_copy(mAf[:, 0:128], rwm[:, 1, :])
            e.tensor_copy(mAf[:, 128:256], rwm[:, 3, :])
            e.tensor_copy(mAb[:, 0:128], rwm[:, 0, :])
            return e.tensor_copy(mAb[:, 128:256], rwm[:, 2, :])
        sc.op("dve", mkmasks, reads=[tk_rwm], writes=[tk_mA])

        seq("pool", [lambda e: e.memset(mask01[:], 1.0),
                     lambda e: e.memset(mask01[:].rearrange("p (a b) -> p a b", b=64)[:, :, 0:1], 0.0)], [], [tk_m01])
        seq("pool", [lambda e: e.memset(bones[:], 0.0),
                     lambda e: e.memset(bones[0:64, 0:64], 1.0),
                     lambda e: e.memset(bones[64:128, 64:128], 1.0)], [], [tk_bones])
        for bdt, tkb in ((ARbd, tk_AR), (Btbd, tk_Bt), (Ktbd, tk_Kt), (Bhbd, tk_Bh), (Khbd, tk_Kh), (Vbd, tk_Vbd), (ynbd, tk_ynbd)):
            sc.op("pool", (lambda e, bdt=bdt: e.memset(bdt[:], 0.0)), writes=[tkb])
        sc.op("pool", lambda e: e.memset(Hs[:], 0.0), writes=[tk_H])

    def load_shift(src3, P, ncol, ub, tk_ub, mu, out_ap, tk_out, t0, eng="pool"):
        tlo = max(t0 - 1, 0)
        thi = min(t0 + TT_ + 1, T)
        off = tlo - (t0 - 1)
        n = thi - tlo

        def ld(e):
            return e.dma_start(out=ub[:, :, off:off + n], in_=src3[:, :, tlo:thi])
        sc.op("pool", ld, reads=[tk_projT], writes=[tk_ub], owner=tk_ub)
        if t0 == 0:
            sc.op(eng, lambda e: e.memset(ub[:, :, 0:1], 0.0), writes=[tk_ub])
        elif t0 % SEGLEN == 0:
            sc.op(eng, lambda e: e.tensor_scalar(ub[:, :, 0:1], ub[:, :, 0:1], flags[:P, 0:1], None, ALU.mult),
                  reads=[tk_flags], writes=[tk_ub])
        if t0 + TT_ == T:
            sc.op(eng, lambda e: e.memset(ub[:, :, TT_ + 1:TT_ + 2], 0.0), writes=[tk_ub])
        elif (t0 + TT_) % SEGLEN == 0:
            sc.op(eng, lambda e: e.tensor_scalar(ub[:, :, TT_ + 1:TT_ + 2], ub[:, :, TT_ + 1:TT_ + 2], flags[:P, 0:1], None, ALU.mult),
                  reads=[tk_flags], writes=[tk_ub])
        sh = shb[:P, :ncol, :]
        u1 = ub[:, :, 1:TT_ + 1]

        seq(eng, [lambda e: e.tensor_tensor(sh, ub[:, :, 0:TT_], ub[:, :, 2:TT_ + 2], ALU.add),
                  lambda e: e.tensor_scalar(sh, sh, 0.5, None, ALU.mult),
                  lambda e: e.tensor_tensor(sh, sh, u1, ALU.subtract),
                  lambda e: e.tensor_tensor(sh, sh, bcast(mu, [P, ncol, TT_]), ALU.mult),
                  lambda e: e.tensor_tensor(out_ap, sh, u1, ALU.add)], [tk_ub, tk_par], [tk_sh, tk_out])

    def bd_write(eng, dst4, col0, fn, reads, tk_dst):
        for hh in range(2):
            ov = dst4[hh * 64:(hh + 1) * 64, :, :, col0 + hh * 64:col0 + (hh + 1) * 64]
            sc.op(eng, (lambda e, ov=ov, hh=hh: fn(e, ov, hh)), reads=reads, writes=[tk_dst])

    def half4(ap3, hh):
        return ap3[hh * 64:(hh + 1) * 64].rearrange("p a (c q) -> p a c q", q=64)

    import os as _os
    RWS = float(_os.environ.get("RWS", "9"))

    def rw_shift(ti):
        t0 = ti * TT_
        for grp in range(3):
            load_shift(projT[4 * grp:4 * grp + 4].rearrange("c p t -> p c t"), 128, 4, uA[grp % 2], tk_uA[grp % 2],
                       muA[:, 4 * grp:4 * grp + 4], xs3[grp][:], tk_x[grp], t0)
        load_shift(projT[12:14].rearrange("c (two p) t -> p (c two) t", two=2), 64, 4, uB, tk_uB, muB[:], xB[:], tk_xB, t0)

    def rw_tile(l, d, ti, nxt):
        t0 = ti * TT_
        last = (d == 1)
        if RWS < 2:
            return
        sc.op("dve", lambda e: e.tensor_tensor(kkn[:], xk[:], bcast(kks[:], [128, 4, TT_]), ALU.mult),
              reads=[tk_x[1], tk_par], writes=[tk_kkn])
        sc.op("dve", lambda e: e.tensor_tensor(sqkb[:], kkn[:], kkn[:], ALU.mult), reads=[tk_kkn], writes=[tk_sqkb])

        def mmss(e):
            r = None
            for hp in range(4):
                r = e.matmul(bank2(0)[:, hp * TT_:(hp + 1) * TT_], bones[:], sqkb[:, hp, :], start=True, stop=True)
            return r
        sc.op("pe", mmss, reads=[tk_sqkb, tk_bones], writes=[tk_pb[0], tk_pb[1]])
        t1f = t1[:].rearrange("p a b -> p (a b)")
        sc.op("act", lambda e: e.activation(t1f, bank2(0), AF.Sqrt), reads=[tk_pb[0], tk_pb[1]], writes=[tk_t1])

        seq("dve", [lambda e: e.tensor_scalar(t1f, t1f, 1e-12, None, ALU.max),
                    lambda e: e.reciprocal(t1f, t1f),
                    lambda e: e.tensor_tensor(kkn[:], kkn[:], t1[:], ALU.mult)], [], [tk_t1, tk_kkn])
        if RWS < 3:
            return
        sc.op("act", lambda e: e.activation(twb[:], xB[:, d, :], AF.Tanh), reads=[tk_xB], writes=[tk_twb])
        sc.op("act", lambda e: e.activation(alb[:], xB[:, 2:4, :], AF.Copy), reads=[tk_xB], writes=[tk_alb])

        def mmz(e):
            r = None
            for hp in range(4):
                r = e.matmul(bank2(2)[:, hp * TT_:(hp + 1) * TT_], wupb[:, d, hp * 128:(hp + 1) * 128], twb[:], start=True, stop=True)
            return r
        sc.op("pe", mmz, reads=[tk_twb, tk_wts], writes=[tk_pb[2], tk_pb[3]])

        def mma(dd, bk):
            def f(e):
                r = None
                for hp in range(4):
                    r = e.matmul(bank2(bk)[:, hp * TT_:(hp + 1) * TT_], aupb[:, dd, hp * 128:(hp + 1) * 128], alb[:, dd, :], start=True, stop=True)
                return r
            return f
        sc.op("pe", mma(d, 4), reads=[tk_alb, tk_wts], writes=[tk_pb[4], tk_pb[5]])

        def sigz(e):
            r = None
            for hp in range(4):
                r = e.activation(sig[:, hp, :], bank2(2)[:, hp * TT_:(hp + 1) * TT_], AF.Sigmoid, bias=w0s[:, d * 4 + hp:d * 4 + hp + 1])
            return r
        sc.op("act", sigz, reads=[tk_pb[2], tk_pb[3], tk_par], writes=[tk_sig])

        def siga(dst, dd, bk):
            def f(e):
                r = None
                for hp in range(4):
                    r = e.activation(dst[:, hp, :], bank2(bk)[:, hp * TT_:(hp + 1) * TT_], AF.Sigmoid, bias=a0s[:, dd * 4 + hp:dd * 4 + hp + 1])
                return r
            return f
        sc.op("act", siga(asg, d, 4), reads=[tk_pb[4], tk_pb[5], tk_par], writes=[tk_asg])
        sc.op("dve", lambda e: e.tensor_tensor(bb[:], kkn[:], asg[:], ALU.mult), reads=[tk_kkn, tk_asg], writes=[tk_bb])

        seq("dve", [lambda e: e.scalar_tensor_tensor(out=kd[:], in0=asg[:], scalar=-1.0, in1=bcast(kas[:], [128, 4, TT_]), op0=ALU.add, op1=ALU.mult),
                    lambda e: e.scalar_tensor_tensor(out=kd[:], in0=kd[:], scalar=1.0, in1=xk[:], op0=ALU.add, op1=ALU.mult)],
            [tk_asg, tk_par, tk_x[1]], [tk_kd])
        if RWS < 4:
            return
        csf = cs[:].rearrange("p a b -> p (a b)")
        sc.op("dve", lambda e: e.tensor_tensor_scan(csf, mask01[:], sig[:].rearrange("p a b -> p (a b)"), 0.0, ALU.mult, ALU.add),
              reads=[tk_sig, tk_m01], writes=[tk_cs])
        cs16 = cs[:].rearrange("p a (c q) -> p (a c) q", q=64)
        totb = cs16[:, :, 63:64].to_broadcast([128, 16, 64])
        as16 = lambda ap: ap[:].rearrange("p a (c q) -> p (a c) q", q=64)
        sc.op("act", lambda e: e.activation(pcb[:], cs16[:, :, 63], AF.Exp, scale=-SC), reads=[tk_cs], writes=[tk_pc])
        if d == 0:
            inc, tk_inc = cs, tk_cs
            sc.op("dve", lambda e: e.tensor_tensor(t1[:], cs[:], sig[:], ALU.subtract), reads=[tk_cs, tk_sig], writes=[tk_t1])
            exc, tk_exc = t1, tk_t1
            sc.op("dve", lambda e: e.tensor_tensor(as16(t2), totb, cs16, ALU.subtract), reads=[tk_cs], writes=[tk_t2])
            rem, tk_rem = t2, tk_t2
        else:
            sc.op("dve", lambda e: e.tensor_tensor(as16(t2), totb, cs16, ALU.subtract), reads=[tk_cs], writes=[tk_t2])
            exc, tk_exc = t2, tk_t2
            sc.op("dve", lambda e: e.tensor_tensor(t1[:], t2[:], sig[:], ALU.add), reads=[tk_t2, tk_sig], writes=[tk_t1])
            inc, tk_inc = t1, tk_t1
            sc.op("dve", lambda e: e.tensor_tensor(sig[:], cs[:], sig[:], ALU.subtract), reads=[tk_cs, tk_sig], writes=[tk_sig])
            rem, tk_rem = sig, tk_sig
        if RWS < 4.2:
            return
        sc.op("act", lambda e: e.activation(e0[:], inc[:], AF.Exp, scale=-SC), reads=[tk_inc], writes=[tk_e0])
        bd_write("dve", ARbd, 128, lambda e, ov, hh: e.tensor_tensor(ov, half4(xr, hh), half4(e0, hh), ALU.mult),
                 [tk_x[0], tk_e0], tk_AR)
        if RWS < 4.4:
            return
        sc.op("act", lambda e: e.activation(e1[:], inc[:], AF.Exp, scale=SC), reads=[tk_inc], writes=[tk_e1])
        bd_write("dve", Btbd, 0, lambda e, ov, hh: e.tensor_tensor(ov, half4(bb, hh), half4(e1, hh), ALU.mult),
                 [tk_bb, tk_e1], tk_Bt)
        bd_write("dve", Ktbd, 0, lambda e, ov, hh: e.tensor_tensor(ov, half4(kd, hh), half4(e1, hh), ALU.mult),
                 [tk_kd, tk_e1], tk_Kt)
        if RWS < 4.6:
            return
        sc.op("act", lambda e: e.activation(e0[:], exc[:], AF.Exp, scale=-SC), reads=[tk_exc], writes=[tk_e0])
        bd_write("dve", ARbd, 0, lambda e, ov, hh: e.scalar_tensor_tensor(out=ov, in0=half4(kkn, hh), scalar=-1.0, in1=half4(e0, hh),
                                                                          op0=ALU.mult, op1=ALU.mult),
                 [tk_kkn, tk_e0], tk_AR)
        sc.op("act", lambda e: e.activation(e1[:], rem[:], AF.Exp, scale=-SC), reads=[tk_rem], writes=[tk_e1])
        bd_write("dve", Bhbd, 0, lambda e, ov, hh: e.tensor_tensor(ov, half4(bb, hh), half4(e1, hh), ALU.mult),
                 [tk_bb, tk_e1], tk_Bh)
        bd_write("dve", Khbd, 0, lambda e, ov, hh: e.tensor_tensor(ov, half4(kd, hh), half4(e1, hh), ALU.mult),
                 [tk_kd, tk_e1], tk_Kh)
        bd_write("act", Vbd, 0, lambda e, ov, hh: e.activation(ov, half4(xv, hh), AF.Copy), [tk_x[2]], tk_Vbd)
        if last:
            sc.op("pe", mma(0, 6), reads=[tk_alb, tk_wts], writes=[tk_pb[6], tk_pb[7]])
            sc.op("act", siga(asf, 0, 6), reads=[tk_pb[6], tk_pb[7], tk_par], writes=[tk_asf])

            seq("dve", [lambda e: e.tensor_tensor(asf[:], asf[:], asg[:], ALU.add),
                        lambda e: e.scalar_tensor_tensor(out=asf[:], in0=asf[:], scalar=-2.0, in1=bcast(kas[:], [128, 4, TT_]), op0=ALU.add, op1=ALU.mult),
                        lambda e: e.scalar_tensor_tensor(out=asf[:], in0=asf[:], scalar=2.0, in1=xk[:], op0=ALU.add, op1=ALU.mult),
                        lambda e: e.tensor_tensor(asf[:], asf[:], xr[:], ALU.mult),
                        lambda e: e.tensor_tensor(sqkb[:], asf[:], bcast(rks[:], [128, 4, TT_]), ALU.mult)],
                [tk_asg, tk_par, tk_x[0], tk_x[1]], [tk_asf, tk_sqkb])

            def mmbd(e):
                r = None
                for hp in range(4):
                    r = e.matmul(bank2(6)[:, hp * TT_:(hp + 1) * TT_], bones[:], sqkb[:, hp, :], start=True, stop=True)
                return r
            sc.op("pe", mmbd, reads=[tk_sqkb, tk_bones], writes=[tk_pb[6], tk_pb[7]])
            sc.op("dve", lambda e: e.tensor_tensor(bonus[:].rearrange("p a b -> p (a b)"), bank2(6), xv[:].rearrange("p a b -> p (a b)"), ALU.mult),
                  reads=[tk_pb[6], tk_pb[7], tk_x[2]], writes=[tk_bonus])
            load_shift(projT[14:15].rearrange("c p t -> p c t"), 128, 1, uC, tk_uC, muC[:], xCf[:], tk_xCf, t0)
            sc.op("act", lambda e: e.activation(xCb[:], xCf[:, 0, :], AF.Sigmoid), reads=[tk_xCf], writes=[tk_xCb])

            def mmg(e):
                r = None
                for hp in range(4):
                    r = e.matmul(bank2(6)[:, hp * TT_:(hp + 1) * TT_], gupb[:, hp * 128:(hp + 1) * 128], xCb[:], start=True, stop=True)
                return r
            sc.op("pe", mmg, reads=[tk_xCb, tk_wts], writes=[tk_pb[6], tk_pb[7]])
            sc.op("act", lambda e: e.activation(gbuf[:].rearrange("p a b -> p (a b)"), bank2(6), AF.Copy),
                  reads=[tk_pb[6], tk_pb[7]], writes=[tk_g])
            sc.op("pool", (lambda e: e.dma_start(out=Yf[:], in_=yf_d[:, ti * NCH:(ti + 1) * NCH, :, :])),
                  reads=[tk_yfd], writes=[tk_Yf], owner=tk_Yf)
        if RWS < 5:
            return
        bnd = (t0 > 0 and t0 % SEGLEN == 0) if d == 0 else (t0 + TT_ < T and (t0 + TT_) % SEGLEN == 0)
        if bnd:
            sc.op("dve", lambda e: e.tensor_scalar(Hs[:], Hs[:], flags[:, 0:1], None, ALU.mult), reads=[tk_flags], writes=[tk_H])
        first_tile = (ti == 0) if d == 0 else (ti == NTR - 1)
        if bnd or first_tile:
            sc.op("act", lambda e: e.activation(Hb[:], Hs[:], AF.Copy), reads=[tk_H], writes=[tk_Hb])
        if nxt is not None:
            rw_shift(nxt)
        mA = mAf if d == 0 else mAb
        mB = rwm[:, 0, :] if d == 0 else rwm[:, 1, :]
        mA4 = mA[:].unsqueeze(1).to_broadcast([128, 4, 256])
        mB4 = mB.unsqueeze(1).to_broadcast([128, 4, 128])
        idb4 = identb[:].unsqueeze(1).to_broadcast([128, 4, 128])
        chs = list(range(NCH)) if d == 0 else list(range(NCH - 1, -1, -1))
        fl = lambda ap3: ap3[:].rearrange("p a b -> p (a b)")
        v4 = lambda ap2: ap2.rearrange("p (a b) -> p a b", a=4)
        for ci, ch in enumerate(chs):
            b1 = 0 if ci % 2 == 0 else 5
            b3 = 4 if ci % 2 == 0 else 7

            def p1(e, ch=ch, b1=b1):
                r = None
                for hp in range(4):
                    r = e.matmul(bank2(b1)[:, hp * 256:(hp + 1) * 256], Btbd[:, hp, ch, :], ARbd[:, hp, ch, :], start=True, stop=True)
                return r

            def p3(e, ch=ch, b3=b3):
                r = None
                for hp in range(4):
                    r = e.matmul(pb[b3][:, hp * 128:(hp + 1) * 128], ARbd[:, hp, ch, 0:128], Btbd[:, hp, ch, :], start=True, stop=True)
                return r

            def p2(e, ch=ch):
                r = None
                for hp in range(4):
                    r = e.matmul(bank2(2)[:, hp * 256:(hp + 1) * 256], Ktbd[:, hp, ch, :], ARbd[:, hp, ch, :], start=True, stop=True)
                return r
            sc.op("pe", p1, reads=[tk_AR, tk_Bt], writes=[tk_pb[b1], tk_pb[b1 + 1]])
            sc.op("pe", p3, reads=[tk_AR, tk_Bt], writes=[tk_pb[b3]])
            sc.op("pe", p2, reads=[tk_AR, tk_Kt], writes=[tk_pb[2], tk_pb[3]])
            sc.op("dve", (lambda e, ch=ch, b1=b1: e.tensor_tensor(M1s[ch][:], v4(bank2(b1)), mA4, ALU.mult)),
                  reads=[tk_pb[b1], tk_pb[b1 + 1], tk_mA], writes=[tk_M1s[ch]])
            sc.op("dve", (lambda e, ch=ch, b3=b3: e.tensor_tensor(Sb[ch][:], v4(pb[b3]), mB4, ALU.mult)),
                  reads=[tk_pb[b3], tk_rwm], writes=[tk_Sb[ch]])
            sc.op("dve", (lambda e, ch=ch: e.tensor_tensor(M2s[ch][:], v4(bank2(2)), mA4, ALU.mult)),
                  reads=[tk_pb[2], tk_pb[3], tk_mA], writes=[tk_M2s[ch]])
        for lev in range(1, 6):
            for i, ch in enumerate(chs):
                STc = M1s[ch][:, :, 0:128] if lev == 1 else STb[ch]
                tkST = tk_M1s[ch] if lev == 1 else tk_STb[ch]

                def sq1(e, ch=ch, i=i, STc=STc):
                    r = None
                    for hp in range(4):
                        r = e.matmul(pb[2 * i][:, hp * 128:(hp + 1) * 128], STc[:, hp, :], Sb[ch][:, hp, :], start=True, stop=True)
                    return r
                sc.op("pe", sq1, reads=[tkST, tk_Sb[ch]], writes=[tk_pb[2 * i]])
                if lev < 5:
                    def sq2(e, ch=ch, i=i, STc=STc):
                        r = None
                        for hp in range(4):
                            r = e.matmul(pb[2 * i + 1][:, hp * 128:(hp + 1) * 128], Sb[ch][:, hp, :], STc[:, hp, :], start=True, stop=True)
                        return r
                    sc.op("pe", sq2, reads=[tkST, tk_Sb[ch]], writes=[tk_pb[2 * i + 1]])
            for i, ch in enumerate(chs):
                sc.op("act", (lambda e, ch=ch, i=i: e.activation(fl(Sb[ch]), pb[2 * i], AF.Copy)), reads=[tk_pb[2 * i]], writes=[tk_Sb[ch]])
                if lev < 5:
                    sc.op("dve", (lambda e, ch=ch, i=i: e.tensor_copy(fl(STb[ch]), pb[2 * i + 1])), reads=[tk_pb[2 * i + 1]], writes=[tk_STb[ch]])
            for i, ch in enumerate(chs):
                def ttu(e, ch=ch, i=i):
                    r = None
                    for hp in range(4):
                        o = pb[i][:, hp * 128:(hp + 1) * 128]
                        e.matmul(o, Sb[ch][:, hp, :], identb[:], start=True, stop=False)
                        r = e.matmul(o, Sb[ch][:, hp, :], M1s[ch][:, hp, 0:128], start=False, stop=True)
                    return r
                sc.op("pe", ttu, reads=[tk_Sb[ch], tk_M1s[ch], tk_identb], writes=[tk_pb[i]])
            for i, ch in enumerate(chs):
                sc.op("dve", (lambda e, ch=ch, i=i: e.tensor_tensor(M1s[ch][:, :, 0:128], M1s[ch][:, :, 0:128], v4(pb[i]), ALU.add)),
                      reads=[tk_pb[i]], writes=[tk_M1s[ch]])

        def emit_tm(ch, k):
            def trs(e):
                r = None
                for hp in range(4):
                    e.matmul(pb[5][:, hp * 128:(hp + 1) * 128], Vbd[:, hp, ch, :], identb[:], start=True, stop=True)
                    e.matmul(pb[6][:, hp * 128:(hp + 1) * 128], Bhbd[:, hp, ch, :], identb[:], start=True, stop=True)
                    r = e.matmul(pb[7][:, hp * 128:(hp + 1) * 128], Khbd[:, hp, ch, :], identb[:], start=True, stop=True)
                return r
            sc.op("pe", trs, reads=[tk_Vbd, tk_Bh, tk_Kh, tk_identb], writes=[tk_pb[5], tk_pb[6], tk_pb[7]])
            sc.op("act", lambda e: e.activation(fl(Vtms[k]), pb[5], AF.Copy), reads=[tk_pb[5]], writes=[tk_Vtms[k]])
            sc.op("dve", lambda e: e.tensor_copy(fl(Bhtms[k]), pb[6]), reads=[tk_pb[6]], writes=[tk_Bhtms[k]])
            sc.op("act", lambda e: e.activation(fl(Khtms[k]), pb[7], AF.Copy), reads=[tk_pb[7]], writes=[tk_Khtms[k]])

        emit_tm(chs[0], 0)
        for idx, ch in enumerate(chs):
            k = idx % 2
            if idx + 1 < NCH:
                emit_tm(chs[idx + 1], (idx + 1) % 2)
            M1, tk_M1, M2, tk_M2 = M1s[ch], tk_M1s[ch], M2s[ch], tk_M2s[ch]
            Vtm, tk_Vtm, Bhtm, tk_Bhtm, Khtm, tk_Khtm = Vtms[k], tk_Vtms[k], Bhtms[k], tk_Bhtms[k], Khtms[k], tk_Khtms[k]
            TTf, tk_TTf = TTb[ch], tk_TTb[ch]

            def mmw(e, ch=ch, M2=M2, Vtm=Vtm):
                r = None
                for hp in range(4):
                    e.matmul(pb[0][:, hp * 128:(hp + 1) * 128], ARbd[:, hp, ch, 0:128], Hb[:, hp, :], start=True, stop=False)
                    r = e.matmul(pb[0][:, hp * 128:(hp + 1) * 128], M2[:, hp, 0:128], Vtm[:, hp, :], start=False, stop=True)
                return r
            sc.op("pe", mmw, reads=[tk_AR, tk_Hb, tk_M2, tk_Vtm], writes=[tk_pb[0]])
            sc.op("act", lambda e: e.activation(fl(Wb), pb[0], AF.Copy), reads=[tk_pb[0]], writes=[tk_Wb])

            def mmu(e, M1=M1):
                r = None
                for hp in range(4):
                    o = pb[1][:, hp * 128:(hp + 1) * 128]
                    e.matmul(o, identb[:], Wb[:, hp, :], start=True, stop=False)
                    r = e.matmul(o, M1[:, hp, 0:128], Wb[:, hp, :], start=False, stop=True)
                return r
            sc.op("pe", mmu, reads=[tk_M1, tk_Wb, tk_identb], writes=[tk_pb[1]])
            sc.op("act", lambda e: e.activation(fl(Ub), pb[1], AF.Copy), reads=[tk_pb[1]], writes=[tk_Ub])

            def mmh(e, Bhtm=Bhtm, Khtm=Khtm, Vtm=Vtm):
                r = None
                for hp in range(4):
                    o = pb[3][:, hp * 128:(hp + 1) * 128]
                    e.matmul(o, Bhtm[:, hp, :], Ub[:, hp, :], start=True, stop=False)
                    r = e.matmul(o, Khtm[:, hp, :], Vtm[:, hp, :], start=False, stop=True)
                return r
            sc.op("pe", mmh, reads=[tk_Bhtm, tk_Ub, tk_Khtm, tk_Vtm], writes=[tk_pb[3]])

            def mmy(e, ch=ch, M1=M1, M2=M2, Vtm=Vtm):
                r = None
                for hp in range(4):
                    o = pb[2][:, hp * 128:(hp + 1) * 128]
                    e.matmul(o, ARbd[:, hp, ch, 128:256], Hb[:, hp, :], start=True, stop=False)
                    e.matmul(o, M1[:, hp, 128:256], Ub[:, hp, :], start=False, stop=False)
                    r = e.matmul(o, M2[:, hp, 128:256], Vtm[:, hp, :], start=False, stop=True)
                return r
            sc.op("pe", mmy, reads=[tk_AR, tk_Hb, tk_M1, tk_Ub, tk_M2, tk_Vtm], writes=[tk_pb[2]])
            pcv = pcb[:].rearrange("p (a c) -> p a c", c=NCH)[:, :, ch:ch + 1].to_broadcast([128, 4, 128])
            seq("dve", [(lambda e, pcv=pcv: e.tensor_tensor(Hs[:], Hs[:], pcv, ALU.mult)),
                        lambda e: e.tensor_tensor(Hs[:], Hs[:], v4(pb[3]), ALU.add)],
                [tk_pb[3], tk_pc], [tk_H])
            sc.op("act", lambda e: e.activation(Hb[:], Hs[:], AF.Copy), reads=[tk_H], writes=[tk_Hb])
            for hh in range(2):
                src = v4(pb[2][hh * 64:(hh + 1) * 64, :])[:, :, hh * 64:(hh + 1) * 64]
                dst = Yt[hh * 64:(hh + 1) * 64, ch, :, :]
                if last:
                    yfv = Yf[hh * 64:(hh + 1) * 64, ch, :, :]
                    sc.op("dve", (lambda e, src=src, dst=dst, yfv=yfv: e.tensor_tensor(dst, src, yfv, ALU.add)),
                          reads=[tk_pb[2], tk_Yf], writes=[tk_Yt])
                else:
                    sc.op("act", (lambda e, src=src, dst=dst: e.activation(dst, src, AF.Copy)), reads=[tk_pb[2]], writes=[tk_Yt])
        if RWS < 9:
            return
        if not last:
            sc.op("pool", (lambda e: e.dma_start(out=yf_d[:, ti * NCH:(ti + 1) * NCH, :, :], in_=Yt[:])),
                  reads=[tk_Yt], writes=[tk_yfd], owner=tk_Yt)
            return
        Y16 = Yt[:].rearrange("p c a v -> p (c a) v")
        cen16 = cen[:].rearrange("p c a v -> p (c a) v")

        seq("dve", [lambda e: e.tensor_reduce(st1[:], Y16, AX.X, ALU.add),
                    lambda e: e.tensor_scalar(st1[:], st1[:], -1.0 / 64, None, ALU.mult),
                    lambda e: e.tensor_tensor(cen16, Y16, st1[:].unsqueeze(2).to_broadcast([128, 16, 64]), ALU.add),
                    lambda e: e.tensor_tensor(Y16, cen16, cen16, ALU.mult),
                    lambda e: e.tensor_reduce(st2[:], Y16, AX.X, ALU.add)], [], [tk_Yt, tk_cen, tk_st])
        sc.op("act", lambda e: e.activation(st2[:], st2[:], AF.Sqrt, bias=64e-5, scale=1.0 / 64), reads=[tk_st], writes=[tk_st])
        sc.op("dve", lambda e: e.reciprocal(st2[:], st2[:]), reads=[tk_st], writes=[tk_st])
        for hh in range(2):
            ov = ynbd[hh * 64:(hh + 1) * 64, :, :, hh * 64:(hh + 1) * 64]
            iv = cen[hh * 64:(hh + 1) * 64]
            rv = st2[hh * 64:(hh + 1) * 64, :].rearrange("p (c a) -> p c a", a=4).unsqueeze(3).to_broadcast([64, NCH, 4, 64])
            sc.op("dve", (lambda e, ov=ov, iv=iv, rv=rv: e.tensor_tensor(ov, iv, rv, ALU.mult)), reads=[tk_cen, tk_st], writes=[tk_ynbd])
        for ch in range(NCH):
            def try_(e, ch=ch):
                r = None
                for hp in range(4):
                    r = e.matmul(pb[0][:, hp * 128:(hp + 1) * 128], ynbd[:, ch, hp, :], identb[:], start=True, stop=True)
                return r
            sc.op("pe", try_, reads=[tk_ynbd, tk_identb], writes=[tk_pb[0]])
            for hh in range(2):
                src = pb[0][hh * 64:(hh + 1) * 64, :].rearrange("p (a b) -> p a b", a=4)[:, :, hh * 64:(hh + 1) * 64]
                dst = ynT[hh * 64:(hh + 1) * 64, :, ch * 64:(ch + 1) * 64]
                if hh == 0:
                    sc.op("act", (lambda e, src=src, dst=dst: e.activation(dst, src, AF.Copy)), reads=[tk_pb[0]], writes=[tk_ynT])
                else:
                    sc.op("dve", (lambda e, src=src, dst=dst: e.tensor_copy(dst, src)), reads=[tk_pb[0]], writes=[tk_ynT])

        seq("dve", [lambda e: e.tensor_tensor(ynT[:], ynT[:], bcast(lws[:], [128, 4, TT_]), ALU.mult),
                    lambda e: e.tensor_tensor(ynT[:], ynT[:], bcast(lbs[:], [128, 4, TT_]), ALU.add),
                    lambda e: e.tensor_tensor(ynT[:], ynT[:], bonus[:], ALU.add),
                    lambda e: e.tensor_tensor(orw[:], ynT[:], gbuf[:], ALU.mult)], [tk_par, tk_bonus, tk_g], [tk_ynT, tk_orw])
        sc.op("pool", (lambda e: e.dma_start(out=yrwT[:, :, t0:t0 + TT_], in_=orw[:])), reads=[tk_orw], writes=[tk_yrwT], owner=tk_orw)

    def rw_phase(l):
        rw_setup(l)
        rw_shift(0)
        for ti in range(NTR):
            rw_tile(l, 0, ti, ti + 1 if ti + 1 < NTR else NTR - 1)
        sc.op("pool", lambda e: e.memset(Hs[:], 0.0), writes=[tk_H])
        for ti in range(NTR - 1, -1, -1):
            rw_tile(l, 1, ti, ti - 1 if ti > 0 else None)


    def mixers(l):
        sc.barrier()
        if mode in ("full", "na"):
            na_phase(l)
        else:
            zero_fill(ynaT, tk_ynaT, 64)
        sc.barrier()
        if mode in ("full", "rw"):
            rw_phase(l)
        else:
            zero_fill(yrwT, tk_yrwT, 128)
        sc.barrier()

    for t in range(NT):
        load_x_tile(t)
        phase_a(0, t)
    for l in range(LRUN):
        if do_mix:
            mixers(l)
        for t in range(NT):
            phase_c(l, t)
            if l + 1 < LRUN:
                phase_a(l + 1, t)
            else:
                store_out_tile(t)
    sc.finish("pool")
    sc.emit()
    return nc, sc


def arrange(W, kp):
    Kd, Nd = W.shape
    KC = Kd // kp
    NJ = Nd // 128
    return np.ascontiguousarray(W.reshape(KC, kp, NJ, 128).transpose(2, 1, 0, 3)).reshape(NJ, kp, KC * 128)


def host_weights(inp):
    out = {}

    def st(fn):
        return np.stack([fn(l) for l in range(L)], 0)
    out["w_wg1"] = st(lambda l: arrange(inp["ffn1_wg"][l], 128))
    out["w_wu1"] = st(lambda l: arrange(inp["ffn1_wu"][l], 128))
    out["w_wd1"] = st(lambda l: arrange(inp["ffn1_wd"][l], 128))
    out["w_win"] = st(lambda l: arrange(inp["w_in"][l], 128))
    out["w_wona"] = st(lambda l: arrange(inp["w_out"][l][:512], 64))
    out["w_worw"] = st(lambda l: arrange(inp["w_out"][l][512:], 128))
    out["w_wg2"] = st(lambda l: arrange(inp["ffn2_wg"][l], 128))
    out["w_wu2"] = st(lambda l: arrange(inp["ffn2_wu"][l], 128))
    out["w_wd2"] = st(lambda l: arrange(inp["ffn2_wd"][l], 128))
    out["w_pgate"] = st(lambda l: arrange(inp["ple_gate"][l], 128))
    out["w_pup"] = st(lambda l: arrange(inp["ple_up"][l], 128))
    for nm in NORMS:
        out[nm] = np.ascontiguousarray(inp[nm], dtype=np.float32)
    out["final_norm"] = np.ascontiguousarray(inp["final_norm"], dtype=np.float32)
    return out


def host_extra(inp):
    rpb = np.asarray(inp["na_rpb"], dtype=np.float32)
    tb = np.full((L, 2, 64, 8, 14, 64), NEG, np.float32)
    j = np.arange(64)
    cs = np.clip(j - 8, 0, 48)
    for par in range(2):
        for m in range(14):
            if m + par > 14:
                continue
            for cp in range(64):
                ok = (cp >= cs) & (cp < cs + 16)
                jj = j[ok]
                tb[:, par, cp, :, m, jj] = np.transpose(rpb[:, :, m + par, cp - jj + 15], (2, 0, 1))
    out = {"tb": np.ascontiguousarray(tb.reshape(L, 128, 8 * 14 * 64))}
    p = np.arange(128)[:, None]
    f = np.arange(128)[None, :]
    same = (p // 64) == (f // 64)
    pl, fl = p % 64, f % 64
    rwm = np.stack([same & (fl < pl), same & (fl > pl), same & (fl <= pl), same & (fl >= pl)], 1).astype(np.float32)
    out["rwmask"] = np.ascontiguousarray(rwm)
    for nm in ("rw_mu", "rw_w0", "rw_w_up", "rw_a0", "rw_a_up", "rw_g_up", "rw_k_k", "rw_k_a", "rw_lnx_w", "rw_lnx_b"):
        out[nm] = np.ascontiguousarray(inp[nm], dtype=np.float32)
    out["rw_r_k"] = np.ascontiguousarray(inp["rw_r_k"], dtype=np.float32).reshape(L, 512)
    return out


def kernel(**inputs):
    NSEG, SEGLEN = 4, 4096
    TC = NSEG * SEGLEN
    inp = {k: np.asarray(v) for k, v in inputs.items()}
    nc, sc = build_program(NSEG, SEGLEN, "full")
    hw = host_weights(inp)
    hw.update(host_extra(inp))
    xp = np.asarray(inp["x_prompt"], dtype=np.float32)
    xs = np.asarray(inp["x_sample"], dtype=np.float32)
    pp = np.asarray(inp["p_prompt"], dtype=np.float32)
    ps = np.asarray(inp["p_sample"], dtype=np.float32)
    zx = np.zeros((TC, D), np.float32)
    zp = np.zeros((L, TC, PLE), np.float32)
    in_maps = []
    for c in range(8):
        m = dict(hw)
        fl = np.zeros((128, 2), np.float32)
        if c < 4:
            m["xin"] = np.ascontiguousarray(xp[4 * c:4 * c + 4].reshape(TC, D))
            m["pin"] = np.ascontiguousarray(pp[:, 4 * c:4 * c + 4].reshape(L, TC, PLE))
            fl[:, 1] = 1.0
        elif c < 6:
            m["xin"] = np.ascontiguousarray(xs[c - 4])
            m["pin"] = np.ascontiguousarray(ps[:, c - 4])
            fl[:, 0] = 1.0
        else:
            m["xin"] = zx
            m["pin"] = zp
            fl[:, 1] = 1.0
        m["flags"] = fl
        in_maps.append(m)
    res = run_bass_kernel_spmd(nc, in_maps, core_ids=list(range(8)))
    outs = [np.asarray(r["yout"], dtype=np.float32) for r in res.results]
    y_prompt = np.concatenate([outs[c].reshape(4, SEGLEN, D) for c in range(4)], axis=0)
    y_sample = np.stack([outs[4], outs[5]], axis=0)
    return (y_prompt, y_sample)
```
